# Optimizing a Trainium2 kernel written in Bass

```python
import math
import jax, jax.numpy as jnp
from jax import lax
import numpy as np

D_MODEL = 2048
BATCH = 2
SEQ = 8192
DEPTH = 4

N_EVEN = (DEPTH + 1) // 2
N_ODD = DEPTH // 2

A_HEAD = 64
A_HEADS = D_MODEL // 2 // A_HEAD
A_W = A_HEADS * A_HEAD
DECAY_LORA = 64
AAA_LORA = 64
MV_LORA = 32
LN_X_EPS = A_HEAD * 1e-5
B_HEAD = 64
B_HQ = D_MODEL // 2 // B_HEAD
B_HKV = 4
B_GROUP = B_HQ // B_HKV
B_W = B_HQ * B_HEAD
B_KVW = B_HKV * B_HEAD
WINDOW = 128
BLOCK = 128
ROPE_DIM = B_HEAD // 4
ROPE_THETA = 500000.0
C_CHUNK = 128
C_GROUPS = 16
C_W = D_MODEL
C_GW = C_W // C_GROUPS
RMS_EPS = 1e-5
LN_EPS = 1e-5

SHIFT_W = 3 * A_W + DECAY_LORA + AAA_LORA
E_COLS = SHIFT_W + A_W + B_W + 2 * B_KVW + B_W
O_COLS = 3 * C_W

kernel_name = 'hybrid_rwkv7_swa_sinks_sgu'


def rmsnorm(x, g):
    xf = x.astype(jnp.float32)
    y = xf * lax.rsqrt(jnp.mean(xf * xf, axis=-1, keepdims=True) + RMS_EPS)
    return (y * g.astype(jnp.float32)).astype(x.dtype)


def layernorm(x, w, b):
    xf = x.astype(jnp.float32)
    mean = jnp.mean(xf, axis=-1, keepdims=True)
    var = jnp.mean(jnp.square(xf - mean), axis=-1, keepdims=True)
    y = (xf - mean) * lax.rsqrt(var + LN_EPS)
    return (y * w.astype(jnp.float32) + b.astype(jnp.float32)).astype(x.dtype)


def token_shift(z, mu):
    prev = jnp.pad(z[:, :-1], ((0, 0), (1, 0), (0, 0)))
    return z + (prev - z) * mu


def wkv7_scan(r, w, k, v, a, b):
    f32 = jnp.float32
    xs = tuple(jnp.moveaxis(t.astype(f32), 1, 0) for t in (r, w, k, v, a, b))
    B_, _, H, N = r.shape
    s0 = jnp.zeros((B_, H, N, N), f32)

    def step(S, inp):
        r_t, w_t, k_t, v_t, a_t, b_t = inp
        sa = jnp.einsum('bhvk,bhk->bhv', S, a_t)
        S = S * w_t[:, :, None, :] + sa[..., None] * b_t[:, :, None, :] + v_t[..., None] * k_t[:, :, None, :]
        y = jnp.einsum('bhvk,bhk->bhv', S, r_t)
        return S, y

    _, ys = lax.scan(step, s0, xs)
    return jnp.moveaxis(ys, 0, 1)


def rwkv7_mix(z, mu, w0, w2, a0, a2, k_k, k_a, r_k, ln_w, ln_b, v_first, v0=None, v1=None, v2=None):
    B_, T, _ = z.shape
    z = token_shift(z, mu)
    r, k, v, xw, xa = jnp.split(z, [A_W, 2 * A_W, 3 * A_W, 3 * A_W + DECAY_LORA], axis=-1)
    w_log = -jax.nn.softplus(-(w0 + jnp.tanh(xw) @ w2)) - 0.5
    decay = jnp.exp(-jnp.exp(w_log.astype(jnp.float32)))
    a = jax.nn.sigmoid(a0 + xa @ a2)
    if v_first is None:
        v_first = v
    else:
        v = v + (v_first - v) * jax.nn.sigmoid(v0 + (v @ v1) @ v2)
    heads = lambda t: t.reshape(B_, T, A_HEADS, A_HEAD)
    kk = heads(k * k_k).astype(jnp.float32)
    kk = kk * lax.rsqrt(jnp.sum(kk * kk, axis=-1, keepdims=True) + 1e-12)
    k = k * (1 + (a - 1) * k_a)
    r_h, k_h, v_h, a_h = heads(r), heads(k), heads(v), heads(a)
    y = wkv7_scan(r_h, heads(decay), k_h, v_h, -kk, kk * a_h.astype(jnp.float32))
    mean = jnp.mean(y, axis=-1, keepdims=True)
    var = jnp.mean(jnp.square(y - mean), axis=-1, keepdims=True)
    y = ((y - mean) * lax.rsqrt(var + LN_X_EPS)).reshape(B_, T, A_W)
    y = y * ln_w.astype(jnp.float32) + ln_b.astype(jnp.float32)
    bonus = jnp.sum(r_h * k_h * r_k, axis=-1, keepdims=True) * v_h
    y = y + bonus.reshape(B_, T, A_W).astype(jnp.float32)
    return y.astype(z.dtype), v_first


def rope_partial(x, positions):
    half = ROPE_DIM // 2
    inv_freq = jnp.power(jnp.float32(ROPE_THETA), -jnp.arange(half, dtype=jnp.float32) / half)
    ang = positions.astype(jnp.float32)[..., None] * inv_freq
    cos = jnp.cos(ang)[:, :, None, :]
    sin = jnp.sin(ang)[:, :, None, :]
    xr = x[..., :ROPE_DIM].astype(jnp.float32)
    x1, x2 = xr[..., :half], xr[..., half:]
    rot = jnp.concatenate([x1 * cos - x2 * sin, x2 * cos + x1 * sin], axis=-1)
    return jnp.concatenate([rot.astype(x.dtype), x[..., ROPE_DIM:]], axis=-1)


def swa_sinks(q, k, v, sinks):
    B_, T, _, _ = q.shape
    nb = T // BLOCK
    qb = q.reshape(B_, nb, BLOCK, B_HKV, B_GROUP, B_HEAD)
    kb = k.reshape(B_, nb, BLOCK, B_HKV, B_HEAD)
    vb = v.reshape(B_, nb, BLOCK, B_HKV, B_HEAD)

    def with_prev(t):
        prev = jnp.pad(t[:, :-1], ((0, 0), (1, 0), (0, 0), (0, 0), (0, 0)))
        return jnp.concatenate([prev, t], axis=2)

    kw, vw = with_prev(kb), with_prev(vb)
    scale = 1.0 / math.sqrt(B_HEAD)
    s = jnp.einsum('bnqhgd,bnkhd->bnhgqk', qb, kw, preferred_element_type=jnp.float32) * scale
    qi = jnp.arange(BLOCK)[:, None] + BLOCK
    ki = jnp.arange(2 * BLOCK)[None, :]
    band = (ki <= qi) & (qi - ki < WINDOW)
    has_prev = (jnp.arange(nb)[:, None, None] > 0) | (ki >= BLOCK)[None]
    valid = band[None] & has_prev
    s = jnp.where(valid[None, :, None, None], s, -jnp.inf)
    sink = sinks.astype(jnp.float32).reshape(B_HKV, B_GROUP)[None, None, :, :, None, None]
    m = jnp.maximum(jnp.max(s, axis=-1, keepdims=True), sink)
    p = jnp.exp(s - m)
    p = p / (jnp.sum(p, axis=-1, keepdims=True) + jnp.exp(sink - m))
    o = jnp.einsum('bnhgqk,bnkhd->bnqhgd', p.astype(v.dtype), vw)
    return o.reshape(B_, T, B_W)


def chunked_sgu(u, v, ln_w, ln_b, ws, bs):
    B_, T, _ = u.shape
    nc = T // C_CHUNK
    v = layernorm(v, ln_w, ln_b)
    vc = v.reshape(B_, nc, C_CHUNK, C_GROUPS, C_GW)
    causal = jnp.tril(jnp.ones((C_CHUNK, C_CHUNK), dtype=bool))
    wm = jnp.where(causal[None], ws, jnp.zeros_like(ws))
    mixed = jnp.einsum('gts,bnsgc->bntgc', wm, vc) + bs.T[None, None, :, :, None]
    return u * mixed.reshape(B_, T, C_W)


def setup_inputs(seed: int = 0) -> dict:
    key = jax.random.key(seed)
    ks = iter(jax.random.split(key, 32))
    nrm = lambda shape, s: jax.random.normal(next(ks), shape, jnp.float32) * s
    NE, NO = N_EVEN, N_ODD
    return {
        'x': nrm((BATCH, SEQ, D_MODEL), 1.0),
        'positions': (jax.random.randint(next(ks), (BATCH, 1), 0, 4096) + jnp.arange(SEQ)[None, :]).astype(jnp.int32),
        'e_norm': 1.0 + nrm((NE, D_MODEL), 0.02),
        'e_w_in': nrm((NE, D_MODEL, E_COLS), D_MODEL ** -0.5),
        'e_mu': jax.random.uniform(next(ks), (NE, SHIFT_W), jnp.float32),
        'rwkv_w0': jax.random.uniform(next(ks), (NE, A_W), jnp.float32, -6.0, -1.0),
        'rwkv_w2': nrm((NE, DECAY_LORA, A_W), 0.1 * DECAY_LORA ** -0.5),
        'rwkv_a0': nrm((NE, A_W), 0.1),
        'rwkv_a2': nrm((NE, AAA_LORA, A_W), 0.1 * AAA_LORA ** -0.5),
        'rwkv_k_k': 1.0 + nrm((NE, A_W), 0.1),
        'rwkv_k_a': 1.0 + nrm((NE, A_W), 0.1),
        'rwkv_r_k': nrm((NE, A_HEADS, A_HEAD), 0.1),
        'rwkv_ln_w': 1.0 + nrm((NE, A_W), 0.02),
        'rwkv_ln_b': nrm((NE, A_W), 0.01),
        'rwkv_v0': nrm((NE - 1, A_W), 0.1),
        'rwkv_v1': nrm((NE - 1, A_W, MV_LORA), A_W ** -0.5),
        'rwkv_v2': nrm((NE - 1, MV_LORA, A_W), 0.1 * MV_LORA ** -0.5),
        'attn_sinks': nrm((NE, B_HQ), 0.5),
        'e_w_out': nrm((NE, A_W + B_W, D_MODEL), (A_W + B_W) ** -0.5),
        'o_norm': 1.0 + nrm((NO, D_MODEL), 0.02),
        'o_w_in': nrm((NO, D_MODEL, O_COLS), D_MODEL ** -0.5),
        'sgu_ln_w': 1.0 + nrm((NO, C_W), 0.02),
        'sgu_ln_b': nrm((NO, C_W), 0.01),
        'sgu_ws': nrm((NO, C_GROUPS, C_CHUNK, C_CHUNK), C_CHUNK ** -0.5),
        'sgu_bs': 1.0 + nrm((NO, C_GROUPS, C_CHUNK), 0.1),
        'o_w_out': nrm((NO, C_W, D_MODEL), C_W ** -0.5),
        'final_norm': 1.0 + nrm((D_MODEL,), 0.02),
    }


def reference(x, positions, e_norm, e_w_in, e_mu, rwkv_w0, rwkv_w2, rwkv_a0, rwkv_a2, rwkv_k_k, rwkv_k_a,
              rwkv_r_k, rwkv_ln_w, rwkv_ln_b, rwkv_v0, rwkv_v1, rwkv_v2, attn_sinks, e_w_out,
              o_norm, o_w_in, sgu_ln_w, sgu_ln_b, sgu_ws, sgu_bs, o_w_out, final_norm):
    B_, T, _ = x.shape
    v_first = None
    for layer in range(DEPTH):
        if layer % 2 == 0:
            e = layer // 2
            h = rmsnorm(x, e_norm[e])
            z = h @ e_w_in[e]
            a_in, a_gate, q, kB, vB, b_gate = jnp.split(
                z, [SHIFT_W, SHIFT_W + A_W, SHIFT_W + A_W + B_W, SHIFT_W + A_W + B_W + B_KVW,
                    SHIFT_W + A_W + B_W + 2 * B_KVW], axis=-1)
            if e == 0:
                yA, v_first = rwkv7_mix(a_in, e_mu[e], rwkv_w0[e], rwkv_w2[e], rwkv_a0[e], rwkv_a2[e],
                                        rwkv_k_k[e], rwkv_k_a[e], rwkv_r_k[e], rwkv_ln_w[e], rwkv_ln_b[e], None)
            else:
                yA, v_first = rwkv7_mix(a_in, e_mu[e], rwkv_w0[e], rwkv_w2[e], rwkv_a0[e], rwkv_a2[e],
                                        rwkv_k_k[e], rwkv_k_a[e], rwkv_r_k[e], rwkv_ln_w[e], rwkv_ln_b[e],
                                        v_first, rwkv_v0[e - 1], rwkv_v1[e - 1], rwkv_v2[e - 1])
            qh = rope_partial(q.reshape(B_, T, B_HQ, B_HEAD), positions)
            kh = rope_partial(kB.reshape(B_, T, B_HKV, B_HEAD), positions)
            vh = vB.reshape(B_, T, B_HKV, B_HEAD)
            yB = swa_sinks(qh, kh, vh, attn_sinks[e])
            y = jnp.concatenate([yA * jax.nn.silu(a_gate), yB * jax.nn.silu(b_gate)], axis=-1)
            x = x + y @ e_w_out[e]
        else:
            o = layer // 2
            h = rmsnorm(x, o_norm[o])
            z = h @ o_w_in[o]
            u, vv, gate = jnp.split(z, [C_W, 2 * C_W], axis=-1)
            y = chunked_sgu(u, vv, sgu_ln_w[o], sgu_ln_b[o], sgu_ws[o], sgu_bs[o]) * jax.nn.silu(gate)
            x = x + y @ o_w_out[o]
    return rmsnorm(x, final_norm)
```

```python
import math
import numpy as np
import concourse.bass as bass
import concourse.mybir as mybir
from concourse.bass_utils import run_bass_kernel_spmd

F32 = mybir.dt.float32
F32R = mybir.dt.float32r
I32 = mybir.dt.int32
AF = mybir.ActivationFunctionType
ALU = mybir.AluOpType
AX = mybir.AxisListType

D = 2048
NTOK = 2048
KC = 16
RMS_EPS = 1e-5
LN_EPS = 1e-5


class T:
    def __init__(self, t):
        self.t = t
        self.w = None
        self.r = {}
        self.dsem = None
        self.tr = self

    def __getitem__(self, k):
        return self.t[k]


class DT(T):
    def __init__(self, ap):
        super().__init__(None)
        self.ap = ap
        self.wd = {}


class Ctx:
    def __init__(self, nc):
        self.nc = nc
        self.eng = {'pe': nc.tensor, 'act': nc.scalar, 'dve': nc.vector, 'pool': nc.gpsimd, 'sp': nc.sync}
        self.sems = {}
        self.cnt = {}
        for k in ('pe', 'act', 'dve', 'pool'):
            self.sems[k] = nc.semaphore('s_' + k).__enter__()
            self.cnt[k] = 0
        self.seen = {}
        self.nsb = 0
        self.nps = 0
        self.ndram = 0
        self.out_events = []
        self.stack = []

    def sb(self, shape, dt=F32, name=None):
        self.nsb += 1
        cmgr = self.nc.sbuf_tensor(name or f"sb{self.nsb}", list(shape), dt)
        t = T(cmgr.__enter__())
        self.stack.append(cmgr)
        return t

    def mark(self):
        return len(self.stack)

    def release(self, mark):
        self.barrier()
        while len(self.stack) > mark:
            self.stack.pop().__exit__(None, None, None)

    def barrier(self):
        for e in ('pe', 'act', 'dve', 'pool', 'sp'):
            for k, v in self.cnt.items():
                if v > 0 and k != e and self.seen.get((e, k), 0) < v:
                    self.eng[e].wait_ge(self.sems[k], v)
                    self.seen[(e, k)] = v

    def ps(self, shape, dt=F32, name=None):
        self.nps += 1
        t = T(self.nc.psum_tensor(name or f"ps{self.nps}", list(shape), dt).__enter__())
        t.excl = True
        return t

    def ps_views(self, nbanks, width):
        banks = []
        for b in range(nbanks):
            self.nps += 1
            banks.append(T(self.nc.psum_tensor(f"pbank{self.nps}", [128, 512], F32).__enter__()))
            banks[-1].excl = True
        out = []
        for i in range(512 // width):
            for bk in banks:
                v = T(bk.t[:, i * width:(i + 1) * width])
                v.tr = bk
                out.append(v)
        return out

    def _dsem(self, t):
        if t.dsem is None:
            key = f"d{len(self.sems)}"
            self.sems[key] = self.nc.semaphore('s_' + key).__enter__()
            self.cnt[key] = 0
            t.dsem = key
        return t.dsem

    def _waits(self, e, reads, writes):
        need = {}
        reads = [t.tr for t in reads]
        writes = [t.tr for t in writes]
        for t in reads:
            if isinstance(t, DT):
                for k, v in t.wd.items():
                    need[k] = max(need.get(k, 0), v)
            elif t.w:
                need[t.w[0]] = max(need.get(t.w[0], 0), t.w[1])
            if getattr(t, 'excl', False):
                for k, v in t.r.items():
                    if k != e:
                        need[k] = max(need.get(k, 0), v)
        for t in writes:
            if t.w and not isinstance(t, DT):
                need[t.w[0]] = max(need.get(t.w[0], 0), t.w[1])
            for k, v in t.r.items():
                need[k] = max(need.get(k, 0), v)
        eng = self.eng[e]
        for k, v in need.items():
            if e == 'pe' and k == 'pe':
                continue
            if self.seen.get((e, k), 0) < v:
                eng.wait_ge(self.sems[k], v)
                self.seen[(e, k)] = v

    def op(self, e, fn, reads=(), writes=()):
        self._waits(e, reads, writes)
        ins = fn(self.eng[e])
        self.cnt[e] += 1
        ins.then_inc(self.sems[e], 1)
        ev = (e, self.cnt[e])
        for t in reads:
            t.tr.r[e] = ev[1]
        for t in writes:
            t.tr.w = ev
            t.tr.r = {}
        return ins

    def dma(self, q, out, in_, reads=(), writes=(), track=None):
        self._waits(q, reads, writes)
        t = track
        key = self._dsem(t)
        ins = self.eng[q].dma_start(out=out, in_=in_)
        self.cnt[key] += 16
        ins.then_inc(self.sems[key], 16)
        ev = (key, self.cnt[key])
        for r_ in reads:
            r_.r[key] = ev[1]
        for w_ in writes:
            if isinstance(w_, DT):
                w_.wd[key] = ev[1]
            else:
                w_.w = ev
                w_.r = {}
        return ev

    def finish(self):
        for key, v in self.out_events:
            self.eng['sp'].wait_ge(self.sems[key], v)


def r32(ap):
    return ap.bitcast(F32R)


def f32(ap):
    return ap.bitcast(F32)


class Common:
    def __init__(self, cx, consts, TT):
        nc = cx.nc
        self.cx = cx
        self.TT = TT
        self.ones = cx.sb([128, 128], F32R, "ones")
        cx.dma('pool', self.ones[:], consts['ones'], writes=[self.ones], track=self.ones)
        self.tri = cx.sb([128, 128], F32, "tri")
        cx.dma('sp', self.tri[:], consts['tri'], writes=[self.tri], track=self.tri)
        self.ident = cx.sb([128, 128], F32, "ident")
        cx.dma('sp', self.ident[:], consts['ident'], writes=[self.ident], track=self.ident)
        self.A = cx.sb([128, KC, TT], F32, "bufA")
        self.Y = cx.sb([128, KC, TT], F32R, "bufY")
        self.B = cx.sb([128, KC, TT], F32R, "bufB")
        self.wb = [cx.sb([128, KC, 256], F32R, f"wb{i}") for i in range(2)]
        self.wi = 0
        self.pbig = [cx.ps([128, 512], F32, f"pbig{i}") for i in range(4)]
        self.pi = 0
        self.sq = [cx.sb([128, TT], F32R, f"sq{i}") for i in range(1)]
        self.rstd = cx.sb([128, TT], F32, "rstd")
        self.xc = [cx.sb([128, TT], F32, f"xc{i}") for i in range(2)]
        self.xo = [cx.sb([128, TT], F32, f"xo{i}") for i in range(2)]
        self.xci = 0
        self.eps_rms = cx.sb([128, 1], F32, "eps_rms")
        cx.op('dve', lambda e: e.memset(self.eps_rms[:], RMS_EPS), writes=[self.eps_rms])

    def next_w(self):
        w = self.wb[self.wi % len(self.wb)]
        self.wi += 1
        return w

    def next_p(self):
        p = self.pbig[self.pi % len(self.pbig)]
        self.pi += 1
        return p

    def load_w(self, dram_group):
        w = self.next_w()
        self.cx.dma('pool', w[:], dram_group, writes=[w], track=w)
        return w

    def load_x_and_norm(self, x_src, t0, gvec):
        cx, TT = self.cx, self.TT
        A, B = self.A, self.B
        src = x_src.ap.rearrange("(k p) t -> p k t", p=128)[:, :, t0:t0 + TT]
        cx.dma('sp', A[:], src, reads=[x_src], writes=[A], track=A)
        pss = self.next_p()
        for k in range(KC):
            sq = self.sq[0]
            cx.op('act', lambda e, k=k, sq=sq: e.activation(sq[:], A[:, k, :], AF.Square), reads=[A], writes=[sq])
            cx.op('pe', lambda e, k=k, sq=sq: e.matmul(pss[:, :TT], self.ones[:], sq[:], start=(k == 0), stop=(k == KC - 1)),
                  reads=[sq, self.ones], writes=[pss])
        rstd = self.rstd
        cx.op('act', lambda e: e.activation(rstd[:], pss[:, :TT], AF.Sqrt, bias=self.eps_rms[:], scale=1.0 / D),
              reads=[pss, self.eps_rms], writes=[rstd])
        cx.op('dve', lambda e: e.reciprocal(rstd[:], rstd[:]), reads=[rstd], writes=[rstd])
        for k in range(KC):
            cx.op('dve', lambda e, k=k: e.scalar_tensor_tensor(B[:, k, :], A[:, k, :], gvec[:, k:k + 1], rstd[:],
                                                                ALU.mult, ALU.mult),
                  reads=[A, gvec, rstd], writes=[B])

    def out_proj_residual(self, wout_groups, x_src, x_dst, t0, is_output=False):
        cx, TT = self.cx, self.TT
        A = self.Y
        xs = x_src.ap.rearrange("(k p) t -> p k t", p=128)
        xd = x_dst.ap.rearrange("(k p) t -> p k t", p=128)
        for g in range(8):
            w = self.load_w(wout_groups[g])
            for j in range(2):
                dk = g * 2 + j
                ps = self.next_p()
                for k in range(KC):
                    cx.op('pe', lambda e, k=k, j=j, w=w, ps=ps: e.matmul(ps[:, :TT], w[:, k, j * 128:(j + 1) * 128],
                                                                         A[:, k, :], start=(k == 0), stop=(k == KC - 1)),
                          reads=[w, A], writes=[ps])
                xc = self.xc[self.xci % 2]
                xo = self.xo[self.xci % 2]
                self.xci += 1
                cx.dma('sp', xc[:], xs[:, dk, t0:t0 + TT], reads=[x_src], writes=[xc], track=xc)
                cx.op('dve', lambda e, xc=xc, xo=xo, ps=ps: e.tensor_tensor(xo[:], ps[:, :TT], xc[:], ALU.add),
                      reads=[ps, xc], writes=[xo])
                ev = cx.dma('sp', xd[:, dk, t0:t0 + TT], xo[:], reads=[xo], writes=[x_dst], track=xo)
                if is_output:
                    cx.out_events.append(ev)


def emit_final_norm(cm, x_src, out_dst, gvec):
    cx, TT = cm.cx, cm.TT
    od = out_dst.ap.rearrange("(k p) t -> p k t", p=128)
    for t0 in range(0, NTOK, TT):
        cm.load_x_and_norm(x_src, t0, gvec)
        A = cm.A
        for k in range(KC):
            cx.op('dve', lambda e, k=k: e.scalar_tensor_tensor(A[:, k, :], A[:, k, :], gvec[:, k:k + 1], cm.rstd[:],
                                                                ALU.mult, ALU.mult),
                  reads=[A, gvec, cm.rstd], writes=[A])
        ev = cx.dma('sp', od[:, :, t0:t0 + TT], A[:], reads=[A], writes=[out_dst], track=A)
        cx.out_events.append(ev)


def emit_odd_layer(cm, P, x_src, x_dst):
    cx, TT = cm.cx, cm.TT
    mark = cx.mark()
    nb = TT // 128
    g = cx.sb([128, KC], F32, None)
    cx.dma('sp', g[:], P['norm'], writes=[g], track=g)
    lnw = cx.sb([128, D], F32)
    lnb = cx.sb([128, D], F32)
    cx.dma('sp', lnw[:], P['ln_w'].partition_broadcast(128), writes=[lnw], track=lnw)
    cx.dma('sp', lnb[:], P['ln_b'].partition_broadcast(128), writes=[lnb], track=lnb)
    bsb2 = [cx.sb([128, 128], F32) for _ in range(2)]
    wm = cx.sb([128, 16, 128], F32R)
    wstage = cm.A[:, 0:D // TT, :].rearrange("p k t -> p (k t)").rearrange("p (g t) -> p g t", t=128)
    cx.dma('sp', wstage, P['wsT'], writes=[cm.A], track=cm.A)
    cx.op('pool', lambda e: e.tensor_tensor(wm[:], wstage, cm.tri[:].unsqueeze(1).to_broadcast([128, 16, 128]), ALU.mult),
          reads=[cm.A, cm.tri], writes=[wm])
    eps_ln = cx.sb([128, 1], F32)
    cx.op('dve', lambda e: e.memset(eps_ln[:], LN_EPS), writes=[eps_ln])
    vv = cx.sb([128, nb, D], F32R)
    st = cx.sb([128, 8], F32)
    sg = [cx.sb([128, TT], F32) for _ in range(1)]
    t1 = [cx.sb([128, TT], F32) for _ in range(2)]

    for t0 in range(0, NTOK, TT):
        cm.load_x_and_norm(x_src, t0, g)
        B = cm.B
        for gi in range(8):
            w = cm.load_w(P['w_in'][16 + gi])
            for b in range(nb):
                ps = cm.next_p()
                for k in range(KC):
                    cx.op('pe', lambda e, k=k, b=b, w=w, ps=ps: e.matmul(ps[:, :256], B[:, k, b * 128:(b + 1) * 128], w[:, k, :],
                                                                         start=(k == 0), stop=(k == KC - 1)),
                          reads=[B, w], writes=[ps])
                cx.op('act', lambda e, b=b, gi=gi, ps=ps: e.copy(vv[:, b, gi * 256:(gi + 1) * 256], ps[:, :256]),
                      reads=[ps], writes=[vv])
        for b in range(nb):
            cx.op('dve', lambda e, b=b: e.reduce_sum(st[:, 0:1], f32(vv[:, b, :]), axis=AX.X), reads=[vv], writes=[st])
            cx.op('dve', lambda e: e.tensor_scalar(st[:, 1:2], st[:, 0:1], -1.0 / D, None, ALU.mult), reads=[st], writes=[st])
            cx.op('dve', lambda e, b=b: e.tensor_scalar(vv[:, b, :], f32(vv[:, b, :]), st[:, 1:2], None, ALU.add),
                  reads=[vv, st], writes=[vv])
            cx.op('dve', lambda e: e.memset(st[:, 2:3], 0.0), writes=[st])
            cx.op('act', lambda e, b=b: e.activation(cm.A[:, 0:D // TT, :].rearrange("p k t -> p (k t)"), f32(vv[:, b, :]), AF.Square,
                                                     accum_out=st[:, 2:3]),
                  reads=[vv], writes=[cm.A, st])
            cx.op('act', lambda e: e.activation(st[:, 3:4], st[:, 2:3], AF.Sqrt, bias=eps_ln[:], scale=1.0 / D),
                  reads=[st, eps_ln], writes=[st])
            cx.op('dve', lambda e: e.reciprocal(st[:, 4:5], st[:, 3:4]), reads=[st], writes=[st])
            cx.op('dve', lambda e, b=b: e.scalar_tensor_tensor(vv[:, b, :], f32(vv[:, b, :]), st[:, 4:5], lnw[:], ALU.mult, ALU.mult),
                  reads=[vv, st, lnw], writes=[vv])
            cx.op('pool', lambda e, b=b: e.tensor_tensor(vv[:, b, :], f32(vv[:, b, :]), lnb[:], ALU.add),
                  reads=[vv, lnb], writes=[vv])
        A = cm.Y
        for gi in range(16):
            w = cm.load_w(P['w_in'][gi])
            pu = cm.next_p()
            pg = cm.next_p()
            for k in range(KC):
                cx.op('pe', lambda e, k=k, w=w, pu=pu: e.matmul(pu[:, :TT], w[:, k, 0:128], B[:, k, :], start=(k == 0), stop=(k == KC - 1)),
                      reads=[B, w], writes=[pu])
            for k in range(KC):
                cx.op('pe', lambda e, k=k, w=w, pg=pg: e.matmul(pg[:, :TT], w[:, k, 128:256], B[:, k, :], start=(k == 0), stop=(k == KC - 1)),
                      reads=[B, w], writes=[pg])
            pm = cm.next_p()
            for b in range(nb):
                cx.op('pe', lambda e, b=b, gi=gi, pm=pm: e.matmul(pm[:, b * 128:(b + 1) * 128], vv[:, b, gi * 128:(gi + 1) * 128], wm[:, gi, :],
                                                                  start=True, stop=True),
                      reads=[vv, wm], writes=[pm])
            sgt = sg[0]
            t1t = t1[gi % 2]
            bsb = bsb2[gi % 2]
            cx.dma('sp', bsb[:], P['bs'][gi * 128:(gi + 1) * 128].partition_broadcast(128), writes=[bsb], track=bsb)
            cx.op('act', lambda e, sgt=sgt, pg=pg: e.activation(sgt[:], pg[:, :TT], AF.Silu), reads=[pg], writes=[sgt])
            cx.op('dve', lambda e, sgt=sgt, pu=pu: e.tensor_tensor(sgt[:], pu[:, :TT], sgt[:], ALU.mult), reads=[pu, sgt], writes=[sgt])
            cx.op('dve', lambda e, t1t=t1t, pm=pm, bsb=bsb: e.tensor_tensor(
                t1t[:].rearrange("p (b t) -> p b t", t=128), pm[:, :TT].rearrange("p (b t) -> p b t", t=128),
                bsb[:].unsqueeze(1).to_broadcast([128, nb, 128]), ALU.add), reads=[pm, bsb], writes=[t1t])
            cx.op('pool', lambda e, t1t=t1t, sgt=sgt, gi=gi: e.tensor_tensor(A[:, gi, :], t1t[:], sgt[:], ALU.mult),
                  reads=[t1t, sgt], writes=[A])
        cm.out_proj_residual(P['w_out'], x_src, x_dst, t0)
    cx.release(mark)


def tile_w(Wcols):
    n = Wcols.shape[1] // 256
    a = Wcols.reshape(KC, 128, n, 256).transpose(2, 1, 0, 3)
    return np.ascontiguousarray(a, dtype=np.float32)


def vec_pk(v):
    return np.ascontiguousarray(v.reshape(-1, 128).T, dtype=np.float32)


def prep_odd(o_norm, o_w_in, ln_w, ln_b, ws, bs, w_out):
    u = o_w_in[:, 0:2048].reshape(2048, 16, 128)
    gt = o_w_in[:, 4096:6144].reshape(2048, 16, 128)
    ug = np.concatenate([u, gt], axis=2).reshape(2048, 16 * 256)
    wcols = np.concatenate([ug, o_w_in[:, 2048:4096]], axis=1)
    return {
        'norm': vec_pk(o_norm),
        'w_in': tile_w(wcols),
        'ln_w': np.ascontiguousarray(ln_w, dtype=np.float32),
        'ln_b': np.ascontiguousarray(ln_b, dtype=np.float32),
        'wsT': np.ascontiguousarray(ws.transpose(2, 0, 1), dtype=np.float32),
        'bs': np.ascontiguousarray(bs.reshape(-1), dtype=np.float32),
        'w_out': tile_w(w_out),
    }


def make_consts():
    s = np.arange(128)
    return {
        'ones': np.ones((128, 128), np.float32),
        'tri': (s[None, :] >= s[:, None]).astype(np.float32),
        'ident': np.eye(128, dtype=np.float32),
    }


def declare(nc, name, arr, kind="ExternalInput"):
    dt = I32 if arr.dtype == np.int32 else F32
    return nc.dram_tensor(name, list(arr.shape), dt, kind=kind).ap()


TE = 256
LCH = 64
NCH = TE // LCH
GN_EPS = 64 * 1e-5
TWO_PI = 2.0 * math.pi


class Ring:
    def __init__(self, tiles):
        self.t = tiles
        self.i = 0

    def next(self):
        t = self.t[self.i % len(self.t)]
        self.i += 1
        return t


def emit_even_phase(cx, P, consts, x_src, y_dst, vf_dst, vf_src, ntok, layer2, dbg=9):
    nc = cx.nc
    mark = cx.mark()
    sb, ps, op, dma = cx.sb, cx.ps, cx.op, cx.dma
    ones = sb([128, 128], F32R, "e_ones")
    dma('pool', ones[:], consts['ones'], writes=[ones], track=ones)
    ident = sb([128, 128], F32, "e_ident")
    dma('sp', ident[:], consts['ident'], writes=[ident], track=ident)
    bones = sb([128, 128], F32, "e_bones")
    dma('sp', bones[:], consts['bones'], writes=[bones], track=bones)
    idb2 = sb([128, 64], F32, "e_idb2")
    dma('sp', idb2[:], consts['idb2'], writes=[idb2], track=idb2)
    msk = sb([128, 3, 128], F32, "e_msk")
    dma('sp', msk[:], consts['emsk'], writes=[msk], track=msk)
    mb = sb([128, 2, 256], F32, "e_mb")
    dma('sp', mb[:], consts['amask'], writes=[mb], track=mb)
    rm = sb([128, 128], F32, "e_rm")
    dma('sp', rm[:], consts['ropem'], writes=[rm], track=rm)
    rst = sb([128, TE], F32, "e_rst")
    dma('sp', rst[:], consts['rst'], writes=[rst], track=rst)
    cv = sb([128, 8], F32, "e_cv")
    dma('sp', cv[:], consts['cvec'], writes=[cv], track=cv)
    g = sb([128, KC], F32, "e_g")
    dma('sp', g[:], P['norm'], writes=[g], track=g)
    vec = sb([128, 40], F32, "e_vec")
    dma('sp', vec[:], P['vec'], writes=[vec], track=vec)
    op('dve', lambda e: e.tensor_scalar(vec[:, 14:16], vec[:, 12:14], -1.0, 1.0, ALU.mult, ALU.add), reads=[vec], writes=[vec])
    w2a2 = sb([128, 256], F32, "e_w2a2")
    dma('sp', w2a2[:], P['w2a2'], writes=[w2a2], track=w2a2)
    sinkb = sb([128, 4], F32, "e_sink")
    dma('sp', sinkb[:], P['sinks'].partition_broadcast(128), writes=[sinkb], track=sinkb)
    W = sb([128, 7, KC, 256], F32R, "e_W")
    for gi in range(7):
        dma('pool', W[:, gi], P['w_in'][gi], writes=[W], track=W)
    V_MUR, V_MUK, V_MUV, V_W0, V_A0, V_KK, V_KA, V_OMKA, V_RK, V_LNW, V_LNB, V_V0, V_MUX = 0, 2, 4, 6, 8, 10, 12, 14, 16, 18, 20, 22, 24

    xr = Ring([sb([128, TE], F32, f"e_xr{i}") for i in range(3)])
    B = sb([128, KC, TE + 2], F32R, "e_B")
    sq = sb([128, TE], F32R, "e_sq")
    Z = {n: sb([128, 2, TE + 1], F32, "e_z" + n) for n in ('r', 'k', 'v')}
    Zx = sb([128, TE + 1], F32, "e_zx")
    for zt in list(Z.values()) + [Zx]:
        op('pool', lambda e, zt=zt: e.memset(zt[:], 0.0), writes=[zt])
    ZG = sb([128, 2, TE], F32, "e_zag")
    ZBG = sb([128, 2, TE], F32, "e_zbg")
    KV = sb([128, 128 + TE], F32, "e_kv")
    op('pool', lambda e: e.memset(KV[:], 0.0), writes=[KV])
    Yt = sb([128, 4, TE], F32, "e_y")
    S = [[sb([64, 64], F32, f"e_S{h}_{i}") for i in range(2)] for h in range(4)]
    for h in range(4):
        op('pool', lambda e, h=h: e.memset(S[h][0][:], 0.0), writes=[S[h][0]])
    si = [0, 0, 0, 0]
    tmp = Ring([sb([128, TE], F32, f"e_t{i}") for i in range(7)])
    ACM = sb([128, 2, TE], F32, "e_acm")
    RW4 = sb([128, 4, TE], F32, "e_rw4")

    def view(ap):
        v = T(ap)
        v.tr = RW4
        return v
    BCM = view(RW4.t[:, 0:2, :])
    BH = view(RW4.t[:, 2, :])
    KH = view(RW4.t[:, 3, :])
    Q4 = view(RW4.t[0:64, :, :])
    VS = sb([128, 2, TE], F32, "e_vs")
    ECL = sb([128, TE], F32, "e_ecl")
    CL = sb([128, TE], F32, "e_cl")
    LW = sb([128, TE], F32, "e_lw")
    AA = sb([128, TE], F32, "e_aa")
    KKN = sb([128, TE], F32, "e_kkn")
    KF = sb([128, TE], F32, "e_kf")
    BV = sb([128, TE], F32, "e_bv")
    RSs = [sb([128, TE], F32, f"e_rs{i}") for i in range(2)]
    KSs = [sb([128, TE], F32, f"e_ks{i}") for i in range(2)]
    XS = sb([128, TE], F32, "e_xs")
    TXW = XS
    cosq = sb([64, TE], F32, "e_cos")
    sinq = sb([64, TE], F32, "e_sin")
    small = Ring([sb([128, 128], F32, f"e_s{i}") for i in range(16)])
    smtm = Ring([sb([128, 128], F32, f"e_m{i}") for i in range(7)])
    st = sb([128, 8], F32, "e_st")
    Vtm = Ring([sb([128, 128], F32, f"e_vtm{i}") for i in range(3)])
    pbig = Ring(cx.ps_views(3, 256))
    psm = Ring(cx.ps_views(5, 128))

    def mm(out_t, out_ap, lhsT_t, lhsT_ap, rhs_t, rhs_ap, start=True, stop=True):
        op('pe', lambda e: e.matmul(out_ap, lhsT_ap, rhs_ap, start=start, stop=stop), reads=[lhsT_t, rhs_t], writes=[out_t])

    scale = 1.0 / 8.0
    vtm_prev = None
    op('dve', lambda e: e.tensor_scalar(B[:, :, 0:2], g[:].unsqueeze(2).to_broadcast([128, KC, 2]), 0.0, None, ALU.mult), reads=[g], writes=[B])

    if layer2:
        v2t = sb([32, 256], F32, "e_v2")
        dma('sp', v2t[:], P['v2'], writes=[v2t], track=v2t)
        Weff = sb([128, KC, 64], F32, "e_weff")
        VFt = sb([128, 2, TE], F32, "e_vft")
        v1t = tmp.next()
        v1v = v1t[:].rearrange("p (c m) -> p c m", m=32)
        dma('sp', v1v, P['v1'], writes=[v1t], track=v1t)
        muv = tmp.next()
        dma('sp', muv[:, 0:8], P['muv'], writes=[muv], track=muv)
        M12 = ACM[:].rearrange("p a t -> p (a t)").rearrange("p (c m) -> p c m", m=64)
        for c in range(8):
            op('dve', lambda e, c=c: e.tensor_scalar(M12[:, c, 32:64], v1v[:, c, :], muv[:, c:c + 1], None, ALU.mult), reads=[v1t, muv], writes=[ACM])
            op('dve', lambda e, c=c: e.tensor_tensor(M12[:, c, 0:32], v1v[:, c, :], M12[:, c, 32:64], ALU.subtract), reads=[v1t, ACM], writes=[ACM])
        for dk2 in range(8):
            pw0, pw1 = psm.next(), psm.next()
            for c in range(8):
                piece = xr.next()
                dma('sp', piece[:], P['wvT'][c * 128:(c + 1) * 128, dk2 * 256:(dk2 + 1) * 256], writes=[piece], track=piece)
                mm(pw0, pw0[:, 0:64], piece, piece[:, 0:128], ACM, M12[:, c, :], start=(c == 0), stop=(c == 7))
                mm(pw1, pw1[:, 0:64], piece, piece[:, 128:256], ACM, M12[:, c, :], start=(c == 0), stop=(c == 7))
            op('act', lambda e, dk2=dk2, pw0=pw0: e.copy(Weff[:, 2 * dk2, :], pw0[:, 0:64]), reads=[pw0], writes=[Weff])
            op('dve', lambda e, dk2=dk2, pw1=pw1: e.tensor_copy(Weff[:, 2 * dk2 + 1, :], pw1[:, 0:64]), reads=[pw1], writes=[Weff])

    for t0 in range(0, ntok, TE):
        first_tile = (t0 == 0)
        src = x_src.ap.rearrange("(k p) t -> p k t", p=128)[:, :, t0:t0 + TE]
        rstd = tmp.next()
        pss = pbig.next()
        for k in range(KC):
            xa = xr.next()
            dma('sp', xa[:], src[:, k, :], reads=[x_src], writes=[xa], track=xa)
            op('act', lambda e, xa=xa: e.activation(sq[:], xa[:], AF.Square), reads=[xa], writes=[sq])
            mm(pss, pss[:], ones, ones[:], sq, sq[:], start=(k == 0), stop=(k == KC - 1))
        op('act', lambda e: e.activation(rstd[:], pss[:], AF.Sqrt, bias=cv[:, 0:1], scale=1.0 / D), reads=[pss, cv], writes=[rstd])
        op('dve', lambda e: e.reciprocal(rstd[:], rstd[:]), reads=[rstd], writes=[rstd])
        for k in range(KC):
            xa = xr.next()
            dma('sp', xa[:], src[:, k, :], reads=[x_src], writes=[xa], track=xa)
            op('dve', lambda e, k=k, xa=xa: e.scalar_tensor_tensor(B[:, k, 2:TE + 2], xa[:], g[:, k:k + 1], rstd[:], ALU.mult, ALU.mult),
               reads=[xa, g, rstd], writes=[B])

        if dbg <= 0:
            op('pool', lambda e: e.memset(Yt[:], 0.0), writes=[Yt])
            op('dve', lambda e: e.tensor_copy(Yt[:, 0, :], B[:, 0, 2:TE + 2]), reads=[B], writes=[Yt])
            ev = dma('sp', y_dst.ap.rearrange("(j p) t -> p j t", p=128)[:, :, t0:t0 + TE], Yt[:], reads=[Yt], writes=[y_dst], track=Yt)
            cx.out_events.append(ev)
            continue

        if layer2:
            lvp = pbig.next()
            for k in range(KC):
                mm(lvp, lvp[0:32, :], Weff, Weff[:, k, 0:32], B, f32(B[:, k, 2:TE + 2]), start=(k == 0), stop=False)
                mm(lvp, lvp[0:32, :], Weff, Weff[:, k, 32:64], B, f32(B[:, k, 1:TE + 1]), start=False, stop=(k == KC - 1))
            LV = tmp.next()
            op('act', lambda e: e.copy(LV[0:32, :], lvp[0:32, :]), reads=[lvp], writes=[LV])
            for j, sgv in ((0, KKN), (1, KF)):
                psg = pbig.next()
                mm(psg, psg[:], v2t, v2t[0:32, j * 128:(j + 1) * 128], LV, LV[0:32, :])
                op('act', lambda e, j=j, sgv=sgv, psg=psg: e.activation(sgv[:], psg[:], AF.Sigmoid, bias=vec[:, V_V0 + j:V_V0 + j + 1]), reads=[psg, vec], writes=[sgv])
            dma('sp', VFt[:], vf_src.ap.rearrange("(j p) t -> p j t", p=128)[:, :, t0:t0 + TE], reads=[vf_src], writes=[VFt], track=VFt)

        def proj(gi, c0, m, dst_t, dst_ap, eng='act'):
            pp = pbig.next()
            for k in range(KC):
                mm(pp, pp[0:m, :], W, W[:, gi, k, c0:c0 + m], B, B[:, k, 2:TE + 2], start=(k == 0), stop=(k == KC - 1))
            if eng == 'act':
                op('act', lambda e: e.copy(dst_ap, pp[0:m, :]), reads=[pp], writes=[dst_t])
            else:
                op('dve', lambda e: e.tensor_copy(dst_ap, pp[0:m, :]), reads=[pp], writes=[dst_t])

        proj(0, 0, 128, Zx, Zx[:, 1:TE + 1])
        if dbg <= 0.2:
            op('pool', lambda e: e.memset(Yt[:], 0.0), writes=[Yt])
            ev = dma('sp', y_dst.ap.rearrange("(j p) t -> p j t", p=128)[:, :, t0:t0 + TE], Yt[:], reads=[Yt], writes=[y_dst], track=Yt)
            cx.out_events.append(ev)
            continue
        proj(0, 128, 128, KV, KV[:, 128:128 + TE], 'dve')
        if dbg <= 0.3:
            op('pool', lambda e: e.memset(Yt[:], 0.0), writes=[Yt])
            ev = dma('sp', y_dst.ap.rearrange("(j p) t -> p j t", p=128)[:, :, t0:t0 + TE], Yt[:], reads=[Yt], writes=[y_dst], track=Yt)
            cx.out_events.append(ev)
            continue
        for j in range(2):
            proj(1, j * 128, 128, Z['r'], Z['r'][:, j, 1:TE + 1], 'act' if j == 0 else 'dve')
            proj(2, j * 128, 128, Z['k'], Z['k'][:, j, 1:TE + 1], 'act' if j == 0 else 'dve')
            proj(3, j * 128, 128, Z['v'], Z['v'][:, j, 1:TE + 1], 'act' if j == 0 else 'dve')
            proj(4, j * 128, 128, ZG, ZG[:, j, :], 'act')
            proj(6, j * 128, 128, ZBG, ZBG[:, j, :], 'dve')

        if dbg <= 0.4:
            op('pool', lambda e: e.memset(Yt[:], 0.0), writes=[Yt])
            ev = dma('sp', y_dst.ap.rearrange("(j p) t -> p j t", p=128)[:, :, t0:t0 + TE], Yt[:], reads=[Yt], writes=[y_dst], track=Yt)
            cx.out_events.append(ev)
            continue
        def shift(zt, view, mucol):
            d = tmp.next()
            op('dve', lambda e: e.tensor_tensor(d[:], view(0), view(1), ALU.subtract), reads=[zt], writes=[d])
            return d

        for n, mu0 in (('r', V_MUR), ('k', V_MUK), ('v', V_MUV)):
            for j in range(2):
                zt = Z[n]
                d = tmp.next()
                op('dve', lambda e, zt=zt, j=j, d=d: e.tensor_tensor(d[:], zt[:, j, 0:TE], zt[:, j, 1:TE + 1], ALU.subtract), reads=[zt], writes=[d])
                dst_t, dst_ap = {'r': (RSs[j], RSs[j][:]), 'k': (KSs[j], KSs[j][:]), 'v': (VS, VS[:, j, :])}[n]
                op('dve', lambda e, zt=zt, j=j, d=d, mc=mu0 + j, dst_ap=dst_ap: e.scalar_tensor_tensor(dst_ap, d[:], vec[:, mc:mc + 1], zt[:, j, 1:TE + 1], ALU.mult, ALU.add),
                   reads=[d, vec, zt], writes=[dst_t])
        dx = tmp.next()
        op('dve', lambda e: e.tensor_tensor(dx[:], Zx[:, 0:TE], Zx[:, 1:TE + 1], ALU.subtract), reads=[Zx], writes=[dx])
        op('dve', lambda e: e.scalar_tensor_tensor(XS[:], dx[:], vec[:, V_MUX:V_MUX + 1], Zx[:, 1:TE + 1], ALU.mult, ALU.add),
           reads=[dx, vec, Zx], writes=[XS])
        op('pool', lambda e: e.tensor_copy(Zx[:, 0:1], Zx[:, TE:TE + 1]), reads=[Zx], writes=[Zx])
        op('act', lambda e: e.activation(TXW[0:64, :], XS[0:64, :], AF.Tanh), reads=[XS], writes=[TXW])
        if dbg <= 0.6:
            op('pool', lambda e: e.memset(Yt[:], 0.0), writes=[Yt])
            ev = dma('sp', y_dst.ap.rearrange("(j p) t -> p j t", p=128)[:, :, t0:t0 + TE], Yt[:], reads=[Yt], writes=[y_dst], track=Yt)
            cx.out_events.append(ev)
            continue
        if layer2:
            for j, sgv in ((0, KKN), (1, KF)):
                d2 = tmp.next()
                op('dve', lambda e, j=j, d2=d2: e.tensor_tensor(d2[:], VFt[:, j, :], VS[:, j, :], ALU.subtract), reads=[VFt, VS], writes=[d2])
                op('dve', lambda e, sgv=sgv, d2=d2: e.tensor_tensor(d2[:], d2[:], sgv[:], ALU.mult), reads=[d2, sgv], writes=[d2])
                op('dve', lambda e, j=j, d2=d2: e.tensor_tensor(VS[:, j, :], VS[:, j, :], d2[:], ALU.add), reads=[VS, d2], writes=[VS])
        if not layer2:
            dma('sp', vf_dst.ap.rearrange("(j p) t -> p j t", p=128)[:, :, t0:t0 + TE], VS[:], reads=[VS], writes=[vf_dst], track=VS)

        if dbg <= 1:
            op('pool', lambda e: e.memset(Yt[:], 0.0), writes=[Yt])
            ev = dma('sp', y_dst.ap.rearrange("(j p) t -> p j t", p=128)[:, :, t0:t0 + TE], Yt[:], reads=[Yt], writes=[y_dst], track=Yt)
            cx.out_events.append(ev)
            continue
        posi_t = tmp.next()
        posi = posi_t[:].bitcast(I32)
        posf = tmp.next()
        dma('sp', posi, P['pos'][t0:t0 + TE].partition_broadcast(128), writes=[posi_t], track=posi_t)
        op('dve', lambda e: e.tensor_copy(posf[:], posi), reads=[posi_t], writes=[posf])

        def rope_tables(fcol, cos_t, sin_t, npart):
            ang = tmp.next()
            op('dve', lambda e: e.tensor_scalar(ang[0:npart, :], posf[0:npart, :], cv[0:npart, fcol:fcol + 1], None, ALU.mult), reads=[posf, cv], writes=[ang])
            n1 = tmp.next()
            op('dve', lambda e: e.tensor_scalar(n1[0:npart, :], ang[0:npart, :], 1.0 / TWO_PI, 12582912.0, ALU.mult, ALU.add), reads=[ang], writes=[n1])
            op('dve', lambda e: e.tensor_scalar(n1[0:npart, :], n1[0:npart, :], -12582912.0, None, ALU.add), reads=[n1], writes=[n1])
            op('dve', lambda e: e.scalar_tensor_tensor(ang[0:npart, :], n1[0:npart, :], -TWO_PI, ang[0:npart, :], ALU.mult, ALU.add), reads=[n1, ang], writes=[ang])
            s2 = tmp.next()
            c2 = n1
            op('act', lambda e: e.activation(s2[0:npart, :], ang[0:npart, :], AF.Sin, scale=0.5), reads=[ang], writes=[s2])
            op('act', lambda e: e.activation(c2[0:npart, :], ang[0:npart, :], AF.Sin, bias=cv[0:npart, 3:4], scale=0.5), reads=[ang, cv], writes=[c2])
            op('dve', lambda e: e.scalar_tensor_tensor(sin_t[0:npart, :], s2[0:npart, :], 2.0, c2[0:npart, :], ALU.mult, ALU.mult), reads=[s2, c2], writes=[sin_t])
            op('dve', lambda e: e.tensor_tensor(s2[0:npart, :], s2[0:npart, :], s2[0:npart, :], ALU.mult), reads=[s2], writes=[s2])
            op('dve', lambda e: e.tensor_scalar(cos_t[0:npart, :], s2[0:npart, :], -2.0, 1.0, ALU.mult, ALU.add), reads=[s2], writes=[cos_t])

        ck, sk = tmp.next(), tmp.next()
        rope_tables(5, ck, sk, 128)
        pr = pbig.next()
        mm(pr, pr[:], rm, rm[:], KV, KV[:, 128:128 + TE])
        t_a = tmp.next()
        op('dve', lambda e: e.tensor_tensor(t_a[:], pr[:], sk[:], ALU.mult), reads=[pr, sk], writes=[t_a])
        op('dve', lambda e: e.tensor_tensor(KV[:, 128:128 + TE], KV[:, 128:128 + TE], ck[:], ALU.mult), reads=[KV, ck], writes=[KV])
        op('dve', lambda e: e.tensor_tensor(KV[:, 128:128 + TE], KV[:, 128:128 + TE], t_a[:], ALU.add), reads=[KV, t_a], writes=[KV])
        rope_tables(4, cosq, sinq, 64)
        if dbg <= 2:
            op('pool', lambda e: e.memset(Yt[:], 0.0), writes=[Yt])
            ev = dma('sp', y_dst.ap.rearrange("(j p) t -> p j t", p=128)[:, :, t0:t0 + TE], Yt[:], reads=[Yt], writes=[y_dst], track=Yt)
            cx.out_events.append(ev)
            continue
        for j in range(2):
            RS = RSs[j]
            KS = KSs[j]
            pw = pbig.next()
            mm(pw, pw[:], w2a2, w2a2[0:64, j * 128:(j + 1) * 128], TXW, TXW[0:64, :])
            op('act', lambda e: e.activation(LW[:], pw[:], AF.Sigmoid, bias=vec[:, V_W0 + j:V_W0 + j + 1]), reads=[pw, vec], writes=[LW])
            op('dve', lambda e: e.tensor_scalar(LW[:], LW[:], -math.exp(-0.5), None, ALU.mult), reads=[LW], writes=[LW])
            pa = pbig.next()
            mm(pa, pa[:], w2a2, w2a2[64:128, j * 128:(j + 1) * 128], XS, XS[64:128, :])
            op('act', lambda e: e.activation(AA[:], pa[:], AF.Sigmoid, bias=vec[:, V_A0 + j:V_A0 + j + 1]), reads=[pa, vec], writes=[AA])
            op('dve', lambda e: e.tensor_tensor_scan(CL[:], rst[:], LW[:], 0.0, ALU.mult, ALU.add), reads=[rst, LW], writes=[CL])
            kk = tmp.next()
            op('dve', lambda e: e.tensor_scalar(kk[:], KS[:], vec[:, V_KK + j:V_KK + j + 1], None, ALU.mult), reads=[KS, vec], writes=[kk])
            kk2 = tmp.next()
            op('pool', lambda e: e.tensor_tensor(kk2[:], kk[:], kk[:], ALU.mult), reads=[kk], writes=[kk2])
            pk = pbig.next()
            mm(pk, pk[:], bones, bones[:], kk2, kk2[:])
            op('act', lambda e: e.activation(kk2[:], pk[:], AF.Sqrt, bias=cv[:, 1:2]), reads=[pk, cv], writes=[kk2])
            op('dve', lambda e: e.reciprocal(kk2[:], kk2[:]), reads=[kk2], writes=[kk2])
            op('dve', lambda e: e.tensor_tensor(KKN[:], kk[:], kk2[:], ALU.mult), reads=[kk, kk2], writes=[KKN])
            tk = kk
            op('dve', lambda e: e.tensor_scalar(tk[:], AA[:], vec[:, V_KA + j:V_KA + j + 1], vec[:, V_OMKA + j:V_OMKA + j + 1], ALU.mult, ALU.add),
               reads=[AA, vec], writes=[tk])
            op('dve', lambda e: e.tensor_tensor(KF[:], KS[:], tk[:], ALU.mult), reads=[KS, tk], writes=[KF])
            op('act', lambda e: e.activation(ECL[:], CL[:], AF.Exp), reads=[CL], writes=[ECL])
            encl = tmp.next()
            op('act', lambda e: e.activation(encl[:], CL[:], AF.Exp, scale=-1.0), reads=[CL], writes=[encl])
            eclm = kk2
            op('dve', lambda e: e.tensor_tensor(eclm[:], CL[:], LW[:], ALU.subtract), reads=[CL, LW], writes=[eclm])
            op('act', lambda e: e.activation(eclm[:], eclm[:], AF.Exp), reads=[eclm], writes=[eclm])
            ecL = tmp.next()
            clv = CL[:].rearrange("p (c l) -> p c l", l=LCH)
            op('dve', lambda e: e.tensor_tensor(ecL[:].rearrange("p (c l) -> p c l", l=LCH), clv[:, :, LCH - 1:LCH].to_broadcast([128, NCH, LCH]), clv, ALU.subtract),
               reads=[CL], writes=[ecL])
            op('act', lambda e: e.activation(ecL[:], ecL[:], AF.Exp), reads=[ecL], writes=[ecL])
            op('dve', lambda e: e.scalar_tensor_tensor(ACM[:, 0, :], KKN[:], -1.0, eclm[:], ALU.mult, ALU.mult), reads=[KKN, eclm], writes=[ACM])
            op('pool', lambda e: e.tensor_tensor(ACM[:, 1, :], RS[:], ECL[:], ALU.mult), reads=[RS, ECL], writes=[ACM])
            bb = tk
            op('dve', lambda e: e.tensor_tensor(bb[:], KKN[:], AA[:], ALU.mult), reads=[KKN, AA], writes=[bb])
            op('dve', lambda e: e.tensor_tensor(BCM[:, 0, :], bb[:], encl[:], ALU.mult), reads=[bb, encl], writes=[BCM])
            op('pool', lambda e: e.tensor_tensor(BCM[:, 1, :], KF[:], encl[:], ALU.mult), reads=[KF, encl], writes=[BCM])
            op('dve', lambda e: e.tensor_tensor(BH[:], bb[:], ecL[:], ALU.mult), reads=[bb, ecL], writes=[BH])
            op('pool', lambda e: e.tensor_tensor(KH[:], KF[:], ecL[:], ALU.mult), reads=[KF, ecL], writes=[KH])
            rk = encl
            op('dve', lambda e: e.scalar_tensor_tensor(rk[:], RS[:], vec[:, V_RK + j:V_RK + j + 1], KF[:], ALU.mult, ALU.mult), reads=[RS, vec, KF], writes=[rk])
            pbn = pbig.next()
            mm(pbn, pbn[:], bones, bones[:], rk, rk[:])
            op('dve', lambda e: e.tensor_tensor(BV[:], pbn[:], VS[:, j, :], ALU.mult), reads=[pbn, VS], writes=[BV])

            py = pbig.next()
            if dbg < 4:
                mm(py, py[:], bones, bones[:], BV, BV[:])
            for c in range(NCH if dbg > 3 else 0):
                cs = slice(c * LCH, (c + 1) * LCH)
                tms = []
                DW = smtm.next()
                op('dve', lambda e, DW=DW, c=c: e.tensor_scalar(DW[:, 0:64], idb2[:], ECL[:, (c + 1) * LCH - 1:(c + 1) * LCH], None, ALU.mult), reads=[idb2, ECL], writes=[DW])
                for src_t, src_ap in ((ACM, ACM[:, 0, cs]), (BH, BH[:, cs]), (VS, VS[:, j, cs]), (KH, KH[:, cs]), (ACM, ACM[:, 1, cs]), (DW, DW[:, 0:64])):
                    pt = psm.next()
                    mm(pt, pt[0:64, :], src_t, src_ap, ident, ident[:])
                    tt = smtm.next()
                    op('act' if len(tms) % 2 == 0 else 'dve',
                       (lambda e, tt=tt, pt=pt: e.copy(tt[0:64, :], pt[0:64, :])) if len(tms) % 2 == 0 else (lambda e, tt=tt, pt=pt: e.tensor_copy(tt[0:64, :], pt[0:64, :])),
                       reads=[pt], writes=[tt])
                    tms.append(tt)
                ATm, BHm, Vm, KHm, RTm, DWm = tms
                for hh in range(2 if dbg > 3.1 else 0):
                    h = 2 * j + hh
                    hs = slice(hh * 64, (hh + 1) * 64)
                    fs = slice(hh * 64, (hh + 1) * 64)
                    p1 = psm.next()
                    mm(p1, p1[0:64, :], ACM, ACM[hs, 0, cs], BCM, BCM[hs, :, cs])
                    NN = small.next()
                    op('dve', lambda e, NN=NN, p1=p1: e.tensor_tensor(NN[0:64, :], p1[0:64, :], msk[0:64, 0, :], ALU.mult), reads=[p1, msk], writes=[NN])
                    p2 = psm.next()
                    mm(p2, p2[0:64, :], BCM, BCM[hs, 0, cs], ACM, ACM[hs, :, cs])
                    TR = small.next()
                    op('dve', lambda e, TR=TR, p2=p2: e.tensor_tensor(TR[0:64, :], p2[0:64, :], msk[0:64, 1, :], ALU.mult), reads=[p2, msk], writes=[TR])
                    p3 = psm.next()
                    mm(p3, p3[0:64, 0:64], BCM, BCM[hs, 1, cs], ACM, ACM[hs, 1, cs])
                    QK = small.next()
                    op('dve', lambda e, QK=QK, p3=p3: e.tensor_tensor(QK[0:64, 0:64], p3[0:64, 0:64], msk[0:64, 2, 0:64], ALU.mult), reads=[p3, msk], writes=[QK])
                    if dbg <= 3.2:
                        continue
                    PT = small.next()
                    op('act', lambda e, PT=PT, NN=NN: e.copy(PT[0:64, 0:64], NN[0:64, 0:64]), reads=[NN], writes=[PT])
                    op('dve', lambda e, PT=PT, NN=NN: e.tensor_tensor(PT[0:64, 64:128], NN[0:64, 0:64], ident[0:64, 0:64], ALU.add), reads=[NN, ident], writes=[PT])
                    PkT_t, PkT = TR, TR[0:64, 0:64]
                    nsteps = 7
                    for kstep in range(nsteps - 1):
                        last = (kstep == nsteps - 2)
                        pA = psm.next()
                        if kstep == 0:
                            mm(pA, pA[0:64, 0:64], PkT_t, PkT, PT, PT[0:64, 0:64])
                        elif not last:
                            mm(pA, pA[0:64, :], PkT_t, PkT, PT, PT[0:64, :])
                        else:
                            mm(pA, pA[0:64, 64:128], PkT_t, PkT, PT, PT[0:64, 64:128])
                        PTn = small.next()
                        if not last:
                            pB = psm.next()
                            mm(pB, pB[0:64, 0:64], PT, PT[0:64, 0:64], PkT_t, PkT)
                            PkTn = small.next()
                            op('act', lambda e, PkTn=PkTn, pB=pB: e.copy(PkTn[0:64, 0:64], pB[0:64, 0:64]), reads=[pB], writes=[PkTn])
                            op('act', lambda e, PTn=PTn, pA=pA: e.copy(PTn[0:64, 0:64], pA[0:64, 0:64]), reads=[pA], writes=[PTn])
                        if kstep == 0:
                            op('dve', lambda e, PTn=PTn, PT=PT: e.tensor_copy(PTn[0:64, 64:128], PT[0:64, 64:128]), reads=[PT], writes=[PTn])
                        else:
                            op('dve', lambda e, PTn=PTn, PT=PT, pA=pA: e.tensor_tensor(PTn[0:64, 64:128], pA[0:64, 64:128], PT[0:64, 64:128], ALU.add),
                               reads=[pA, PT], writes=[PTn])
                        PT = PTn
                        if not last:
                            PkT_t, PkT = PkTn, PkTn[0:64, 0:64]
                    Tt, Tm = PT, PT[0:64, 64:128]
                    if dbg <= 3.3:
                        continue
                    p4 = psm.next()
                    mm(p4, p4[0:64, 0:64], Tt, Tm, TR, TR[0:64, 64:128])
                    mm(p4, p4[0:64, 64:128], Tt, Tm, BHm, BHm[0:64, fs])
                    GZ = small.next()
                    op('act', lambda e, GZ=GZ, p4=p4: e.copy(GZ[0:64, :], p4[0:64, :]), reads=[p4], writes=[GZ])
                    p5 = psm.next()
                    mm(p5, p5[0:64, :], NN, NN[0:64, 64:128], GZ, GZ[0:64, :])
                    HX = small.next()
                    op('dve', lambda e, HX=HX, p5=p5, QK=QK: e.tensor_tensor(HX[0:64, 0:64], p5[0:64, 0:64], QK[0:64, 0:64], ALU.add), reads=[p5, QK], writes=[HX])
                    op('dve', lambda e, HX=HX, p5=p5, KHm=KHm, fs=fs: e.tensor_tensor(HX[0:64, 64:128], p5[0:64, 64:128], KHm[0:64, fs], ALU.add), reads=[p5, KHm], writes=[HX])
                    p6 = psm.next()
                    mm(p6, p6[0:64, :], ATm, ATm[0:64, fs], GZ, GZ[0:64, :], start=True, stop=False)
                    mm(p6, p6[0:64, 0:64], RTm, RTm[0:64, fs], ident, ident[0:64, 0:64], start=False, stop=True)
                    mm(p6, p6[0:64, 64:128], ident, ident[0:64, 0:64], DWm, DWm[0:64, fs], start=False, stop=True)
                    REM = small.next()
                    op('act', lambda e, REM=REM, p6=p6: e.copy(REM[0:64, :], p6[0:64, :]), reads=[p6], writes=[REM])
                    if dbg <= 3.4:
                        continue
                    Sc = S[h][si[h] % 2]
                    Sn = S[h][(si[h] + 1) % 2]
                    si[h] += 1
                    mm(py, py[hs, cs], Vm, Vm[0:64, fs], HX, HX[0:64, 0:64], start=True, stop=False)
                    mm(py, py[hs, cs], Sc, Sc[:], REM, REM[0:64, 0:64], start=False, stop=True)
                    p7 = psm.next()
                    mm(p7, p7[0:64, 0:64], REM, REM[0:64, 64:128], Sc, Sc[:], start=True, stop=False)
                    mm(p7, p7[0:64, 0:64], HX, HX[0:64, 64:128], Vm, Vm[0:64, fs], start=False, stop=True)
                    op('dve', lambda e, Sn=Sn, p7=p7: e.tensor_copy(Sn[:], p7[0:64, 0:64]), reads=[p7], writes=[Sn])
            yv = tmp.next()
            op('act', lambda e: e.copy(yv[:], py[:]), reads=[py], writes=[yv])
            pm_ = pbig.next()
            mm(pm_, pm_[:], bones, bones[:], yv, yv[:])
            yc = tmp.next()
            op('dve', lambda e: e.scalar_tensor_tensor(yc[:], pm_[:], -1.0 / 64, yv[:], ALU.mult, ALU.add), reads=[pm_, yv], writes=[yc])
            y2 = yv
            op('pool', lambda e: e.tensor_tensor(y2[:], yc[:], yc[:], ALU.mult), reads=[yc], writes=[y2])
            pv_ = pbig.next()
            mm(pv_, pv_[:], bones, bones[:], y2, y2[:])
            op('act', lambda e: e.activation(y2[:], pv_[:], AF.Sqrt, bias=cv[:, 2:3], scale=1.0 / 64), reads=[pv_, cv], writes=[y2])
            op('dve', lambda e: e.reciprocal(y2[:], y2[:]), reads=[y2], writes=[y2])
            op('dve', lambda e: e.tensor_tensor(yc[:], yc[:], y2[:], ALU.mult), reads=[yc, y2], writes=[yc])
            op('dve', lambda e: e.tensor_scalar(yc[:], yc[:], vec[:, V_LNW + j:V_LNW + j + 1], vec[:, V_LNB + j:V_LNB + j + 1], ALU.mult, ALU.add),
               reads=[yc, vec], writes=[yc])
            op('dve', lambda e: e.tensor_tensor(yc[:], yc[:], BV[:], ALU.add), reads=[yc, BV], writes=[yc])
            sgt = y2
            op('act', lambda e: e.activation(sgt[:], ZG[:, j, :], AF.Silu), reads=[ZG], writes=[sgt])
            op('dve', lambda e: e.tensor_tensor(Yt[:, j, :], yc[:], sgt[:], ALU.mult), reads=[yc, sgt], writes=[Yt])

        for n in ('r', 'k', 'v'):
            zt = Z[n]
            op('pool', lambda e, zt=zt: e.tensor_copy(zt[:, :, 0:1], zt[:, :, TE:TE + 1]), reads=[zt], writes=[zt])

        for jj in range(2):
            proj(5, jj * 128, 128, ACM, ACM[:, jj, :], 'act' if jj == 0 else 'dve')
        for h in range(4):
            hs_ = slice((h % 2) * 64, (h % 2) * 64 + 64)
            pq = pbig.next()
            mm(pq, pq[0:64, :], rm, rm[hs_, hs_], ACM, ACM[hs_, h // 2, :])
            pq2 = pbig.next()
            mm(pq2, pq2[0:64, :], ident, ident[hs_, hs_], ACM, ACM[hs_, h // 2, :])
            t_b = tmp.next()
            op('dve', lambda e, pq=pq, t_b=t_b: e.tensor_tensor(t_b[0:64, :], pq[0:64, :], sinq[:], ALU.mult), reads=[pq, sinq], writes=[t_b])
            op('dve', lambda e, h=h, pq2=pq2: e.tensor_tensor(Q4[:, h, :], pq2[0:64, :], cosq[:], ALU.mult), reads=[pq2, cosq], writes=[Q4])
            op('dve', lambda e, h=h, t_b=t_b: e.tensor_tensor(Q4[:, h, :], Q4[:, h, :], t_b[0:64, :], ALU.add), reads=[Q4, t_b], writes=[Q4])

        if dbg <= 4:
            op('pool', lambda e: e.memset(Yt[:, 2:4, :], 0.0), writes=[Yt])
        for bq in range(TE // 128 if dbg > 4 else 0):
            blk_first = first_tile and bq == 0
            pvt = psm.next()
            mm(pvt, pvt[:], KV, KV[:, 128 + bq * 128:128 + (bq + 1) * 128], ident, ident[:])
            vt_cur = Vtm.next()
            op('act', lambda e, vt_cur=vt_cur, pvt=pvt: e.copy(vt_cur[:], pvt[:]), reads=[pvt], writes=[vt_cur])
            if vtm_prev is None:
                vtm_prev = vt_cur
            po = pbig.next()
            for h in range(4):
                pssc = pbig.next()
                mm(pssc, pssc[:], Q4, Q4[:, h, bq * 128:(bq + 1) * 128], KV, KV[0:64, bq * 128:bq * 128 + 256])
                sc = tmp.next()
                op('dve', lambda e, sc=sc, pssc=pssc: e.scalar_tensor_tensor(sc[:], pssc[:], scale, mb[:, 1 if blk_first else 0, :], ALU.mult, ALU.add),
                   reads=[pssc, mb], writes=[sc])
                op('dve', lambda e, sc=sc: e.reduce_max(st[:, 0:1], sc[:], axis=AX.X), reads=[sc], writes=[st])
                op('dve', lambda e, h=h: e.tensor_scalar(st[:, 1:2], st[:, 0:1], sinkb[:, h:h + 1], -1.0, ALU.max, ALU.mult), reads=[st, sinkb], writes=[st])
                op('dve', lambda e: e.memset(st[:, 2:3], 0.0), writes=[st])
                pe_ = tmp.next()
                op('act', lambda e, sc=sc, pe_=pe_: e.activation(pe_[:], sc[:], AF.Exp, bias=st[:, 1:2], accum_out=st[:, 2:3]), reads=[sc, st], writes=[pe_, st])
                op('act', lambda e, h=h: e.activation(st[:, 3:4], sinkb[:, h:h + 1], AF.Exp, bias=st[:, 1:2]), reads=[sinkb, st], writes=[st])
                op('dve', lambda e: e.tensor_tensor(st[:, 4:5], st[:, 2:3], st[:, 3:4], ALU.add), reads=[st], writes=[st])
                op('dve', lambda e: e.reciprocal(st[:, 5:6], st[:, 4:5]), reads=[st], writes=[st])
                op('dve', lambda e, pe_=pe_: e.tensor_scalar(pe_[:], pe_[:], st[:, 5:6], None, ALU.mult), reads=[pe_, st], writes=[pe_])
                pts = []
                for half in range(2):
                    ptp = psm.next()
                    mm(ptp, ptp[:], pe_, pe_[:, half * 128:(half + 1) * 128], ident, ident[:])
                    pT = small.next()
                    op('act' if half == 0 else 'dve',
                       (lambda e, pT=pT, ptp=ptp: e.copy(pT[:], ptp[:])) if half == 0 else (lambda e, pT=pT, ptp=ptp: e.tensor_copy(pT[:], ptp[:])),
                       reads=[ptp], writes=[pT])
                    pts.append(pT)
                osl = po[(h % 2) * 64:(h % 2) * 64 + 64, (h // 2) * 128:(h // 2) * 128 + 128]
                mm(po, osl, vtm_prev, vtm_prev[:, 64:128], pts[0], pts[0][:], start=True, stop=False)
                mm(po, osl, vt_cur, vt_cur[:, 64:128], pts[1], pts[1][:], start=False, stop=True)
            vtm_prev = vt_cur
            for jj in range(2):
                sgt = tmp.next()
                op('act', lambda e, sgt=sgt, jj=jj, bq=bq: e.activation(sgt[:, 0:128], ZBG[:, jj, bq * 128:(bq + 1) * 128], AF.Silu), reads=[ZBG], writes=[sgt])
                op('dve', lambda e, sgt=sgt, jj=jj, bq=bq, po=po: e.tensor_tensor(Yt[:, 2 + jj, bq * 128:(bq + 1) * 128], po[:, jj * 128:(jj + 1) * 128], sgt[:, 0:128], ALU.mult),
                   reads=[po, sgt], writes=[Yt])
        op('pool', lambda e: e.tensor_copy(KV[:, 0:128], KV[:, TE:TE + 128]), reads=[KV], writes=[KV])
        if layer2:
            op('pool', lambda e: e.tensor_copy(B[:, :, 1:2], B[:, :, TE + 1:TE + 2]), reads=[B], writes=[B])
        ev = dma('sp', y_dst.ap.rearrange("(j p) t -> p j t", p=128)[:, :, t0:t0 + TE], Yt[:], reads=[Yt], writes=[y_dst], track=Yt)
        cx.out_events.append(ev)
    cx.release(mark)


A_W = 1024
SHIFT_W = 3200
C_R, C_K, C_V, C_XW, C_AG, C_Q, C_KB, C_VB, C_BG = 0, 1024, 2048, 3072, 3200, 4224, 5248, 5504, 5760


def make_consts_even():
    c = make_consts()
    i = np.arange(128)
    c['bones'] = ((i[:, None] // 64) == (i[None, :] // 64)).astype(np.float32)
    c['idb2'] = ((i[:, None] % 64) == np.arange(64)[None, :]).astype(np.float32)
    t = np.arange(64)
    SL = (t[:, None] > t[None, :]).astype(np.float32)
    SU = SL.T.copy()
    UI = (t[None, :] >= t[:, None]).astype(np.float32)
    em = np.zeros((128, 3, 128), np.float32)
    em[0:64, 0, 0:64] = SL
    em[0:64, 0, 64:128] = SL
    em[0:64, 1, 0:64] = SU
    em[0:64, 1, 64:128] = UI
    em[0:64, 2, 0:64] = UI
    c['emsk'] = em
    q = np.arange(128)
    NEG = -30000.0
    am = np.full((128, 2, 256), NEG, np.float32)
    prev_ok = q[None, :] > q[:, None]
    cur_ok = q[None, :] <= q[:, None]
    am[:, 0, 0:128] = np.where(prev_ok, 0.0, NEG)
    am[:, 0, 128:256] = np.where(cur_ok, 0.0, NEG)
    am[:, 1, 128:256] = np.where(cur_ok, 0.0, NEG)
    c['amask'] = am
    rmat = np.zeros((128, 128), np.float32)
    for p in range(128):
        d = p % 64
        if d < 8:
            rmat[p + 8, p] = -1.0
        elif d < 16:
            rmat[p - 8, p] = 1.0
    c['ropem'] = rmat
    rst = np.ones((128, TE), np.float32)
    rst[:, ::LCH] = 0.0
    c['rst'] = rst
    cvec = np.zeros((128, 8), np.float32)
    cvec[:, 0] = RMS_EPS
    cvec[:, 1] = 1e-12
    cvec[:, 2] = GN_EPS
    cvec[:, 3] = math.pi / 2
    inv = np.power(np.float32(500000.0), -np.arange(8, dtype=np.float32) / np.float32(8)).astype(np.float32)
    fq = np.zeros(128, np.float32)
    for p in range(128):
        d = p % 64
        if d < 16:
            fq[p] = inv[d % 8]
    fk = fq.copy()
    fk[64:] = 0.0
    cvec[:, 4] = fq
    cvec[:, 5] = fk
    c['cvec'] = cvec
    return c


def prep_even(e, hg, P):
    W = P['e_w_in'][e]
    hs = slice(hg * 256, (hg + 1) * 256)

    def cols(base):
        return W[:, base + hg * 256: base + (hg + 1) * 256]
    kvb = np.concatenate([W[:, C_KB + hg * 64: C_KB + (hg + 1) * 64], W[:, C_VB + hg * 64: C_VB + (hg + 1) * 64]], axis=1)
    wcols = np.concatenate([W[:, C_XW:C_XW + 128], kvb, cols(C_R), cols(C_K), cols(C_V), cols(C_AG), cols(C_Q), cols(C_BG)], axis=1)
    mu = P['e_mu'][e]
    vec = np.zeros((128, 40), np.float32)

    def put(col, v256):
        vec[:, col:col + 2] = v256.reshape(2, 128).T
    put(0, mu[C_R:C_R + 1024][hs]); put(2, mu[C_K:C_K + 1024][hs]); put(4, mu[C_V:C_V + 1024][hs])
    put(6, P['rwkv_w0'][e][hs]); put(8, P['rwkv_a0'][e][hs]); put(10, P['rwkv_k_k'][e][hs]); put(12, P['rwkv_k_a'][e][hs])
    put(16, P['rwkv_r_k'][e].reshape(-1)[hs]); put(18, P['rwkv_ln_w'][e][hs]); put(20, P['rwkv_ln_b'][e][hs])
    if e > 0:
        put(22, P['rwkv_v0'][e - 1][hs])
    vec[:, 24] = mu[C_XW:C_XW + 128]
    w2a2 = np.concatenate([P['rwkv_w2'][e][:, hs], P['rwkv_a2'][e][:, hs]], axis=0)
    out = {
        'norm': vec_pk(P['e_norm'][e]),
        'w_in': tile_w(wcols),
        'vec': vec,
        'w2a2': np.ascontiguousarray(w2a2, dtype=np.float32),
        'sinks': np.ascontiguousarray(P['attn_sinks'][e][hg * 4:(hg + 1) * 4], dtype=np.float32),
    }
    if e > 0:
        out['wvT'] = np.ascontiguousarray(W[:, C_V:C_V + 1024].T, dtype=np.float32)
        out['v1'] = np.ascontiguousarray(P['rwkv_v1'][e - 1].reshape(8, 128, 32).transpose(1, 0, 2), dtype=np.float32)
        out['muv'] = vec_pk(mu[C_V:C_V + 1024])
        out['v2'] = np.ascontiguousarray(P['rwkv_v2'][e - 1][:, hs], dtype=np.float32)
    return out


SEQ = 8192
_PROG_CACHE = {}


def emit_outproj_stage(cm, w_out, y_src, x_src, x_dst):
    cx, TT = cm.cx, cm.TT
    for t0 in range(0, NTOK, TT):
        cx.dma('pool', cm.Y[:], y_src.ap.rearrange("(k p) t -> p k t", p=128)[:, :, t0:t0 + TT], reads=[y_src], writes=[cm.Y], track=cm.Y)
        cm.out_proj_residual(w_out, x_src, x_dst, t0)


def build_E(e, in_shapes):
    nc = bass.Bass("TRN2", target_bir_lowering=False)
    aps = {k: declare(nc, k, v) for k, v in in_shapes.items()}
    yo = nc.dram_tensor("y", [512, SEQ], F32, kind="ExternalOutput").ap()
    cx = Ctx(nc)
    PP = {k[2:]: aps[k] for k in aps if k.startswith('p_')}
    CC = {k[2:]: aps[k] for k in aps if k.startswith('c_')}
    yd = DT(yo)
    if e == 0:
        vfo = nc.dram_tensor("vf", [256, SEQ], F32, kind="ExternalOutput").ap()
        vd = DT(vfo)
        emit_even_phase(cx, PP, CC, DT(aps['xT']), yd, vd, None, SEQ, False)
        for key, v in vd.wd.items():
            cx.out_events.append((key, v))
    else:
        emit_even_phase(cx, PP, CC, DT(aps['xT']), yd, None, DT(aps['vfin']), SEQ, True)
    cx.finish()
    return nc


def build_O(final, in_shapes):
    nc = bass.Bass("TRN2", target_bir_lowering=False)
    aps = {k: declare(nc, k, v) for k, v in in_shapes.items()}
    xo = nc.dram_tensor("xo", [D, NTOK], F32, kind="ExternalOutput").ap()
    xm = nc.dram_tensor("xmid", [D, NTOK], F32, kind="Internal").ap()
    cx = Ctx(nc)
    CC = {k[2:]: aps[k] for k in aps if k.startswith('c_')}
    cm = Common(cx, CC, 512)
    PP = {k[2:]: aps[k] for k in aps if k.startswith('p_')}
    x_in, y_in, x_mid, x_out = DT(aps['xT']), DT(aps['yT']), DT(xm), DT(xo)
    emit_outproj_stage(cm, aps['wout_e'], y_in, x_in, x_mid)
    if not final:
        emit_odd_layer(cm, PP, x_mid, x_out)
        for key, v in x_out.wd.items():
            cx.out_events.append((key, v))
    else:
        xm2 = nc.dram_tensor("xmid2", [D, NTOK], F32, kind="Internal").ap()
        x_mid2 = DT(xm2)
        emit_odd_layer(cm, PP, x_mid, x_mid2)
        fn = cx.sb([128, KC], F32)
        cx.dma('sp', fn[:], aps['fnorm'], writes=[fn], track=fn)
        emit_final_norm(cm, x_mid2, x_out, fn)
    cx.finish()
    return nc


def _shapes(d):
    return {k: v for k, v in d.items()}


def kernel(**inputs):
    P = {k: np.asarray(v) for k, v in inputs.items()}
    x = P['x'].astype(np.float32, copy=False)
    pos = P['positions'].astype(np.int32, copy=False)
    B_ = x.shape[0]
    cores = list(range(8))
    ce = make_consts_even()
    co = make_consts()
    xT_b = [np.ascontiguousarray(x[b].T) for b in range(B_)]
    vf_cores = None
    for e in range(2):
        in_maps = []
        for c in cores:
            b, hg = c // 4, c % 4
            m = {'xT': xT_b[b]}
            for k, v in ce.items():
                m['c_' + k] = v
            for k, v in prep_even(e, hg, P).items():
                m['p_' + k] = v
            m['p_pos'] = np.ascontiguousarray(pos[b])
            if e > 0:
                m['vfin'] = vf_cores[c]
            in_maps.append(m)
        nc = build_E(e, in_maps[0])
        res = run_bass_kernel_spmd(nc, in_maps, core_ids=cores)
        if e == 0:
            vf_cores = [np.ascontiguousarray(res.results[c]['vf']) for c in cores]
        yT_b = []
        for b in range(B_):
            yT = np.empty((2048, SEQ), np.float32)
            for hg in range(4):
                r = res.results[b * 4 + hg]['y']
                yT[hg * 256:(hg + 1) * 256] = r[0:256]
                yT[1024 + hg * 256:1024 + (hg + 1) * 256] = r[256:512]
            yT_b.append(yT)
        del res
        final = (e == 1)
        in_maps = []
        podd = prep_odd(P['o_norm'][e], P['o_w_in'][e], P['sgu_ln_w'][e], P['sgu_ln_b'][e], P['sgu_ws'][e], P['sgu_bs'][e], P['o_w_out'][e])
        wout_e = tile_w(P['e_w_out'][e])
        for c in cores:
            b, sg = c // 4, c % 4
            sl = slice(sg * NTOK, (sg + 1) * NTOK)
            m = {'xT': np.ascontiguousarray(xT_b[b][:, sl]), 'yT': np.ascontiguousarray(yT_b[b][:, sl]), 'wout_e': wout_e}
            for k, v in co.items():
                m['c_' + k] = v
            for k, v in podd.items():
                m['p_' + k] = v
            if final:
                m['fnorm'] = vec_pk(P['final_norm'])
            in_maps.append(m)
        nc = build_O(final, in_maps[0])
        res = run_bass_kernel_spmd(nc, in_maps, core_ids=cores)
        for b in range(B_):
            xT_b[b] = np.concatenate([res.results[b * 4 + sg]['xo'] for sg in range(4)], axis=1)
        del res
    out = np.stack([np.ascontiguousarray(xT_b[b].T) for b in range(B_)], axis=0)
    return out.astype(np.float32, copy=False)
```

```python
import math
import numpy as np
import concourse.bass as bass
import concourse.mybir as mybir
from concourse.bass_utils import run_bass_kernel_spmd

F32 = mybir.dt.float32
F32R = mybir.dt.float32r
I32 = mybir.dt.int32
AF = mybir.ActivationFunctionType
ALU = mybir.AluOpType
AX = mybir.AxisListType

D = 2048
NTOK = 2048
KC = 16
RMS_EPS = 1e-5
LN_EPS = 1e-5


class T:
    def __init__(self, t):
        self.t = t
        self.w = None
        self.r = {}
        self.dsem = None
        self.tr = self

    def __getitem__(self, k):
        return self.t[k]


class DT(T):
    def __init__(self, ap):
        super().__init__(None)
        self.ap = ap
        self.wd = {}


class Ctx:
    def __init__(self, nc):
        self.nc = nc
        self.eng = {'pe': nc.tensor, 'act': nc.scalar, 'dve': nc.vector, 'pool': nc.gpsimd, 'sp': nc.sync}
        self.sems = {}
        self.cnt = {}
        for k in ('pe', 'act', 'dve', 'pool'):
            self.sems[k] = nc.semaphore('s_' + k).__enter__()
            self.cnt[k] = 0
        self.seen = {}
        self.nsb = 0
        self.nps = 0
        self.ndram = 0
        self.out_events = []
        self.stack = []

    def sb(self, shape, dt=F32, name=None):
        self.nsb += 1
        cmgr = self.nc.sbuf_tensor(name or f"sb{self.nsb}", list(shape), dt)
        t = T(cmgr.__enter__())
        self.stack.append(cmgr)
        return t

    def mark(self):
        return len(self.stack)

    def release(self, mark):
        self.barrier()
        while len(self.stack) > mark:
            self.stack.pop().__exit__(None, None, None)

    def barrier(self):
        for e in ('pe', 'act', 'dve', 'pool', 'sp'):
            for k, v in self.cnt.items():
                if v > 0 and k != e and self.seen.get((e, k), 0) < v:
                    self.eng[e].wait_ge(self.sems[k], v)
                    self.seen[(e, k)] = v

    def ps(self, shape, dt=F32, name=None):
        self.nps += 1
        t = T(self.nc.psum_tensor(name or f"ps{self.nps}", list(shape), dt).__enter__())
        t.excl = True
        return t

    def ps_views(self, nbanks, width):
        banks = []
        for b in range(nbanks):
            self.nps += 1
            banks.append(T(self.nc.psum_tensor(f"pbank{self.nps}", [128, 512], F32).__enter__()))
            banks[-1].excl = True
        out = []
        for i in range(512 // width):
            for bk in banks:
                v = T(bk.t[:, i * width:(i + 1) * width])
                v.tr = bk
                out.append(v)
        return out

    def _dsem(self, t):
        if t.dsem is None:
            key = f"d{len(self.sems)}"
            self.sems[key] = self.nc.semaphore('s_' + key).__enter__()
            self.cnt[key] = 0
            t.dsem = key
        return t.dsem

    def _waits(self, e, reads, writes):
        need = {}
        reads = [t.tr for t in reads]
        writes = [t.tr for t in writes]
        for t in reads:
            if isinstance(t, DT):
                for k, v in t.wd.items():
                    need[k] = max(need.get(k, 0), v)
            elif t.w:
                need[t.w[0]] = max(need.get(t.w[0], 0), t.w[1])
            if getattr(t, 'excl', False):
                for k, v in t.r.items():
                    if k != e:
                        need[k] = max(need.get(k, 0), v)
        for t in writes:
            if t.w and not isinstance(t, DT):
                need[t.w[0]] = max(need.get(t.w[0], 0), t.w[1])
            for k, v in t.r.items():
                need[k] = max(need.get(k, 0), v)
        eng = self.eng[e]
        for k, v in need.items():
            if e == 'pe' and k == 'pe':
                continue
            if self.seen.get((e, k), 0) < v:
                eng.wait_ge(self.sems[k], v)
                self.seen[(e, k)] = v

    def op(self, e, fn, reads=(), writes=()):
        self._waits(e, reads, writes)
        ins = fn(self.eng[e])
        self.cnt[e] += 1
        ins.then_inc(self.sems[e], 1)
        ev = (e, self.cnt[e])
        for t in reads:
            t.tr.r[e] = ev[1]
        for t in writes:
            t.tr.w = ev
            t.tr.r = {}
        return ins

    def dma(self, q, out, in_, reads=(), writes=(), track=None):
        self._waits(q, reads, writes)
        t = track
        key = self._dsem(t)
        ins = self.eng[q].dma_start(out=out, in_=in_)
        self.cnt[key] += 16
        ins.then_inc(self.sems[key], 16)
        ev = (key, self.cnt[key])
        for r_ in reads:
            r_.r[key] = ev[1]
        for w_ in writes:
            if isinstance(w_, DT):
                w_.wd[key] = ev[1]
            else:
                w_.w = ev
                w_.r = {}
        return ev

    def collective(self, kind, in_ap, out_ap, groups, in_dt, out_dt):
        self._waits('pool', [in_dt], [out_dt])
        key = self._dsem(out_dt)
        ins = self.nc.gpsimd.collective_compute(kind, ALU.bypass, replica_groups=groups, ins=[in_ap], outs=[out_ap])
        self.cnt[key] += 16
        ins.then_inc(self.sems[key], 16)
        ev = (key, self.cnt[key])
        in_dt.r[key] = ev[1]
        out_dt.wd[key] = ev[1]
        return ev

    def finish(self):
        for key, v in self.out_events:
            self.eng['sp'].wait_ge(self.sems[key], v)


def r32(ap):
    return ap.bitcast(F32R)


def f32(ap):
    return ap.bitcast(F32)


class Common:
    def __init__(self, cx, consts, TT):
        nc = cx.nc
        self.cx = cx
        self.TT = TT
        self.ones = cx.sb([128, 128], F32R, "ones")
        cx.dma('pool', self.ones[:], consts['ones'], writes=[self.ones], track=self.ones)
        self.tri = cx.sb([128, 128], F32, "tri")
        cx.dma('sp', self.tri[:], consts['tri'], writes=[self.tri], track=self.tri)
        self.ident = cx.sb([128, 128], F32, "ident")
        cx.dma('sp', self.ident[:], consts['ident'], writes=[self.ident], track=self.ident)
        self.A = cx.sb([128, KC, TT], F32, "bufA")
        self.Y = cx.sb([128, KC, TT], F32R, "bufY")
        self.B = cx.sb([128, KC, TT], F32R, "bufB")
        self.wb = [cx.sb([128, KC, 256], F32R, f"wb{i}") for i in range(2)]
        self.wi = 0
        self.pbig = [cx.ps([128, 512], F32, f"pbig{i}") for i in range(4)]
        self.pi = 0
        self.sq = [cx.sb([128, TT], F32R, f"sq{i}") for i in range(1)]
        self.rstd = cx.sb([128, TT], F32, "rstd")
        self.xc = [cx.sb([128, TT], F32, f"xc{i}") for i in range(2)]
        self.xo = [cx.sb([128, TT], F32, f"xo{i}") for i in range(2)]
        self.xci = 0
        self.eps_rms = cx.sb([128, 1], F32, "eps_rms")
        cx.op('dve', lambda e: e.memset(self.eps_rms[:], RMS_EPS), writes=[self.eps_rms])

    def next_w(self):
        w = self.wb[self.wi % len(self.wb)]
        self.wi += 1
        return w

    def next_p(self):
        p = self.pbig[self.pi % len(self.pbig)]
        self.pi += 1
        return p

    def load_w(self, dram_group):
        w = self.next_w()
        self.cx.dma('pool', w[:], dram_group, writes=[w], track=w)
        return w

    def load_x_and_norm(self, x_src, t0, gvec):
        cx, TT = self.cx, self.TT
        A, B = self.A, self.B
        src = x_src.ap.rearrange("(k p) t -> p k t", p=128)[:, :, t0:t0 + TT]
        cx.dma('sp', A[:], src, reads=[x_src], writes=[A], track=A)
        pss = self.next_p()
        for k in range(KC):
            sq = self.sq[0]
            cx.op('act', lambda e, k=k, sq=sq: e.activation(sq[:], A[:, k, :], AF.Square), reads=[A], writes=[sq])
            cx.op('pe', lambda e, k=k, sq=sq: e.matmul(pss[:, :TT], self.ones[:], sq[:], start=(k == 0), stop=(k == KC - 1)),
                  reads=[sq, self.ones], writes=[pss])
        rstd = self.rstd
        cx.op('act', lambda e: e.activation(rstd[:], pss[:, :TT], AF.Sqrt, bias=self.eps_rms[:], scale=1.0 / D),
              reads=[pss, self.eps_rms], writes=[rstd])
        cx.op('dve', lambda e: e.reciprocal(rstd[:], rstd[:]), reads=[rstd], writes=[rstd])
        for k in range(KC):
            cx.op('dve', lambda e, k=k: e.scalar_tensor_tensor(B[:, k, :], A[:, k, :], gvec[:, k:k + 1], rstd[:],
                                                                ALU.mult, ALU.mult),
                  reads=[A, gvec, rstd], writes=[B])

    def out_proj_residual(self, wout_groups, x_src, x_dst, t0, is_output=False):
        cx, TT = self.cx, self.TT
        A = self.Y
        xs = x_src.ap.rearrange("(k p) t -> p k t", p=128)
        xd = x_dst.ap.rearrange("(k p) t -> p k t", p=128)
        for g in range(8):
            w = self.load_w(wout_groups[g])
            for j in range(2):
                dk = g * 2 + j
                ps = self.next_p()
                for k in range(KC):
                    cx.op('pe', lambda e, k=k, j=j, w=w, ps=ps: e.matmul(ps[:, :TT], w[:, k, j * 128:(j + 1) * 128],
                                                                         A[:, k, :], start=(k == 0), stop=(k == KC - 1)),
                          reads=[w, A], writes=[ps])
                xc = self.xc[self.xci % 2]
                xo = self.xo[self.xci % 2]
                self.xci += 1
                cx.dma('sp', xc[:], xs[:, dk, t0:t0 + TT], reads=[x_src], writes=[xc], track=xc)
                cx.op('dve', lambda e, xc=xc, xo=xo, ps=ps: e.tensor_tensor(xo[:], ps[:, :TT], xc[:], ALU.add),
                      reads=[ps, xc], writes=[xo])
                ev = cx.dma('sp', xd[:, dk, t0:t0 + TT], xo[:], reads=[xo], writes=[x_dst], track=xo)
                if is_output:
                    cx.out_events.append(ev)


def emit_final_norm(cm, x_src, out_dst, gvec):
    cx, TT = cm.cx, cm.TT
    od = out_dst.ap.rearrange("(k p) t -> p k t", p=128)
    for t0 in range(0, NTOK, TT):
        cm.load_x_and_norm(x_src, t0, gvec)
        A = cm.A
        for k in range(KC):
            cx.op('dve', lambda e, k=k: e.scalar_tensor_tensor(A[:, k, :], A[:, k, :], gvec[:, k:k + 1], cm.rstd[:],
                                                                ALU.mult, ALU.mult),
                  reads=[A, gvec, cm.rstd], writes=[A])
        ev = cx.dma('sp', od[:, :, t0:t0 + TT], A[:], reads=[A], writes=[out_dst], track=A)
        cx.out_events.append(ev)


def emit_odd_layer(cm, P, x_src, x_dst):
    cx, TT = cm.cx, cm.TT
    mark = cx.mark()
    nb = TT // 128
    g = cx.sb([128, KC], F32, None)
    cx.dma('sp', g[:], P['norm'], writes=[g], track=g)
    lnw = cx.sb([128, D], F32)
    lnb = cx.sb([128, D], F32)
    cx.dma('sp', lnw[:], P['ln_w'].partition_broadcast(128), writes=[lnw], track=lnw)
    cx.dma('sp', lnb[:], P['ln_b'].partition_broadcast(128), writes=[lnb], track=lnb)
    bsb2 = [cx.sb([128, 128], F32) for _ in range(2)]
    wm = cx.sb([128, 16, 128], F32R)
    wstage = cm.A[:, 0:D // TT, :].rearrange("p k t -> p (k t)").rearrange("p (g t) -> p g t", t=128)
    cx.dma('sp', wstage, P['wsT'], writes=[cm.A], track=cm.A)
    cx.op('pool', lambda e: e.tensor_tensor(wm[:], wstage, cm.tri[:].unsqueeze(1).to_broadcast([128, 16, 128]), ALU.mult),
          reads=[cm.A, cm.tri], writes=[wm])
    eps_ln = cx.sb([128, 1], F32)
    cx.op('dve', lambda e: e.memset(eps_ln[:], LN_EPS), writes=[eps_ln])
    vv = cx.sb([128, nb, D], F32R)
    st = cx.sb([128, 8], F32)
    sg = [cx.sb([128, TT], F32) for _ in range(1)]
    t1 = [cx.sb([128, TT], F32) for _ in range(2)]

    for t0 in range(0, NTOK, TT):
        cm.load_x_and_norm(x_src, t0, g)
        B = cm.B
        for gi in range(8):
            w = cm.load_w(P['w_in'][16 + gi])
            for b in range(nb):
                ps = cm.next_p()
                for k in range(KC):
                    cx.op('pe', lambda e, k=k, b=b, w=w, ps=ps: e.matmul(ps[:, :256], B[:, k, b * 128:(b + 1) * 128], w[:, k, :],
                                                                         start=(k == 0), stop=(k == KC - 1)),
                          reads=[B, w], writes=[ps])
                cx.op('act', lambda e, b=b, gi=gi, ps=ps: e.copy(vv[:, b, gi * 256:(gi + 1) * 256], ps[:, :256]),
                      reads=[ps], writes=[vv])
        for b in range(nb):
            cx.op('dve', lambda e, b=b: e.reduce_sum(st[:, 0:1], f32(vv[:, b, :]), axis=AX.X), reads=[vv], writes=[st])
            cx.op('dve', lambda e: e.tensor_scalar(st[:, 1:2], st[:, 0:1], -1.0 / D, None, ALU.mult), reads=[st], writes=[st])
            cx.op('dve', lambda e, b=b: e.tensor_scalar(vv[:, b, :], f32(vv[:, b, :]), st[:, 1:2], None, ALU.add),
                  reads=[vv, st], writes=[vv])
            cx.op('dve', lambda e: e.memset(st[:, 2:3], 0.0), writes=[st])
            cx.op('act', lambda e, b=b: e.activation(cm.A[:, 0:D // TT, :].rearrange("p k t -> p (k t)"), f32(vv[:, b, :]), AF.Square,
                                                     accum_out=st[:, 2:3]),
                  reads=[vv], writes=[cm.A, st])
            cx.op('act', lambda e: e.activation(st[:, 3:4], st[:, 2:3], AF.Sqrt, bias=eps_ln[:], scale=1.0 / D),
                  reads=[st, eps_ln], writes=[st])
            cx.op('dve', lambda e: e.reciprocal(st[:, 4:5], st[:, 3:4]), reads=[st], writes=[st])
            cx.op('dve', lambda e, b=b: e.scalar_tensor_tensor(vv[:, b, :], f32(vv[:, b, :]), st[:, 4:5], lnw[:], ALU.mult, ALU.mult),
                  reads=[vv, st, lnw], writes=[vv])
            cx.op('pool', lambda e, b=b: e.tensor_tensor(vv[:, b, :], f32(vv[:, b, :]), lnb[:], ALU.add),
                  reads=[vv, lnb], writes=[vv])
        A = cm.Y
        for gi in range(16):
            w = cm.load_w(P['w_in'][gi])
            pu = cm.next_p()
            pg = cm.next_p()
            for k in range(KC):
                cx.op('pe', lambda e, k=k, w=w, pu=pu: e.matmul(pu[:, :TT], w[:, k, 0:128], B[:, k, :], start=(k == 0), stop=(k == KC - 1)),
                      reads=[B, w], writes=[pu])
            for k in range(KC):
                cx.op('pe', lambda e, k=k, w=w, pg=pg: e.matmul(pg[:, :TT], w[:, k, 128:256], B[:, k, :], start=(k == 0), stop=(k == KC - 1)),
                      reads=[B, w], writes=[pg])
            pm = cm.next_p()
            for b in range(nb):
                cx.op('pe', lambda e, b=b, gi=gi, pm=pm: e.matmul(pm[:, b * 128:(b + 1) * 128], vv[:, b, gi * 128:(gi + 1) * 128], wm[:, gi, :],
                                                                  start=True, stop=True),
                      reads=[vv, wm], writes=[pm])
            sgt = sg[0]
            t1t = t1[gi % 2]
            bsb = bsb2[gi % 2]
            cx.dma('sp', bsb[:], P['bs'][gi * 128:(gi + 1) * 128].partition_broadcast(128), writes=[bsb], track=bsb)
            cx.op('act', lambda e, sgt=sgt, pg=pg: e.activation(sgt[:], pg[:, :TT], AF.Silu), reads=[pg], writes=[sgt])
            cx.op('dve', lambda e, sgt=sgt, pu=pu: e.tensor_tensor(sgt[:], pu[:, :TT], sgt[:], ALU.mult), reads=[pu, sgt], writes=[sgt])
            cx.op('dve', lambda e, t1t=t1t, pm=pm, bsb=bsb: e.tensor_tensor(
                t1t[:].rearrange("p (b t) -> p b t", t=128), pm[:, :TT].rearrange("p (b t) -> p b t", t=128),
                bsb[:].unsqueeze(1).to_broadcast([128, nb, 128]), ALU.add), reads=[pm, bsb], writes=[t1t])
            cx.op('pool', lambda e, t1t=t1t, sgt=sgt, gi=gi: e.tensor_tensor(A[:, gi, :], t1t[:], sgt[:], ALU.mult),
                  reads=[t1t, sgt], writes=[A])
        cm.out_proj_residual(P['w_out'], x_src, x_dst, t0)
    cx.release(mark)


def tile_w(Wcols):
    n = Wcols.shape[1] // 256
    a = Wcols.reshape(KC, 128, n, 256).transpose(2, 1, 0, 3)
    return np.ascontiguousarray(a, dtype=np.float32)


def vec_pk(v):
    return np.ascontiguousarray(v.reshape(-1, 128).T, dtype=np.float32)


def prep_odd(o_norm, o_w_in, ln_w, ln_b, ws, bs, w_out):
    u = o_w_in[:, 0:2048].reshape(2048, 16, 128)
    gt = o_w_in[:, 4096:6144].reshape(2048, 16, 128)
    ug = np.concatenate([u, gt], axis=2).reshape(2048, 16 * 256)
    wcols = np.concatenate([ug, o_w_in[:, 2048:4096]], axis=1)
    return {
        'norm': vec_pk(o_norm),
        'w_in': tile_w(wcols),
        'ln_w': np.ascontiguousarray(ln_w, dtype=np.float32),
        'ln_b': np.ascontiguousarray(ln_b, dtype=np.float32),
        'wsT': np.ascontiguousarray(ws.transpose(2, 0, 1), dtype=np.float32),
        'bs': np.ascontiguousarray(bs.reshape(-1), dtype=np.float32),
        'w_out': tile_w(w_out),
    }


def make_consts():
    s = np.arange(128)
    return {
        'ones': np.ones((128, 128), np.float32),
        'tri': (s[None, :] >= s[:, None]).astype(np.float32),
        'ident': np.eye(128, dtype=np.float32),
    }


def declare(nc, name, arr, kind="ExternalInput"):
    dt = I32 if arr.dtype == np.int32 else F32
    return nc.dram_tensor(name, list(arr.shape), dt, kind=kind).ap()


TE = 256
LCH = 64
NCH = TE // LCH
GN_EPS = 64 * 1e-5
TWO_PI = 2.0 * math.pi


class Ring:
    def __init__(self, tiles):
        self.t = tiles
        self.i = 0

    def next(self):
        t = self.t[self.i % len(self.t)]
        self.i += 1
        return t


def emit_even_phase(cx, P, consts, x_src, y_dst, vf_dst, vf_src, ntok, layer2, dbg=9):
    nc = cx.nc
    mark = cx.mark()
    sb, ps, op, dma = cx.sb, cx.ps, cx.op, cx.dma
    ones = sb([128, 128], F32R, "e_ones")
    dma('pool', ones[:], consts['ones'], writes=[ones], track=ones)
    ident = sb([128, 128], F32, "e_ident")
    dma('sp', ident[:], consts['ident'], writes=[ident], track=ident)
    bones = sb([128, 128], F32, "e_bones")
    dma('sp', bones[:], consts['bones'], writes=[bones], track=bones)
    msk = sb([128, 3, 128], F32, "e_msk")
    dma('sp', msk[:], consts['emsk'], writes=[msk], track=msk)
    mb = sb([128, 2, 256], F32, "e_mb")
    dma('sp', mb[:], consts['amask'], writes=[mb], track=mb)
    rm = sb([128, 128], F32, "e_rm")
    dma('sp', rm[:], consts['ropem'], writes=[rm], track=rm)
    rst = sb([128, TE], F32, "e_rst")
    dma('sp', rst[:], consts['rst'], writes=[rst], track=rst)
    cv = sb([128, 8], F32, "e_cv")
    dma('sp', cv[:], consts['cvec'], writes=[cv], track=cv)
    g = sb([128, KC], F32, "e_g")
    dma('sp', g[:], P['norm'], writes=[g], track=g)
    vec = sb([128, 40], F32, "e_vec")
    dma('sp', vec[:], P['vec'], writes=[vec], track=vec)
    op('dve', lambda e: e.tensor_scalar(vec[:, 14:16], vec[:, 12:14], -1.0, 1.0, ALU.mult, ALU.add), reads=[vec], writes=[vec])
    w2a2 = sb([128, 256], F32, "e_w2a2")
    dma('sp', w2a2[:], P['w2a2'], writes=[w2a2], track=w2a2)
    sinkb = sb([128, 4], F32, "e_sink")
    dma('sp', sinkb[:], P['sinks'].partition_broadcast(128), writes=[sinkb], track=sinkb)
    W = sb([128, 7, KC, 256], F32R, "e_W")
    for gi in range(7):
        dma('pool', W[:, gi], P['w_in'][gi], writes=[W], track=W)
    V_MUR, V_MUK, V_MUV, V_W0, V_A0, V_KK, V_KA, V_OMKA, V_RK, V_LNW, V_LNB, V_V0, V_MUX = 0, 2, 4, 6, 8, 10, 12, 14, 16, 18, 20, 22, 24

    xr = Ring([sb([128, TE], F32, f"e_xr{i}") for i in range(2)])
    B = sb([128, KC, TE + 2], F32R, "e_B")
    sq = sb([128, TE], F32R, "e_sq")
    Z = {n: sb([128, 2, TE + 1], F32, "e_z" + n) for n in ('r', 'k', 'v')}
    Zx = sb([128, TE + 1], F32, "e_zx")
    for zt in list(Z.values()) + [Zx]:
        op('pool', lambda e, zt=zt: e.memset(zt[:], 0.0), writes=[zt])
    KV = sb([128, 128 + TE], F32, "e_kv")
    op('pool', lambda e: e.memset(KV[:], 0.0), writes=[KV])
    Yt = sb([128, 4, TE], F32, "e_y")
    S2 = [[sb([128, 128], F32, f"e_S{h}_{i}") for i in range(2)] for h in range(2)]
    for h in range(2):
        op('pool', lambda e, h=h: e.memset(S2[h][0][:], 0.0), writes=[S2[h][0]])
    si = [0, 0]
    bm2 = sb([128, 2, 64], F32, "e_bm2")
    dma('sp', bm2[:], consts['bm2'], writes=[bm2], track=bm2)
    tmp = Ring([sb([128, TE], F32, f"e_t{i}") for i in range(7)])
    ACM = sb([128, 2, TE], F32, "e_acm")
    RW4 = sb([128, 4, TE], F32, "e_rw4")

    def view(ap):
        v = T(ap)
        v.tr = RW4
        return v
    BCM = view(RW4.t[:, 0:2, :])
    BH = view(RW4.t[:, 2, :])
    KH = view(RW4.t[:, 3, :])
    Q4 = view(RW4.t[0:64, :, :])
    VS = sb([128, 2, TE], F32, "e_vs")
    ECL = sb([128, TE], F32, "e_ecl")
    CL = sb([128, TE], F32, "e_cl")
    LW = sb([128, TE], F32, "e_lw")
    AA = sb([128, TE], F32, "e_aa")
    KKN = sb([128, TE], F32, "e_kkn")
    KF = sb([128, TE], F32, "e_kf")
    BV = sb([128, TE], F32, "e_bv")
    RSs = [sb([128, TE], F32, f"e_rs{i}") for i in range(2)]
    KSs = [sb([128, TE], F32, f"e_ks{i}") for i in range(2)]
    XS = sb([128, TE], F32, "e_xs")
    TXW = XS
    cosq = sb([64, TE], F32, "e_cos")
    sinq = sb([64, TE], F32, "e_sin")
    small = Ring([sb([128, 128], F32, f"e_s{i}") for i in range(6)])
    big = Ring([sb([128, 256], F32, f"e_b{i}") for i in range(5)])
    keep = Ring([sb([128, 256], F32, f"e_k{i}") for i in range(4)])
    tm4 = Ring([sb([128, 512], F32, f"e_tm{i}") for i in range(2)])
    st = sb([128, 8], F32, "e_st")
    Vtm = Ring([sb([128, 128], F32, f"e_vtm{i}") for i in range(3)])
    pbig = Ring(cx.ps_views(4, 256))
    psm = Ring(cx.ps_views(2, 128))
    pfull = Ring(cx.ps_views(2, 512))

    def mm(out_t, out_ap, lhsT_t, lhsT_ap, rhs_t, rhs_ap, start=True, stop=True):
        op('pe', lambda e: e.matmul(out_ap, lhsT_ap, rhs_ap, start=start, stop=stop), reads=[lhsT_t, rhs_t], writes=[out_t])

    scale = 1.0 / 8.0
    vtm_prev = None
    op('dve', lambda e: e.tensor_scalar(B[:, :, 0:2], g[:].unsqueeze(2).to_broadcast([128, KC, 2]), 0.0, None, ALU.mult), reads=[g], writes=[B])

    if layer2:
        v2t = sb([32, 256], F32, "e_v2")
        dma('sp', v2t[:], P['v2'], writes=[v2t], track=v2t)
        Weff = sb([128, KC, 64], F32, "e_weff")
        VFt = sb([128, 2, TE], F32, "e_vft")
        v1t = tmp.next()
        v1v = v1t[:].rearrange("p (c m) -> p c m", m=32)
        dma('sp', v1v, P['v1'], writes=[v1t], track=v1t)
        muv = tmp.next()
        dma('sp', muv[:, 0:8], P['muv'], writes=[muv], track=muv)
        M12 = ACM[:].rearrange("p a t -> p (a t)").rearrange("p (c m) -> p c m", m=64)
        for c in range(8):
            op('dve', lambda e, c=c: e.tensor_scalar(M12[:, c, 32:64], v1v[:, c, :], muv[:, c:c + 1], None, ALU.mult), reads=[v1t, muv], writes=[ACM])
            op('dve', lambda e, c=c: e.tensor_tensor(M12[:, c, 0:32], v1v[:, c, :], M12[:, c, 32:64], ALU.subtract), reads=[v1t, ACM], writes=[ACM])
        for dk2 in range(8):
            pw0, pw1 = psm.next(), psm.next()
            for c in range(8):
                piece = xr.next()
                dma('sp', piece[:], P['wvT'][c * 128:(c + 1) * 128, dk2 * 256:(dk2 + 1) * 256], writes=[piece], track=piece)
                mm(pw0, pw0[:, 0:64], piece, piece[:, 0:128], ACM, M12[:, c, :], start=(c == 0), stop=(c == 7))
                mm(pw1, pw1[:, 0:64], piece, piece[:, 128:256], ACM, M12[:, c, :], start=(c == 0), stop=(c == 7))
            op('act', lambda e, dk2=dk2, pw0=pw0: e.copy(Weff[:, 2 * dk2, :], pw0[:, 0:64]), reads=[pw0], writes=[Weff])
            op('dve', lambda e, dk2=dk2, pw1=pw1: e.tensor_copy(Weff[:, 2 * dk2 + 1, :], pw1[:, 0:64]), reads=[pw1], writes=[Weff])

    for t0 in range(0, ntok, TE):
        first_tile = (t0 == 0)
        src = x_src.ap.rearrange("(k p) t -> p k t", p=128)[:, :, t0:t0 + TE]
        rstd = tmp.next()
        pss = pbig.next()
        for k in range(KC):
            xa = xr.next()
            dma('sp', xa[:], src[:, k, :], reads=[x_src], writes=[xa], track=xa)
            op('act', lambda e, xa=xa: e.activation(sq[:], xa[:], AF.Square), reads=[xa], writes=[sq])
            mm(pss, pss[:], ones, ones[:], sq, sq[:], start=(k == 0), stop=(k == KC - 1))
        op('act', lambda e: e.activation(rstd[:], pss[:], AF.Sqrt, bias=cv[:, 0:1], scale=1.0 / D), reads=[pss, cv], writes=[rstd])
        op('dve', lambda e: e.reciprocal(rstd[:], rstd[:]), reads=[rstd], writes=[rstd])
        for k in range(KC):
            xa = xr.next()
            dma('sp', xa[:], src[:, k, :], reads=[x_src], writes=[xa], track=xa)
            op('dve', lambda e, k=k, xa=xa: e.scalar_tensor_tensor(B[:, k, 2:TE + 2], xa[:], g[:, k:k + 1], rstd[:], ALU.mult, ALU.mult),
               reads=[xa, g, rstd], writes=[B])

        if dbg <= 0:
            op('pool', lambda e: e.memset(Yt[:], 0.0), writes=[Yt])
            op('dve', lambda e: e.tensor_copy(Yt[:, 0, :], B[:, 0, 2:TE + 2]), reads=[B], writes=[Yt])
            ev = dma('sp', y_dst.ap.rearrange("(j p) t -> p j t", p=128)[:, :, t0:t0 + TE], Yt[:], reads=[Yt], writes=[y_dst], track=Yt)
            cx.out_events.append(ev)
            continue

        if layer2:
            lvp = pbig.next()
            for k in range(KC):
                mm(lvp, lvp[0:32, :], Weff, Weff[:, k, 0:32], B, f32(B[:, k, 2:TE + 2]), start=(k == 0), stop=False)
                mm(lvp, lvp[0:32, :], Weff, Weff[:, k, 32:64], B, f32(B[:, k, 1:TE + 1]), start=False, stop=(k == KC - 1))
            LV = tmp.next()
            op('act', lambda e: e.copy(LV[0:32, :], lvp[0:32, :]), reads=[lvp], writes=[LV])
            for j, sgv in ((0, KKN), (1, KF)):
                psg = pbig.next()
                mm(psg, psg[:], v2t, v2t[0:32, j * 128:(j + 1) * 128], LV, LV[0:32, :])
                op('act', lambda e, j=j, sgv=sgv, psg=psg: e.activation(sgv[:], psg[:], AF.Sigmoid, bias=vec[:, V_V0 + j:V_V0 + j + 1]), reads=[psg, vec], writes=[sgv])
            dma('sp', VFt[:], vf_src.ap.rearrange("(j p) t -> p j t", p=128)[:, :, t0:t0 + TE], reads=[vf_src], writes=[VFt], track=VFt)

        def proj(gi, c0, m, dst_t, dst_ap, eng='act'):
            pp = pbig.next()
            for k in range(KC):
                mm(pp, pp[0:m, :], W, W[:, gi, k, c0:c0 + m], B, B[:, k, 2:TE + 2], start=(k == 0), stop=(k == KC - 1))
            if eng == 'act':
                op('act', lambda e: e.copy(dst_ap, pp[0:m, :]), reads=[pp], writes=[dst_t])
            else:
                op('dve', lambda e: e.tensor_copy(dst_ap, pp[0:m, :]), reads=[pp], writes=[dst_t])

        proj(0, 0, 128, Zx, Zx[:, 1:TE + 1])
        if dbg <= 0.2:
            op('pool', lambda e: e.memset(Yt[:], 0.0), writes=[Yt])
            ev = dma('sp', y_dst.ap.rearrange("(j p) t -> p j t", p=128)[:, :, t0:t0 + TE], Yt[:], reads=[Yt], writes=[y_dst], track=Yt)
            cx.out_events.append(ev)
            continue
        proj(0, 128, 128, KV, KV[:, 128:128 + TE], 'dve')
        if dbg <= 0.3:
            op('pool', lambda e: e.memset(Yt[:], 0.0), writes=[Yt])
            ev = dma('sp', y_dst.ap.rearrange("(j p) t -> p j t", p=128)[:, :, t0:t0 + TE], Yt[:], reads=[Yt], writes=[y_dst], track=Yt)
            cx.out_events.append(ev)
            continue
        for j in range(2):
            proj(1, j * 128, 128, Z['r'], Z['r'][:, j, 1:TE + 1], 'act' if j == 0 else 'dve')
            proj(2, j * 128, 128, Z['k'], Z['k'][:, j, 1:TE + 1], 'act' if j == 0 else 'dve')
            proj(3, j * 128, 128, Z['v'], Z['v'][:, j, 1:TE + 1], 'act' if j == 0 else 'dve')

        if dbg <= 0.4:
            op('pool', lambda e: e.memset(Yt[:], 0.0), writes=[Yt])
            ev = dma('sp', y_dst.ap.rearrange("(j p) t -> p j t", p=128)[:, :, t0:t0 + TE], Yt[:], reads=[Yt], writes=[y_dst], track=Yt)
            cx.out_events.append(ev)
            continue
        def shift(zt, view, mucol):
            d = tmp.next()
            op('dve', lambda e: e.tensor_tensor(d[:], view(0), view(1), ALU.subtract), reads=[zt], writes=[d])
            return d

        for n, mu0 in (('r', V_MUR), ('k', V_MUK), ('v', V_MUV)):
            for j in range(2):
                zt = Z[n]
                d = tmp.next()
                op('dve', lambda e, zt=zt, j=j, d=d: e.tensor_tensor(d[:], zt[:, j, 0:TE], zt[:, j, 1:TE + 1], ALU.subtract), reads=[zt], writes=[d])
                dst_t, dst_ap = {'r': (RSs[j], RSs[j][:]), 'k': (KSs[j], KSs[j][:]), 'v': (VS, VS[:, j, :])}[n]
                op('dve', lambda e, zt=zt, j=j, d=d, mc=mu0 + j, dst_ap=dst_ap: e.scalar_tensor_tensor(dst_ap, d[:], vec[:, mc:mc + 1], zt[:, j, 1:TE + 1], ALU.mult, ALU.add),
                   reads=[d, vec, zt], writes=[dst_t])
        dx = tmp.next()
        op('dve', lambda e: e.tensor_tensor(dx[:], Zx[:, 0:TE], Zx[:, 1:TE + 1], ALU.subtract), reads=[Zx], writes=[dx])
        op('dve', lambda e: e.scalar_tensor_tensor(XS[:], dx[:], vec[:, V_MUX:V_MUX + 1], Zx[:, 1:TE + 1], ALU.mult, ALU.add),
           reads=[dx, vec, Zx], writes=[XS])
        op('pool', lambda e: e.tensor_copy(Zx[:, 0:1], Zx[:, TE:TE + 1]), reads=[Zx], writes=[Zx])
        op('act', lambda e: e.activation(TXW[0:64, :], XS[0:64, :], AF.Tanh), reads=[XS], writes=[TXW])
        if dbg <= 0.6:
            op('pool', lambda e: e.memset(Yt[:], 0.0), writes=[Yt])
            ev = dma('sp', y_dst.ap.rearrange("(j p) t -> p j t", p=128)[:, :, t0:t0 + TE], Yt[:], reads=[Yt], writes=[y_dst], track=Yt)
            cx.out_events.append(ev)
            continue
        if layer2:
            for j, sgv in ((0, KKN), (1, KF)):
                d2 = tmp.next()
                op('dve', lambda e, j=j, d2=d2: e.tensor_tensor(d2[:], VFt[:, j, :], VS[:, j, :], ALU.subtract), reads=[VFt, VS], writes=[d2])
                op('dve', lambda e, sgv=sgv, d2=d2: e.tensor_tensor(d2[:], d2[:], sgv[:], ALU.mult), reads=[d2, sgv], writes=[d2])
                op('dve', lambda e, j=j, d2=d2: e.tensor_tensor(VS[:, j, :], VS[:, j, :], d2[:], ALU.add), reads=[VS, d2], writes=[VS])
        if not layer2:
            dma('sp', vf_dst.ap.rearrange("(j p) t -> p j t", p=128)[:, :, t0:t0 + TE], VS[:], reads=[VS], writes=[vf_dst], track=VS)

        if dbg <= 1:
            op('pool', lambda e: e.memset(Yt[:], 0.0), writes=[Yt])
            ev = dma('sp', y_dst.ap.rearrange("(j p) t -> p j t", p=128)[:, :, t0:t0 + TE], Yt[:], reads=[Yt], writes=[y_dst], track=Yt)
            cx.out_events.append(ev)
            continue
        posi_t = tmp.next()
        posi = posi_t[:].bitcast(I32)
        posf = tmp.next()
        dma('sp', posi, P['pos'][t0:t0 + TE].partition_broadcast(128), writes=[posi_t], track=posi_t)
        op('dve', lambda e: e.tensor_copy(posf[:], posi), reads=[posi_t], writes=[posf])

        def rope_tables(fcol, cos_t, sin_t, npart):
            ang = tmp.next()
            op('dve', lambda e: e.tensor_scalar(ang[0:npart, :], posf[0:npart, :], cv[0:npart, fcol:fcol + 1], None, ALU.mult), reads=[posf, cv], writes=[ang])
            n1 = tmp.next()
            op('dve', lambda e: e.tensor_scalar(n1[0:npart, :], ang[0:npart, :], 1.0 / TWO_PI, 12582912.0, ALU.mult, ALU.add), reads=[ang], writes=[n1])
            op('dve', lambda e: e.tensor_scalar(n1[0:npart, :], n1[0:npart, :], -12582912.0, None, ALU.add), reads=[n1], writes=[n1])
            op('dve', lambda e: e.scalar_tensor_tensor(ang[0:npart, :], n1[0:npart, :], -TWO_PI, ang[0:npart, :], ALU.mult, ALU.add), reads=[n1, ang], writes=[ang])
            s2 = tmp.next()
            c2 = n1
            op('act', lambda e: e.activation(s2[0:npart, :], ang[0:npart, :], AF.Sin, scale=0.5), reads=[ang], writes=[s2])
            op('act', lambda e: e.activation(c2[0:npart, :], ang[0:npart, :], AF.Sin, bias=cv[0:npart, 3:4], scale=0.5), reads=[ang, cv], writes=[c2])
            op('dve', lambda e: e.scalar_tensor_tensor(sin_t[0:npart, :], s2[0:npart, :], 2.0, c2[0:npart, :], ALU.mult, ALU.mult), reads=[s2, c2], writes=[sin_t])
            op('dve', lambda e: e.tensor_tensor(s2[0:npart, :], s2[0:npart, :], s2[0:npart, :], ALU.mult), reads=[s2], writes=[s2])
            op('dve', lambda e: e.tensor_scalar(cos_t[0:npart, :], s2[0:npart, :], -2.0, 1.0, ALU.mult, ALU.add), reads=[s2], writes=[cos_t])

        ck, sk = tmp.next(), tmp.next()
        rope_tables(5, ck, sk, 128)
        pr = pbig.next()
        mm(pr, pr[:], rm, rm[:], KV, KV[:, 128:128 + TE])
        t_a = tmp.next()
        op('dve', lambda e: e.tensor_tensor(t_a[:], pr[:], sk[:], ALU.mult), reads=[pr, sk], writes=[t_a])
        op('dve', lambda e: e.tensor_tensor(KV[:, 128:128 + TE], KV[:, 128:128 + TE], ck[:], ALU.mult), reads=[KV, ck], writes=[KV])
        op('dve', lambda e: e.tensor_tensor(KV[:, 128:128 + TE], KV[:, 128:128 + TE], t_a[:], ALU.add), reads=[KV, t_a], writes=[KV])
        rope_tables(4, cosq, sinq, 64)
        if dbg <= 2:
            op('pool', lambda e: e.memset(Yt[:], 0.0), writes=[Yt])
            ev = dma('sp', y_dst.ap.rearrange("(j p) t -> p j t", p=128)[:, :, t0:t0 + TE], Yt[:], reads=[Yt], writes=[y_dst], track=Yt)
            cx.out_events.append(ev)
            continue
        for j in range(2):
            RS = RSs[j]
            KS = KSs[j]
            pw = pbig.next()
            mm(pw, pw[:], w2a2, w2a2[0:64, j * 128:(j + 1) * 128], TXW, TXW[0:64, :])
            op('act', lambda e: e.activation(LW[:], pw[:], AF.Sigmoid, bias=vec[:, V_W0 + j:V_W0 + j + 1]), reads=[pw, vec], writes=[LW])
            op('dve', lambda e: e.tensor_scalar(LW[:], LW[:], -math.exp(-0.5), None, ALU.mult), reads=[LW], writes=[LW])
            pa = pbig.next()
            mm(pa, pa[:], w2a2, w2a2[64:128, j * 128:(j + 1) * 128], XS, XS[64:128, :])
            op('act', lambda e: e.activation(AA[:], pa[:], AF.Sigmoid, bias=vec[:, V_A0 + j:V_A0 + j + 1]), reads=[pa, vec], writes=[AA])
            op('dve', lambda e: e.tensor_tensor_scan(CL[:], rst[:], LW[:], 0.0, ALU.mult, ALU.add), reads=[rst, LW], writes=[CL])
            kk = tmp.next()
            op('dve', lambda e: e.tensor_scalar(kk[:], KS[:], vec[:, V_KK + j:V_KK + j + 1], None, ALU.mult), reads=[KS, vec], writes=[kk])
            kk2 = tmp.next()
            op('pool', lambda e: e.tensor_tensor(kk2[:], kk[:], kk[:], ALU.mult), reads=[kk], writes=[kk2])
            pk = pbig.next()
            mm(pk, pk[:], bones, bones[:], kk2, kk2[:])
            op('act', lambda e: e.activation(kk2[:], pk[:], AF.Sqrt, bias=cv[:, 1:2]), reads=[pk, cv], writes=[kk2])
            op('dve', lambda e: e.reciprocal(kk2[:], kk2[:]), reads=[kk2], writes=[kk2])
            op('dve', lambda e: e.tensor_tensor(KKN[:], kk[:], kk2[:], ALU.mult), reads=[kk, kk2], writes=[KKN])
            tk = kk
            op('dve', lambda e: e.tensor_scalar(tk[:], AA[:], vec[:, V_KA + j:V_KA + j + 1], vec[:, V_OMKA + j:V_OMKA + j + 1], ALU.mult, ALU.add),
               reads=[AA, vec], writes=[tk])
            op('dve', lambda e: e.tensor_tensor(KF[:], KS[:], tk[:], ALU.mult), reads=[KS, tk], writes=[KF])
            op('act', lambda e: e.activation(ECL[:], CL[:], AF.Exp), reads=[CL], writes=[ECL])
            encl = tmp.next()
            op('act', lambda e: e.activation(encl[:], CL[:], AF.Exp, scale=-1.0), reads=[CL], writes=[encl])
            eclm = kk2
            op('dve', lambda e: e.tensor_tensor(eclm[:], CL[:], LW[:], ALU.subtract), reads=[CL, LW], writes=[eclm])
            op('act', lambda e: e.activation(eclm[:], eclm[:], AF.Exp), reads=[eclm], writes=[eclm])
            ecL = tmp.next()
            clv = CL[:].rearrange("p (c l) -> p c l", l=LCH)
            op('dve', lambda e: e.tensor_tensor(ecL[:].rearrange("p (c l) -> p c l", l=LCH), clv[:, :, LCH - 1:LCH].to_broadcast([128, NCH, LCH]), clv, ALU.subtract),
               reads=[CL], writes=[ecL])
            op('act', lambda e: e.activation(ecL[:], ecL[:], AF.Exp), reads=[ecL], writes=[ecL])
            op('dve', lambda e: e.scalar_tensor_tensor(ACM[:, 0, :], KKN[:], -1.0, eclm[:], ALU.mult, ALU.mult), reads=[KKN, eclm], writes=[ACM])
            op('pool', lambda e: e.tensor_tensor(ACM[:, 1, :], RS[:], ECL[:], ALU.mult), reads=[RS, ECL], writes=[ACM])
            bb = tk
            op('dve', lambda e: e.tensor_tensor(bb[:], KKN[:], AA[:], ALU.mult), reads=[KKN, AA], writes=[bb])
            op('dve', lambda e: e.tensor_tensor(BCM[:, 0, :], bb[:], encl[:], ALU.mult), reads=[bb, encl], writes=[BCM])
            op('pool', lambda e: e.tensor_tensor(BCM[:, 1, :], KF[:], encl[:], ALU.mult), reads=[KF, encl], writes=[BCM])
            op('dve', lambda e: e.tensor_tensor(BH[:], bb[:], ecL[:], ALU.mult), reads=[bb, ecL], writes=[BH])
            op('pool', lambda e: e.tensor_tensor(KH[:], KF[:], ecL[:], ALU.mult), reads=[KF, ecL], writes=[KH])
            rk = encl
            op('dve', lambda e: e.scalar_tensor_tensor(rk[:], RS[:], vec[:, V_RK + j:V_RK + j + 1], KF[:], ALU.mult, ALU.mult), reads=[RS, vec, KF], writes=[rk])
            pbn = pbig.next()
            mm(pbn, pbn[:], bones, bones[:], rk, rk[:])
            op('dve', lambda e: e.tensor_tensor(BV[:], pbn[:], VS[:, j, :], ALU.mult), reads=[pbn, VS], writes=[BV])

            YV = tmp.next()
            if dbg < 4:
                op('dve', lambda e: e.tensor_copy(YV[:], BV[:]), reads=[BV], writes=[YV])
            for c in range(NCH if dbg > 3 else 0):
                cs = slice(c * LCH, (c + 1) * LCH)
                bmv = bm2[:].unsqueeze(1).to_broadcast([128, 2, 2, 64])
                AXT = keep.next()
                op('dve', lambda e, AXT=AXT: e.tensor_tensor(AXT[:].rearrange("p (a h t) -> p a h t", a=2, h=2), ACM[:, :, cs].unsqueeze(2).to_broadcast([128, 2, 2, 64]), bmv, ALU.mult),
                   reads=[ACM, bm2], writes=[AXT])
                BX = keep.next()
                op('pool', lambda e, BX=BX: e.tensor_tensor(BX[:].rearrange("p (a h t) -> p a h t", a=2, h=2), BCM[:, :, cs].unsqueeze(2).to_broadcast([128, 2, 2, 64]), bmv, ALU.mult),
                   reads=[BCM, bm2], writes=[BX])
                HX3 = [small.next() for _ in range(3)]
                bm1 = bm2[:]
                for tt_, (src_t, src_ap) in zip(HX3, ((BH, BH[:, cs]), (VS, VS[:, j, cs]), (KH, KH[:, cs]))):
                    op('pool', lambda e, tt_=tt_, src_ap=src_ap: e.tensor_tensor(tt_[:].rearrange("p (h t) -> p h t", h=2), src_ap.unsqueeze(1).to_broadcast([128, 2, 64]), bm1, ALU.mult),
                       reads=[src_t, bm2], writes=[tt_])
                ptm = pfull.next()
                for q_, (src_t, src_ap) in enumerate(((AXT, AXT[:, 0:128]), (HX3[0], HX3[0][:]), (HX3[1], HX3[1][:]), (HX3[2], HX3[2][:]))):
                    mm(ptm, ptm[:, q_ * 128:(q_ + 1) * 128], src_t, src_ap, ident, ident[:])
                TM4 = tm4.next()
                op('act', lambda e, TM4=TM4, ptm=ptm: e.copy(TM4[:], ptm[:]), reads=[ptm], writes=[TM4])
                ATm, BHm, Vm, KHm = (TM4[:, q_ * 128:(q_ + 1) * 128] for q_ in range(4))
                p1 = pbig.next()
                mm(p1, p1[:], AXT, AXT[:, 0:128], BX, BX[:])
                NN = keep.next()
                op('dve', lambda e, NN=NN, p1=p1: e.tensor_tensor(NN[:].rearrange("p (a t) -> p a t", a=2), p1[:].rearrange("p (a t) -> p a t", a=2),
                                                                 msk[:, 0, :].unsqueeze(1).to_broadcast([128, 2, 128]), ALU.mult), reads=[p1, msk], writes=[NN])
                p2 = pbig.next()
                mm(p2, p2[:], BX, BX[:, 0:128], AXT, AXT[:])
                TR = keep.next()
                op('dve', lambda e, TR=TR, p2=p2: e.tensor_tensor(TR[:], p2[:], msk[:, 1:3, :].rearrange("p a t -> p (a t)"), ALU.mult), reads=[p2, msk], writes=[TR])
                p3 = psm.next()
                mm(p3, p3[:], BX, BX[:, 128:256], AXT, AXT[:, 128:256])
                QK = small.next()
                op('dve', lambda e, QK=QK, p3=p3: e.tensor_tensor(QK[:], p3[:], msk[:, 2, :], ALU.mult), reads=[p3, msk], writes=[QK])
                PT = big.next()
                op('act', lambda e, PT=PT, NN=NN: e.copy(PT[:, 0:128], NN[:, 0:128]), reads=[NN], writes=[PT])
                op('dve', lambda e, PT=PT, NN=NN: e.tensor_tensor(PT[:, 128:256], NN[:, 0:128], ident[:], ALU.add), reads=[NN, ident], writes=[PT])
                PkT_t, PkT = TR, TR[:, 0:128]
                nsteps = 7
                for kstep in range(nsteps - 1):
                    last = (kstep == nsteps - 2)
                    pA = pbig.next()
                    if kstep == 0:
                        mm(pA, pA[:, 0:128], PkT_t, PkT, PT, PT[:, 0:128])
                    elif not last:
                        mm(pA, pA[:], PkT_t, PkT, PT, PT[:])
                    else:
                        mm(pA, pA[:, 128:256], PkT_t, PkT, PT, PT[:, 128:256])
                    PTn = big.next()
                    if not last:
                        pB = psm.next()
                        mm(pB, pB[:], PT, PT[:, 0:128], PkT_t, PkT)
                        PkTn = small.next()
                        op('act', lambda e, PkTn=PkTn, pB=pB: e.copy(PkTn[:], pB[:]), reads=[pB], writes=[PkTn])
                        op('act', lambda e, PTn=PTn, pA=pA: e.copy(PTn[:, 0:128], pA[:, 0:128]), reads=[pA], writes=[PTn])
                    if kstep == 0:
                        op('dve', lambda e, PTn=PTn, PT=PT: e.tensor_copy(PTn[:, 128:256], PT[:, 128:256]), reads=[PT], writes=[PTn])
                    else:
                        op('dve', lambda e, PTn=PTn, PT=PT, pA=pA: e.tensor_tensor(PTn[:, 128:256], pA[:, 128:256], PT[:, 128:256], ALU.add),
                           reads=[pA, PT], writes=[PTn])
                    PT = PTn
                    if not last:
                        PkT_t, PkT = PkTn, PkTn[:]
                Tt, Tm = PT, PT[:, 128:256]
                p4 = pbig.next()
                mm(p4, p4[:, 0:128], Tt, Tm, TR, TR[:, 128:256])
                mm(p4, p4[:, 128:256], Tt, Tm, TM4, BHm)
                GZ = big.next()
                op('act', lambda e, GZ=GZ, p4=p4: e.copy(GZ[:], p4[:]), reads=[p4], writes=[GZ])
                p5 = pbig.next()
                mm(p5, p5[:], NN, NN[:, 128:256], GZ, GZ[:])
                HX = big.next()
                op('dve', lambda e, HX=HX, p5=p5, QK=QK: e.tensor_tensor(HX[:, 0:128], p5[:, 0:128], QK[:], ALU.add), reads=[p5, QK], writes=[HX])
                op('dve', lambda e, HX=HX, p5=p5, TM4=TM4, KHm=KHm: e.tensor_tensor(HX[:, 128:256], p5[:, 128:256], KHm, ALU.add), reads=[p5, TM4], writes=[HX])
                DW = small.next()
                op('dve', lambda e, DW=DW, c=c: e.tensor_scalar(DW[:], ident[:], ECL[:, (c + 1) * LCH - 1:(c + 1) * LCH], None, ALU.mult), reads=[ident, ECL], writes=[DW])
                p6 = pbig.next()
                mm(p6, p6[:], TM4, ATm, GZ, GZ[:])
                REM = big.next()
                op('dve', lambda e, REM=REM, p6=p6, AXT=AXT: e.tensor_tensor(REM[:, 0:128], p6[:, 0:128], AXT[:, 128:256], ALU.add), reads=[p6, AXT], writes=[REM])
                op('dve', lambda e, REM=REM, p6=p6, DW=DW: e.tensor_tensor(REM[:, 128:256], p6[:, 128:256], DW[:], ALU.add), reads=[p6, DW], writes=[REM])
                Sc = S2[j][si[j] % 2]
                Sn = S2[j][(si[j] + 1) % 2]
                si[j] += 1
                pyc = psm.next()
                mm(pyc, pyc[:], TM4, Vm, HX, HX[:, 0:128], start=True, stop=False)
                mm(pyc, pyc[:], Sc, Sc[:], REM, REM[:, 0:128], start=False, stop=True)
                op('act', lambda e, pyc=pyc, cs=cs: e.copy(YV[:, cs], pyc[:, 0:64]), reads=[pyc], writes=[YV])
                op('dve', lambda e, pyc=pyc, cs=cs: e.tensor_tensor(YV[:, cs], YV[:, cs], pyc[:, 64:128], ALU.add), reads=[pyc, YV], writes=[YV])
                p7 = psm.next()
                mm(p7, p7[:], REM, REM[:, 128:256], Sc, Sc[:], start=True, stop=False)
                mm(p7, p7[:], HX, HX[:, 128:256], TM4, Vm, start=False, stop=True)
                op('act', lambda e, Sn=Sn, p7=p7: e.copy(Sn[:], p7[:]), reads=[p7], writes=[Sn])
            yv = YV
            pm_ = pbig.next()
            mm(pm_, pm_[:], bones, bones[:], yv, yv[:])
            yc = tmp.next()
            op('dve', lambda e: e.scalar_tensor_tensor(yc[:], pm_[:], -1.0 / 64, yv[:], ALU.mult, ALU.add), reads=[pm_, yv], writes=[yc])
            y2 = yv
            op('pool', lambda e: e.tensor_tensor(y2[:], yc[:], yc[:], ALU.mult), reads=[yc], writes=[y2])
            pv_ = pbig.next()
            mm(pv_, pv_[:], bones, bones[:], y2, y2[:])
            op('act', lambda e: e.activation(y2[:], pv_[:], AF.Sqrt, bias=cv[:, 2:3], scale=1.0 / 64), reads=[pv_, cv], writes=[y2])
            op('dve', lambda e: e.reciprocal(y2[:], y2[:]), reads=[y2], writes=[y2])
            op('dve', lambda e: e.tensor_tensor(yc[:], yc[:], y2[:], ALU.mult), reads=[yc, y2], writes=[yc])
            op('dve', lambda e: e.tensor_scalar(yc[:], yc[:], vec[:, V_LNW + j:V_LNW + j + 1], vec[:, V_LNB + j:V_LNB + j + 1], ALU.mult, ALU.add),
               reads=[yc, vec], writes=[yc])
            op('dve', lambda e: e.tensor_tensor(yc[:], yc[:], BV[:], ALU.add), reads=[yc, BV], writes=[yc])
            sgt = y2
            pgt = pbig.next()
            for k in range(KC):
                mm(pgt, pgt[:], W, W[:, 4, k, j * 128:(j + 1) * 128], B, B[:, k, 2:TE + 2], start=(k == 0), stop=(k == KC - 1))
            op('act', lambda e: e.activation(sgt[:], pgt[:], AF.Silu), reads=[pgt], writes=[sgt])
            op('dve', lambda e: e.tensor_tensor(Yt[:, j, :], yc[:], sgt[:], ALU.mult), reads=[yc, sgt], writes=[Yt])

        for n in ('r', 'k', 'v'):
            zt = Z[n]
            op('pool', lambda e, zt=zt: e.tensor_copy(zt[:, :, 0:1], zt[:, :, TE:TE + 1]), reads=[zt], writes=[zt])

        for jj in range(2):
            proj(5, jj * 128, 128, ACM, ACM[:, jj, :], 'act' if jj == 0 else 'dve')
        for h in range(4):
            hs_ = slice((h % 2) * 64, (h % 2) * 64 + 64)
            pq = pbig.next()
            mm(pq, pq[0:64, :], rm, rm[hs_, hs_], ACM, ACM[hs_, h // 2, :])
            pq2 = pbig.next()
            mm(pq2, pq2[0:64, :], ident, ident[hs_, hs_], ACM, ACM[hs_, h // 2, :])
            t_b = tmp.next()
            op('dve', lambda e, pq=pq, t_b=t_b: e.tensor_tensor(t_b[0:64, :], pq[0:64, :], sinq[:], ALU.mult), reads=[pq, sinq], writes=[t_b])
            op('dve', lambda e, h=h, pq2=pq2: e.tensor_tensor(Q4[:, h, :], pq2[0:64, :], cosq[:], ALU.mult), reads=[pq2, cosq], writes=[Q4])
            op('dve', lambda e, h=h, t_b=t_b: e.tensor_tensor(Q4[:, h, :], Q4[:, h, :], t_b[0:64, :], ALU.add), reads=[Q4, t_b], writes=[Q4])

        if dbg <= 4:
            op('pool', lambda e: e.memset(Yt[:, 2:4, :], 0.0), writes=[Yt])
        for bq in range(TE // 128 if dbg > 4 else 0):
            blk_first = first_tile and bq == 0
            pvt = psm.next()
            mm(pvt, pvt[:], KV, KV[:, 128 + bq * 128:128 + (bq + 1) * 128], ident, ident[:])
            vt_cur = Vtm.next()
            op('act', lambda e, vt_cur=vt_cur, pvt=pvt: e.copy(vt_cur[:], pvt[:]), reads=[pvt], writes=[vt_cur])
            if vtm_prev is None:
                vtm_prev = vt_cur
            po = pbig.next()
            for h in range(4):
                pssc = pbig.next()
                mm(pssc, pssc[:], Q4, Q4[:, h, bq * 128:(bq + 1) * 128], KV, KV[0:64, bq * 128:bq * 128 + 256])
                sc = tmp.next()
                op('dve', lambda e, sc=sc, pssc=pssc: e.scalar_tensor_tensor(sc[:], pssc[:], scale, mb[:, 1 if blk_first else 0, :], ALU.mult, ALU.add),
                   reads=[pssc, mb], writes=[sc])
                op('dve', lambda e, sc=sc: e.reduce_max(st[:, 0:1], sc[:], axis=AX.X), reads=[sc], writes=[st])
                op('dve', lambda e, h=h: e.tensor_scalar(st[:, 1:2], st[:, 0:1], sinkb[:, h:h + 1], -1.0, ALU.max, ALU.mult), reads=[st, sinkb], writes=[st])
                op('dve', lambda e: e.memset(st[:, 2:3], 0.0), writes=[st])
                pe_ = tmp.next()
                op('act', lambda e, sc=sc, pe_=pe_: e.activation(pe_[:], sc[:], AF.Exp, bias=st[:, 1:2], accum_out=st[:, 2:3]), reads=[sc, st], writes=[pe_, st])
                op('act', lambda e, h=h: e.activation(st[:, 3:4], sinkb[:, h:h + 1], AF.Exp, bias=st[:, 1:2]), reads=[sinkb, st], writes=[st])
                op('dve', lambda e: e.tensor_tensor(st[:, 4:5], st[:, 2:3], st[:, 3:4], ALU.add), reads=[st], writes=[st])
                op('dve', lambda e: e.reciprocal(st[:, 5:6], st[:, 4:5]), reads=[st], writes=[st])
                op('dve', lambda e, pe_=pe_: e.tensor_scalar(pe_[:], pe_[:], st[:, 5:6], None, ALU.mult), reads=[pe_, st], writes=[pe_])
                pts = []
                for half in range(2):
                    ptp = psm.next()
                    mm(ptp, ptp[:], pe_, pe_[:, half * 128:(half + 1) * 128], ident, ident[:])
                    pT = small.next()
                    op('act' if half == 0 else 'dve',
                       (lambda e, pT=pT, ptp=ptp: e.copy(pT[:], ptp[:])) if half == 0 else (lambda e, pT=pT, ptp=ptp: e.tensor_copy(pT[:], ptp[:])),
                       reads=[ptp], writes=[pT])
                    pts.append(pT)
                osl = po[(h % 2) * 64:(h % 2) * 64 + 64, (h // 2) * 128:(h // 2) * 128 + 128]
                mm(po, osl, vtm_prev, vtm_prev[:, 64:128], pts[0], pts[0][:], start=True, stop=False)
                mm(po, osl, vt_cur, vt_cur[:, 64:128], pts[1], pts[1][:], start=False, stop=True)
            vtm_prev = vt_cur
            for jj in range(2):
                sgt = tmp.next()
                pgb = psm.next()
                for k in range(KC):
                    mm(pgb, pgb[:], W, W[:, 6, k, jj * 128:(jj + 1) * 128], B, B[:, k, 2 + bq * 128:2 + (bq + 1) * 128], start=(k == 0), stop=(k == KC - 1))
                op('act', lambda e, sgt=sgt, pgb=pgb: e.activation(sgt[:, 0:128], pgb[:], AF.Silu), reads=[pgb], writes=[sgt])
                op('dve', lambda e, sgt=sgt, jj=jj, bq=bq, po=po: e.tensor_tensor(Yt[:, 2 + jj, bq * 128:(bq + 1) * 128], po[:, jj * 128:(jj + 1) * 128], sgt[:, 0:128], ALU.mult),
                   reads=[po, sgt], writes=[Yt])
        op('pool', lambda e: e.tensor_copy(KV[:, 0:128], KV[:, TE:TE + 128]), reads=[KV], writes=[KV])
        if layer2:
            op('pool', lambda e: e.tensor_copy(B[:, :, 1:2], B[:, :, TE + 1:TE + 2]), reads=[B], writes=[B])
        ev = dma('sp', y_dst.ap.rearrange("(j p) t -> p j t", p=128)[:, :, t0:t0 + TE], Yt[:], reads=[Yt], writes=[y_dst], track=Yt)
        cx.out_events.append(ev)
    cx.release(mark)


A_W = 1024
SHIFT_W = 3200
C_R, C_K, C_V, C_XW, C_AG, C_Q, C_KB, C_VB, C_BG = 0, 1024, 2048, 3072, 3200, 4224, 5248, 5504, 5760


def make_consts_even():
    c = make_consts()
    i = np.arange(128)
    c['bones'] = ((i[:, None] // 64) == (i[None, :] // 64)).astype(np.float32)
    c['idb2'] = ((i[:, None] % 64) == np.arange(64)[None, :]).astype(np.float32)
    t = np.arange(64)
    SL = (t[:, None] > t[None, :]).astype(np.float32)
    SU = SL.T.copy()
    UI = (t[None, :] >= t[:, None]).astype(np.float32)
    em = np.zeros((128, 3, 128), np.float32)
    for hh in range(2):
        bs_ = slice(hh * 64, (hh + 1) * 64)
        em[bs_, 0, bs_] = SL
        em[bs_, 1, bs_] = SU
        em[bs_, 2, bs_] = UI
    c['bm2'] = np.ascontiguousarray(np.repeat(((i[:, None] // 64) == np.arange(2)[None, :]).astype(np.float32)[:, :, None], 64, axis=2))
    c['emsk'] = em
    q = np.arange(128)
    NEG = -30000.0
    am = np.full((128, 2, 256), NEG, np.float32)
    prev_ok = q[None, :] > q[:, None]
    cur_ok = q[None, :] <= q[:, None]
    am[:, 0, 0:128] = np.where(prev_ok, 0.0, NEG)
    am[:, 0, 128:256] = np.where(cur_ok, 0.0, NEG)
    am[:, 1, 128:256] = np.where(cur_ok, 0.0, NEG)
    c['amask'] = am
    rmat = np.zeros((128, 128), np.float32)
    for p in range(128):
        d = p % 64
        if d < 8:
            rmat[p + 8, p] = -1.0
        elif d < 16:
            rmat[p - 8, p] = 1.0
    c['ropem'] = rmat
    rst = np.ones((128, TE), np.float32)
    rst[:, ::LCH] = 0.0
    c['rst'] = rst
    cvec = np.zeros((128, 8), np.float32)
    cvec[:, 0] = RMS_EPS
    cvec[:, 1] = 1e-12
    cvec[:, 2] = GN_EPS
    cvec[:, 3] = math.pi / 2
    inv = np.power(np.float32(500000.0), -np.arange(8, dtype=np.float32) / np.float32(8)).astype(np.float32)
    fq = np.zeros(128, np.float32)
    for p in range(128):
        d = p % 64
        if d < 16:
            fq[p] = inv[d % 8]
    fk = fq.copy()
    fk[64:] = 0.0
    cvec[:, 4] = fq
    cvec[:, 5] = fk
    c['cvec'] = cvec
    return c


def prep_even(e, hg, P):
    W = P['e_w_in'][e]
    hs = slice(hg * 256, (hg + 1) * 256)

    def cols(base):
        return W[:, base + hg * 256: base + (hg + 1) * 256]
    kvb = np.concatenate([W[:, C_KB + hg * 64: C_KB + (hg + 1) * 64], W[:, C_VB + hg * 64: C_VB + (hg + 1) * 64]], axis=1)
    wcols = np.concatenate([W[:, C_XW:C_XW + 128], kvb, cols(C_R), cols(C_K), cols(C_V), cols(C_AG), cols(C_Q), cols(C_BG)], axis=1)
    mu = P['e_mu'][e]
    vec = np.zeros((128, 40), np.float32)

    def put(col, v256):
        vec[:, col:col + 2] = v256.reshape(2, 128).T
    put(0, mu[C_R:C_R + 1024][hs]); put(2, mu[C_K:C_K + 1024][hs]); put(4, mu[C_V:C_V + 1024][hs])
    put(6, P['rwkv_w0'][e][hs]); put(8, P['rwkv_a0'][e][hs]); put(10, P['rwkv_k_k'][e][hs]); put(12, P['rwkv_k_a'][e][hs])
    put(16, P['rwkv_r_k'][e].reshape(-1)[hs]); put(18, P['rwkv_ln_w'][e][hs]); put(20, P['rwkv_ln_b'][e][hs])
    if e > 0:
        put(22, P['rwkv_v0'][e - 1][hs])
    vec[:, 24] = mu[C_XW:C_XW + 128]
    w2a2 = np.concatenate([P['rwkv_w2'][e][:, hs], P['rwkv_a2'][e][:, hs]], axis=0)
    out = {
        'norm': vec_pk(P['e_norm'][e]),
        'w_in': tile_w(wcols),
        'vec': vec,
        'w2a2': np.ascontiguousarray(w2a2, dtype=np.float32),
        'sinks': np.ascontiguousarray(P['attn_sinks'][e][hg * 4:(hg + 1) * 4], dtype=np.float32),
    }
    if e > 0:
        out['wvT'] = np.ascontiguousarray(W[:, C_V:C_V + 1024].T, dtype=np.float32)
        out['v1'] = np.ascontiguousarray(P['rwkv_v1'][e - 1].reshape(8, 128, 32).transpose(1, 0, 2), dtype=np.float32)
        out['muv'] = vec_pk(mu[C_V:C_V + 1024])
        out['v2'] = np.ascontiguousarray(P['rwkv_v2'][e - 1][:, hs], dtype=np.float32)
    return out


SEQ = 8192
_PROG_CACHE = {}


def emit_outproj_stage(cm, w_out, y_src, x_src, x_dst):
    cx, TT = cm.cx, cm.TT
    for t0 in range(0, NTOK, TT):
        cx.dma('pool', cm.Y[:], y_src.ap.rearrange("(k p) t -> p k t", p=128)[:, :, t0:t0 + TT], reads=[y_src], writes=[cm.Y], track=cm.Y)
        cm.out_proj_residual(w_out, x_src, x_dst, t0)


def build_E(e, in_shapes):
    nc = bass.Bass("TRN2", target_bir_lowering=False)
    aps = {k: declare(nc, k, v) for k, v in in_shapes.items()}
    yo = nc.dram_tensor("y", [512, SEQ], F32, kind="ExternalOutput").ap()
    cx = Ctx(nc)
    PP = {k[2:]: aps[k] for k in aps if k.startswith('p_')}
    CC = {k[2:]: aps[k] for k in aps if k.startswith('c_')}
    yd = DT(yo)
    if e == 0:
        vfo = nc.dram_tensor("vf", [256, SEQ], F32, kind="ExternalOutput").ap()
        vd = DT(vfo)
        emit_even_phase(cx, PP, CC, DT(aps['xT']), yd, vd, None, SEQ, False)
        for key, v in vd.wd.items():
            cx.out_events.append((key, v))
    else:
        emit_even_phase(cx, PP, CC, DT(aps['xT']), yd, None, DT(aps['vfin']), SEQ, True)
    cx.finish()
    return nc


def build_O(final, in_shapes):
    nc = bass.Bass("TRN2", target_bir_lowering=False)
    aps = {k: declare(nc, k, v) for k, v in in_shapes.items()}
    xo = nc.dram_tensor("xo", [D, NTOK], F32, kind="ExternalOutput").ap()
    xm = nc.dram_tensor("xmid", [D, NTOK], F32, kind="Internal").ap()
    cx = Ctx(nc)
    CC = {k[2:]: aps[k] for k in aps if k.startswith('c_')}
    cm = Common(cx, CC, 512)
    PP = {k[2:]: aps[k] for k in aps if k.startswith('p_')}
    x_in, y_in, x_mid, x_out = DT(aps['xT']), DT(aps['yT']), DT(xm), DT(xo)
    emit_outproj_stage(cm, aps['wout_e'], y_in, x_in, x_mid)
    if not final:
        emit_odd_layer(cm, PP, x_mid, x_out)
        for key, v in x_out.wd.items():
            cx.out_events.append((key, v))
    else:
        xm2 = nc.dram_tensor("xmid2", [D, NTOK], F32, kind="Internal").ap()
        x_mid2 = DT(xm2)
        emit_odd_layer(cm, PP, x_mid, x_mid2)
        fn = cx.sb([128, KC], F32)
        cx.dma('sp', fn[:], aps['fnorm'], writes=[fn], track=fn)
        emit_final_norm(cm, x_mid2, x_out, fn)
    cx.finish()
    return nc


def _shapes(d):
    return {k: v for k, v in d.items()}


def kernel(**inputs):
    P = {k: np.asarray(v) for k, v in inputs.items()}
    x = P['x'].astype(np.float32, copy=False)
    pos = P['positions'].astype(np.int32, copy=False)
    B_ = x.shape[0]
    cores = list(range(8))
    ce = make_consts_even()
    co = make_consts()
    xT_b = [np.ascontiguousarray(x[b].T) for b in range(B_)]
    vf_cores = None
    for e in range(2):
        in_maps = []
        for c in cores:
            b, hg = c // 4, c % 4
            m = {'xT': xT_b[b]}
            for k, v in ce.items():
                m['c_' + k] = v
            for k, v in prep_even(e, hg, P).items():
                m['p_' + k] = v
            m['p_pos'] = np.ascontiguousarray(pos[b])
            if e > 0:
                m['vfin'] = vf_cores[c]
            in_maps.append(m)
        nc = build_E(e, in_maps[0])
        res = run_bass_kernel_spmd(nc, in_maps, core_ids=cores)
        if e == 0:
            vf_cores = [np.ascontiguousarray(res.results[c]['vf']) for c in cores]
        yT_b = []
        for b in range(B_):
            yT = np.empty((2048, SEQ), np.float32)
            for hg in range(4):
                r = res.results[b * 4 + hg]['y']
                yT[hg * 256:(hg + 1) * 256] = r[0:256]
                yT[1024 + hg * 256:1024 + (hg + 1) * 256] = r[256:512]
            yT_b.append(yT)
        del res
        final = (e == 1)
        in_maps = []
        podd = prep_odd(P['o_norm'][e], P['o_w_in'][e], P['sgu_ln_w'][e], P['sgu_ln_b'][e], P['sgu_ws'][e], P['sgu_bs'][e], P['o_w_out'][e])
        wout_e = tile_w(P['e_w_out'][e])
        for c in cores:
            b, sg = c // 4, c % 4
            sl = slice(sg * NTOK, (sg + 1) * NTOK)
            m = {'xT': np.ascontiguousarray(xT_b[b][:, sl]), 'yT': np.ascontiguousarray(yT_b[b][:, sl]), 'wout_e': wout_e}
            for k, v in co.items():
                m['c_' + k] = v
            for k, v in podd.items():
                m['p_' + k] = v
            if final:
                m['fnorm'] = vec_pk(P['final_norm'])
            in_maps.append(m)
        nc = build_O(final, in_maps[0])
        res = run_bass_kernel_spmd(nc, in_maps, core_ids=cores)
        for b in range(B_):
            xT_b[b] = np.concatenate([res.results[b * 4 + sg]['xo'] for sg in range(4)], axis=1)
        del res
    out = np.stack([np.ascontiguousarray(xT_b[b].T) for b in range(B_)], axis=0)
    return out.astype(np.float32, copy=False)
```

```python
import math
import numpy as np
import concourse.bass as bass
import concourse.mybir as mybir
from concourse.bass_utils import run_bass_kernel_spmd

F32 = mybir.dt.float32
F32R = mybir.dt.float32r
I32 = mybir.dt.int32
AF = mybir.ActivationFunctionType
ALU = mybir.AluOpType
AX = mybir.AxisListType

D = 2048
NTOK = 2048
KC = 16
RMS_EPS = 1e-5
LN_EPS = 1e-5


class T:
    def __init__(self, t):
        self.t = t
        self.w = None
        self.r = {}
        self.dsem = None
        self.tr = self

    def __getitem__(self, k):
        return self.t[k]


class DT(T):
    def __init__(self, ap):
        super().__init__(None)
        self.ap = ap
        self.wd = {}


class Ctx:
    def __init__(self, nc):
        self.nc = nc
        self.eng = {'pe': nc.tensor, 'act': nc.scalar, 'dve': nc.vector, 'pool': nc.gpsimd, 'sp': nc.sync}
        self.sems = {}
        self.cnt = {}
        for k in ('pe', 'act', 'dve', 'pool'):
            self.sems[k] = nc.semaphore('s_' + k).__enter__()
            self.cnt[k] = 0
        self.seen = {}
        self.nsb = 0
        self.nps = 0
        self.ndram = 0
        self.out_events = []
        self.stack = []

    def sb(self, shape, dt=F32, name=None):
        self.nsb += 1
        cmgr = self.nc.sbuf_tensor(name or f"sb{self.nsb}", list(shape), dt)
        t = T(cmgr.__enter__())
        self.stack.append(cmgr)
        return t

    def mark(self):
        return len(self.stack)

    def release(self, mark):
        self.barrier()
        while len(self.stack) > mark:
            self.stack.pop().__exit__(None, None, None)

    def barrier(self):
        for e in ('pe', 'act', 'dve', 'pool', 'sp'):
            for k, v in self.cnt.items():
                if v > 0 and k != e and self.seen.get((e, k), 0) < v:
                    self.eng[e].wait_ge(self.sems[k], v)
                    self.seen[(e, k)] = v

    def ps(self, shape, dt=F32, name=None):
        self.nps += 1
        t = T(self.nc.psum_tensor(name or f"ps{self.nps}", list(shape), dt).__enter__())
        t.excl = True
        return t

    def ps_views(self, nbanks, width):
        banks = []
        for b in range(nbanks):
            self.nps += 1
            banks.append(T(self.nc.psum_tensor(f"pbank{self.nps}", [128, 512], F32).__enter__()))
            banks[-1].excl = True
        out = []
        for i in range(512 // width):
            for bk in banks:
                v = T(bk.t[:, i * width:(i + 1) * width])
                v.tr = bk
                out.append(v)
        return out

    def _dsem(self, t):
        if t.dsem is None:
            key = f"d{len(self.sems)}"
            self.sems[key] = self.nc.semaphore('s_' + key).__enter__()
            self.cnt[key] = 0
            t.dsem = key
        return t.dsem

    def _waits(self, e, reads, writes):
        need = {}
        reads = [t.tr for t in reads]
        writes = [t.tr for t in writes]
        for t in reads:
            if isinstance(t, DT):
                for k, v in t.wd.items():
                    need[k] = max(need.get(k, 0), v)
            elif t.w:
                need[t.w[0]] = max(need.get(t.w[0], 0), t.w[1])
            if getattr(t, 'excl', False):
                for k, v in t.r.items():
                    if k != e:
                        need[k] = max(need.get(k, 0), v)
        for t in writes:
            if t.w and not isinstance(t, DT):
                need[t.w[0]] = max(need.get(t.w[0], 0), t.w[1])
            for k, v in t.r.items():
                need[k] = max(need.get(k, 0), v)
        eng = self.eng[e]
        for k, v in need.items():
            if e == 'pe' and k == 'pe':
                continue
            if self.seen.get((e, k), 0) < v:
                eng.wait_ge(self.sems[k], v)
                self.seen[(e, k)] = v

    def op(self, e, fn, reads=(), writes=()):
        self._waits(e, reads, writes)
        ins = fn(self.eng[e])
        self.cnt[e] += 1
        ins.then_inc(self.sems[e], 1)
        ev = (e, self.cnt[e])
        for t in reads:
            t.tr.r[e] = ev[1]
        for t in writes:
            t.tr.w = ev
            t.tr.r = {}
        return ins

    def dma(self, q, out, in_, reads=(), writes=(), track=None):
        self._waits(q, reads, writes)
        t = track
        key = self._dsem(t)
        ins = self.eng[q].dma_start(out=out, in_=in_)
        self.cnt[key] += 16
        ins.then_inc(self.sems[key], 16)
        ev = (key, self.cnt[key])
        for r_ in reads:
            r_.r[key] = ev[1]
        for w_ in writes:
            if isinstance(w_, DT):
                w_.wd[key] = ev[1]
            else:
                w_.w = ev
                w_.r = {}
        return ev

    def collective(self, kind, in_ap, out_ap, groups, in_dt, out_dt):
        self._waits('pool', [in_dt], [out_dt])
        key = self._dsem(out_dt)
        ins = self.nc.gpsimd.collective_compute(kind, ALU.bypass, replica_groups=groups, ins=[in_ap], outs=[out_ap])
        self.cnt[key] += 16
        ins.then_inc(self.sems[key], 16)
        ev = (key, self.cnt[key])
        in_dt.r[key] = ev[1]
        out_dt.wd[key] = ev[1]
        return ev

    def finish(self):
        for key, v in self.out_events:
            self.eng['sp'].wait_ge(self.sems[key], v)


def r32(ap):
    return ap.bitcast(F32R)


def f32(ap):
    return ap.bitcast(F32)


class Common:
    def __init__(self, cx, consts, TT):
        nc = cx.nc
        self.cx = cx
        self.TT = TT
        self.ones = cx.sb([128, 128], F32R, "ones")
        cx.dma('pool', self.ones[:], consts['ones'], writes=[self.ones], track=self.ones)
        self.tri = cx.sb([128, 128], F32, "tri")
        cx.dma('sp', self.tri[:], consts['tri'], writes=[self.tri], track=self.tri)
        self.ident = cx.sb([128, 128], F32, "ident")
        cx.dma('sp', self.ident[:], consts['ident'], writes=[self.ident], track=self.ident)
        self.A = cx.sb([128, KC, TT], F32, "bufA")
        self.Y = cx.sb([128, KC, TT], F32R, "bufY")
        self.B = cx.sb([128, KC, TT], F32R, "bufB")
        self.wb = [cx.sb([128, KC, 256], F32R, f"wb{i}") for i in range(2)]
        self.wi = 0
        self.pbig = [cx.ps([128, 512], F32, f"pbig{i}") for i in range(4)]
        self.pi = 0
        self.sq = [cx.sb([128, TT], F32R, f"sq{i}") for i in range(1)]
        self.rstd = cx.sb([128, TT], F32, "rstd")
        self.xc = [cx.sb([128, TT], F32, f"xc{i}") for i in range(2)]
        self.xo = [cx.sb([128, TT], F32, f"xo{i}") for i in range(2)]
        self.xci = 0
        self.eps_rms = cx.sb([128, 1], F32, "eps_rms")
        cx.op('dve', lambda e: e.memset(self.eps_rms[:], RMS_EPS), writes=[self.eps_rms])

    def next_w(self):
        w = self.wb[self.wi % len(self.wb)]
        self.wi += 1
        return w

    def next_p(self):
        p = self.pbig[self.pi % len(self.pbig)]
        self.pi += 1
        return p

    def load_w(self, dram_group):
        w = self.next_w()
        self.cx.dma('pool', w[:], dram_group, writes=[w], track=w)
        return w

    def load_x_and_norm(self, x_src, t0, gvec):
        cx, TT = self.cx, self.TT
        A, B = self.A, self.B
        src = x_src.ap.rearrange("(k p) t -> p k t", p=128)[:, :, t0:t0 + TT]
        cx.dma('sp', A[:], src, reads=[x_src], writes=[A], track=A)
        pss = self.next_p()
        for k in range(KC):
            sq = self.sq[0]
            cx.op('act', lambda e, k=k, sq=sq: e.activation(sq[:], A[:, k, :], AF.Square), reads=[A], writes=[sq])
            cx.op('pe', lambda e, k=k, sq=sq: e.matmul(pss[:, :TT], self.ones[:], sq[:], start=(k == 0), stop=(k == KC - 1)),
                  reads=[sq, self.ones], writes=[pss])
        rstd = self.rstd
        cx.op('act', lambda e: e.activation(rstd[:], pss[:, :TT], AF.Sqrt, bias=self.eps_rms[:], scale=1.0 / D),
              reads=[pss, self.eps_rms], writes=[rstd])
        cx.op('dve', lambda e: e.reciprocal(rstd[:], rstd[:]), reads=[rstd], writes=[rstd])
        for k in range(KC):
            cx.op('dve', lambda e, k=k: e.scalar_tensor_tensor(B[:, k, :], A[:, k, :], gvec[:, k:k + 1], rstd[:],
                                                                ALU.mult, ALU.mult),
                  reads=[A, gvec, rstd], writes=[B])

    def out_proj_residual(self, wout_groups, x_src, x_dst, t0, is_output=False):
        cx, TT = self.cx, self.TT
        A = self.Y
        xs = x_src.ap.rearrange("(k p) t -> p k t", p=128)
        xd = x_dst.ap.rearrange("(k p) t -> p k t", p=128)
        for g in range(8):
            w = self.load_w(wout_groups[g])
            for j in range(2):
                dk = g * 2 + j
                ps = self.next_p()
                for k in range(KC):
                    cx.op('pe', lambda e, k=k, j=j, w=w, ps=ps: e.matmul(ps[:, :TT], w[:, k, j * 128:(j + 1) * 128],
                                                                         A[:, k, :], start=(k == 0), stop=(k == KC - 1)),
                          reads=[w, A], writes=[ps])
                xc = self.xc[self.xci % 2]
                xo = self.xo[self.xci % 2]
                self.xci += 1
                cx.dma('sp', xc[:], xs[:, dk, t0:t0 + TT], reads=[x_src], writes=[xc], track=xc)
                cx.op('dve', lambda e, xc=xc, xo=xo, ps=ps: e.tensor_tensor(xo[:], ps[:, :TT], xc[:], ALU.add),
                      reads=[ps, xc], writes=[xo])
                ev = cx.dma('sp', xd[:, dk, t0:t0 + TT], xo[:], reads=[xo], writes=[x_dst], track=xo)
                if is_output:
                    cx.out_events.append(ev)


def emit_final_norm(cm, x_src, out_dst, gvec):
    cx, TT = cm.cx, cm.TT
    od = out_dst.ap.rearrange("(k p) t -> p k t", p=128)
    for t0 in range(0, NTOK, TT):
        cm.load_x_and_norm(x_src, t0, gvec)
        A = cm.A
        for k in range(KC):
            cx.op('dve', lambda e, k=k: e.scalar_tensor_tensor(A[:, k, :], A[:, k, :], gvec[:, k:k + 1], cm.rstd[:],
                                                                ALU.mult, ALU.mult),
                  reads=[A, gvec, cm.rstd], writes=[A])
        ev = cx.dma('sp', od[:, :, t0:t0 + TT], A[:], reads=[A], writes=[out_dst], track=A)
        cx.out_events.append(ev)


def emit_odd_layer(cm, P, x_src, x_dst):
    cx, TT = cm.cx, cm.TT
    mark = cx.mark()
    nb = TT // 128
    g = cx.sb([128, KC], F32, None)
    cx.dma('sp', g[:], P['norm'], writes=[g], track=g)
    lnw = cx.sb([128, D], F32)
    lnb = cx.sb([128, D], F32)
    cx.dma('sp', lnw[:], P['ln_w'].partition_broadcast(128), writes=[lnw], track=lnw)
    cx.dma('sp', lnb[:], P['ln_b'].partition_broadcast(128), writes=[lnb], track=lnb)
    bsb2 = [cx.sb([128, 128], F32) for _ in range(2)]
    wm = cx.sb([128, 16, 128], F32R)
    wstage = cm.A[:, 0:D // TT, :].rearrange("p k t -> p (k t)").rearrange("p (g t) -> p g t", t=128)
    cx.dma('sp', wstage, P['wsT'], writes=[cm.A], track=cm.A)
    cx.op('pool', lambda e: e.tensor_tensor(wm[:], wstage, cm.tri[:].unsqueeze(1).to_broadcast([128, 16, 128]), ALU.mult),
          reads=[cm.A, cm.tri], writes=[wm])
    eps_ln = cx.sb([128, 1], F32)
    cx.op('dve', lambda e: e.memset(eps_ln[:], LN_EPS), writes=[eps_ln])
    vv = cx.sb([128, nb, D], F32R)
    st = cx.sb([128, 8], F32)
    sg = [cx.sb([128, TT], F32) for _ in range(1)]
    t1 = [cx.sb([128, TT], F32) for _ in range(2)]

    for t0 in range(0, NTOK, TT):
        cm.load_x_and_norm(x_src, t0, g)
        B = cm.B
        for gi in range(8):
            w = cm.load_w(P['w_in'][16 + gi])
            for b in range(nb):
                ps = cm.next_p()
                for k in range(KC):
                    cx.op('pe', lambda e, k=k, b=b, w=w, ps=ps: e.matmul(ps[:, :256], B[:, k, b * 128:(b + 1) * 128], w[:, k, :],
                                                                         start=(k == 0), stop=(k == KC - 1)),
                          reads=[B, w], writes=[ps])
                cx.op('act', lambda e, b=b, gi=gi, ps=ps: e.copy(vv[:, b, gi * 256:(gi + 1) * 256], ps[:, :256]),
                      reads=[ps], writes=[vv])
        for b in range(nb):
            cx.op('dve', lambda e, b=b: e.reduce_sum(st[:, 0:1], f32(vv[:, b, :]), axis=AX.X), reads=[vv], writes=[st])
            cx.op('dve', lambda e: e.tensor_scalar(st[:, 1:2], st[:, 0:1], -1.0 / D, None, ALU.mult), reads=[st], writes=[st])
            cx.op('dve', lambda e, b=b: e.tensor_scalar(vv[:, b, :], f32(vv[:, b, :]), st[:, 1:2], None, ALU.add),
                  reads=[vv, st], writes=[vv])
            cx.op('dve', lambda e: e.memset(st[:, 2:3], 0.0), writes=[st])
            cx.op('act', lambda e, b=b: e.activation(cm.A[:, 0:D // TT, :].rearrange("p k t -> p (k t)"), f32(vv[:, b, :]), AF.Square,
                                                     accum_out=st[:, 2:3]),
                  reads=[vv], writes=[cm.A, st])
            cx.op('act', lambda e: e.activation(st[:, 3:4], st[:, 2:3], AF.Sqrt, bias=eps_ln[:], scale=1.0 / D),
                  reads=[st, eps_ln], writes=[st])
            cx.op('dve', lambda e: e.reciprocal(st[:, 4:5], st[:, 3:4]), reads=[st], writes=[st])
            cx.op('dve', lambda e, b=b: e.scalar_tensor_tensor(vv[:, b, :], f32(vv[:, b, :]), st[:, 4:5], lnw[:], ALU.mult, ALU.mult),
                  reads=[vv, st, lnw], writes=[vv])
            cx.op('pool', lambda e, b=b: e.tensor_tensor(vv[:, b, :], f32(vv[:, b, :]), lnb[:], ALU.add),
                  reads=[vv, lnb], writes=[vv])
        A = cm.Y
        for gi in range(16):
            w = cm.load_w(P['w_in'][gi])
            pu = cm.next_p()
            pg = cm.next_p()
            for k in range(KC):
                cx.op('pe', lambda e, k=k, w=w, pu=pu: e.matmul(pu[:, :TT], w[:, k, 0:128], B[:, k, :], start=(k == 0), stop=(k == KC - 1)),
                      reads=[B, w], writes=[pu])
            for k in range(KC):
                cx.op('pe', lambda e, k=k, w=w, pg=pg: e.matmul(pg[:, :TT], w[:, k, 128:256], B[:, k, :], start=(k == 0), stop=(k == KC - 1)),
                      reads=[B, w], writes=[pg])
            pm = cm.next_p()
            for b in range(nb):
                cx.op('pe', lambda e, b=b, gi=gi, pm=pm: e.matmul(pm[:, b * 128:(b + 1) * 128], vv[:, b, gi * 128:(gi + 1) * 128], wm[:, gi, :],
                                                                  start=True, stop=True),
                      reads=[vv, wm], writes=[pm])
            sgt = sg[0]
            t1t = t1[gi % 2]
            bsb = bsb2[gi % 2]
            cx.dma('sp', bsb[:], P['bs'][gi * 128:(gi + 1) * 128].partition_broadcast(128), writes=[bsb], track=bsb)
            cx.op('act', lambda e, sgt=sgt, pg=pg: e.activation(sgt[:], pg[:, :TT], AF.Silu), reads=[pg], writes=[sgt])
            cx.op('dve', lambda e, sgt=sgt, pu=pu: e.tensor_tensor(sgt[:], pu[:, :TT], sgt[:], ALU.mult), reads=[pu, sgt], writes=[sgt])
            cx.op('dve', lambda e, t1t=t1t, pm=pm, bsb=bsb: e.tensor_tensor(
                t1t[:].rearrange("p (b t) -> p b t", t=128), pm[:, :TT].rearrange("p (b t) -> p b t", t=128),
                bsb[:].unsqueeze(1).to_broadcast([128, nb, 128]), ALU.add), reads=[pm, bsb], writes=[t1t])
            cx.op('pool', lambda e, t1t=t1t, sgt=sgt, gi=gi: e.tensor_tensor(A[:, gi, :], t1t[:], sgt[:], ALU.mult),
                  reads=[t1t, sgt], writes=[A])
        cm.out_proj_residual(P['w_out'], x_src, x_dst, t0)
    cx.release(mark)


def tile_w(Wcols):
    n = Wcols.shape[1] // 256
    a = Wcols.reshape(KC, 128, n, 256).transpose(2, 1, 0, 3)
    return np.ascontiguousarray(a, dtype=np.float32)


def vec_pk(v):
    return np.ascontiguousarray(v.reshape(-1, 128).T, dtype=np.float32)


def prep_odd(o_norm, o_w_in, ln_w, ln_b, ws, bs, w_out):
    u = o_w_in[:, 0:2048].reshape(2048, 16, 128)
    gt = o_w_in[:, 4096:6144].reshape(2048, 16, 128)
    ug = np.concatenate([u, gt], axis=2).reshape(2048, 16 * 256)
    wcols = np.concatenate([ug, o_w_in[:, 2048:4096]], axis=1)
    return {
        'norm': vec_pk(o_norm),
        'w_in': tile_w(wcols),
        'ln_w': np.ascontiguousarray(ln_w, dtype=np.float32),
        'ln_b': np.ascontiguousarray(ln_b, dtype=np.float32),
        'wsT': np.ascontiguousarray(ws.transpose(2, 0, 1), dtype=np.float32),
        'bs': np.ascontiguousarray(bs.reshape(-1), dtype=np.float32),
        'w_out': tile_w(w_out),
    }


def make_consts():
    s = np.arange(128)
    return {
        'ones': np.ones((128, 128), np.float32),
        'tri': (s[None, :] >= s[:, None]).astype(np.float32),
        'ident': np.eye(128, dtype=np.float32),
    }


def declare(nc, name, arr, kind="ExternalInput"):
    dt = I32 if arr.dtype == np.int32 else F32
    return nc.dram_tensor(name, list(arr.shape), dt, kind=kind).ap()


TE = 256
LCH = 64
NCH = TE // LCH
GN_EPS = 64 * 1e-5
TWO_PI = 2.0 * math.pi


class Ring:
    def __init__(self, tiles):
        self.t = tiles
        self.i = 0

    def next(self):
        t = self.t[self.i % len(self.t)]
        self.i += 1
        return t


def emit_even_phase(cx, P, consts, x_src, y_dst, vf_dst, vf_src, ntok, layer2, dbg=9):
    nc = cx.nc
    mark = cx.mark()
    sb, ps, op, dma = cx.sb, cx.ps, cx.op, cx.dma
    ones = sb([128, 128], F32R, "e_ones")
    dma('pool', ones[:], consts['ones'], writes=[ones], track=ones)
    ident = sb([128, 128], F32, "e_ident")
    dma('sp', ident[:], consts['ident'], writes=[ident], track=ident)
    bones = sb([128, 128], F32, "e_bones")
    dma('sp', bones[:], consts['bones'], writes=[bones], track=bones)
    msk = sb([128, 3, 128], F32, "e_msk")
    dma('sp', msk[:], consts['emsk'], writes=[msk], track=msk)
    mb = sb([128, 2, 256], F32, "e_mb")
    dma('sp', mb[:], consts['amask'], writes=[mb], track=mb)
    rm = sb([128, 128], F32, "e_rm")
    dma('sp', rm[:], consts['ropem'], writes=[rm], track=rm)
    rst = sb([128, TE], F32, "e_rst")
    dma('sp', rst[:], consts['rst'], writes=[rst], track=rst)
    cv = sb([128, 8], F32, "e_cv")
    dma('sp', cv[:], consts['cvec'], writes=[cv], track=cv)
    g = sb([128, KC], F32, "e_g")
    dma('sp', g[:], P['norm'], writes=[g], track=g)
    vec = sb([128, 40], F32, "e_vec")
    dma('sp', vec[:], P['vec'], writes=[vec], track=vec)
    op('dve', lambda e: e.tensor_scalar(vec[:, 14:16], vec[:, 12:14], -1.0, 1.0, ALU.mult, ALU.add), reads=[vec], writes=[vec])
    w2a2 = sb([128, 256], F32, "e_w2a2")
    dma('sp', w2a2[:], P['w2a2'], writes=[w2a2], track=w2a2)
    sinkb = sb([128, 4], F32, "e_sink")
    dma('sp', sinkb[:], P['sinks'].partition_broadcast(128), writes=[sinkb], track=sinkb)
    W = sb([128, 7, KC, 256], mybir.dt.bfloat16, "e_W")
    for gi in range(7):
        dma('pool', W[:, gi], P['w_in'][gi], writes=[W], track=W)
    V_MUR, V_MUK, V_MUV, V_W0, V_A0, V_KK, V_KA, V_OMKA, V_RK, V_LNW, V_LNB, V_V0, V_MUX = 0, 2, 4, 6, 8, 10, 12, 14, 16, 18, 20, 22, 24

    xr = Ring([sb([128, TE], F32, f"e_xr{i}") for i in range(2)])
    B = sb([128, KC, TE + 2], mybir.dt.bfloat16, "e_B")
    sq = sb([128, TE], F32R, "e_sq")
    Z = {n: sb([128, 2, TE + 1], F32, "e_z" + n) for n in ('r', 'k', 'v')}
    Zx = sb([128, TE + 1], F32, "e_zx")
    for zt in list(Z.values()) + [Zx]:
        op('pool', lambda e, zt=zt: e.memset(zt[:], 0.0), writes=[zt])
    KV = sb([128, 128 + TE], F32, "e_kv")
    op('pool', lambda e: e.memset(KV[:], 0.0), writes=[KV])
    Yt = sb([128, 4, TE], F32, "e_y")
    S2 = [[sb([128, 128], F32, f"e_S{h}_{i}") for i in range(2)] for h in range(2)]
    for h in range(2):
        op('pool', lambda e, h=h: e.memset(S2[h][0][:], 0.0), writes=[S2[h][0]])
    si = [0, 0]
    bm2 = sb([128, 2, 64], F32, "e_bm2")
    dma('sp', bm2[:], consts['bm2'], writes=[bm2], track=bm2)
    tmp = Ring([sb([128, TE], F32, f"e_t{i}") for i in range(7)])
    ACM = sb([128, 2, TE], F32, "e_acm")
    RW4 = sb([128, 4, TE], F32, "e_rw4")

    def view(ap):
        v = T(ap)
        v.tr = RW4
        return v
    BCM = view(RW4.t[:, 0:2, :])
    BH = view(RW4.t[:, 2, :])
    KH = view(RW4.t[:, 3, :])
    Q4 = sb([64, 4, TE], F32, "e_q4")
    QR = sb([128, 2, TE], F32, "e_qr")
    VS = sb([128, 2, TE], F32, "e_vs")
    ECL = sb([128, TE], F32, "e_ecl")
    CL = sb([128, TE], F32, "e_cl")
    LW = sb([128, TE], F32, "e_lw")
    AA = sb([128, TE], F32, "e_aa")
    KKN = sb([128, TE], F32, "e_kkn")
    KF = sb([128, TE], F32, "e_kf")
    BV = sb([128, TE], F32, "e_bv")
    RSs = [sb([128, TE], F32, f"e_rs{i}") for i in range(2)]
    KSs = [sb([128, TE], F32, f"e_ks{i}") for i in range(2)]
    XS = sb([128, TE], F32, "e_xs")
    TXW = XS
    cosq = sb([64, TE], F32, "e_cos")
    sinq = sb([64, TE], F32, "e_sin")
    RS_ = [dict(small=Ring([sb([128, 128], F32, f"e_s{r}_{i}") for i in range(6)]),
                big=Ring([sb([128, 256], F32, f"e_b{r}_{i}") for i in range(5)]),
                keep=Ring([sb([128, 256], F32, f"e_k{r}_{i}") for i in range(4)]),
                tm4=Ring([sb([128, 512], F32, f"e_tm{r}_{i}") for i in range(1)])) for r in range(2)]
    atmp = Ring([sb([128, TE], F32, f"e_at{i}") for i in range(4)])
    asmall = Ring([sb([128, 128], F32, f"e_as{i}") for i in range(4)])
    yvs = [sb([128, TE], F32, f"e_yv{i}") for i in range(2)]
    vstate = {'p': None}
    st = sb([128, 8], F32, "e_st")
    Vtm = Ring([sb([128, 128], F32, f"e_vtm{i}") for i in range(3)])
    _pv = cx.ps_views(4, 256)
    pbig = Ring([_pv[i] for i in (0, 1, 2, 4, 5, 6)])
    po_ring = Ring([_pv[3], _pv[7]])
    psm = Ring(cx.ps_views(2, 128))
    pfull = Ring(cx.ps_views(2, 512))

    def mm(out_t, out_ap, lhsT_t, lhsT_ap, rhs_t, rhs_ap, start=True, stop=True):
        op('pe', lambda e: e.matmul(out_ap, lhsT_ap, rhs_ap, start=start, stop=stop), reads=[lhsT_t, rhs_t], writes=[out_t])

    scale = 1.0 / 8.0
    op('dve', lambda e: e.tensor_scalar(B[:, :, 0:2], g[:].unsqueeze(2).to_broadcast([128, KC, 2]), 0.0, None, ALU.mult), reads=[g], writes=[B])

    if layer2:
        v2t = sb([32, 256], F32, "e_v2")
        dma('sp', v2t[:], P['v2'], writes=[v2t], track=v2t)
        Weff = sb([128, KC, 64], mybir.dt.bfloat16, "e_weff")
        VFt = sb([128, 2, TE], F32, "e_vft")
        v1t = tmp.next()
        v1v = v1t[:].rearrange("p (c m) -> p c m", m=32)
        dma('sp', v1v, P['v1'], writes=[v1t], track=v1t)
        muv = tmp.next()
        dma('sp', muv[:, 0:8], P['muv'], writes=[muv], track=muv)
        M12 = ACM[:].rearrange("p a t -> p (a t)").rearrange("p (c m) -> p c m", m=64)
        for c in range(8):
            op('dve', lambda e, c=c: e.tensor_scalar(M12[:, c, 32:64], v1v[:, c, :], muv[:, c:c + 1], None, ALU.mult), reads=[v1t, muv], writes=[ACM])
            op('dve', lambda e, c=c: e.tensor_tensor(M12[:, c, 0:32], v1v[:, c, :], M12[:, c, 32:64], ALU.subtract), reads=[v1t, ACM], writes=[ACM])
        for dk2 in range(8):
            pw0, pw1 = psm.next(), psm.next()
            for c in range(8):
                piece = xr.next()
                dma('sp', piece[:], P['wvT'][c * 128:(c + 1) * 128, dk2 * 256:(dk2 + 1) * 256], writes=[piece], track=piece)
                mm(pw0, pw0[:, 0:64], piece, piece[:, 0:128], ACM, M12[:, c, :], start=(c == 0), stop=(c == 7))
                mm(pw1, pw1[:, 0:64], piece, piece[:, 128:256], ACM, M12[:, c, :], start=(c == 0), stop=(c == 7))
            op('act', lambda e, dk2=dk2, pw0=pw0: e.copy(Weff[:, 2 * dk2, :], pw0[:, 0:64]), reads=[pw0], writes=[Weff])
            op('dve', lambda e, dk2=dk2, pw1=pw1: e.tensor_copy(Weff[:, 2 * dk2 + 1, :], pw1[:, 0:64]), reads=[pw1], writes=[Weff])

    for t0 in range(0, ntok, TE):
        first_tile = (t0 == 0)
        src = x_src.ap.rearrange("(k p) t -> p k t", p=128)[:, :, t0:t0 + TE]
        rstd = tmp.next()
        pss = pbig.next()
        for k in range(KC):
            xa = xr.next()
            dma('sp', xa[:], src[:, k, :], reads=[x_src], writes=[xa], track=xa)
            op('act', lambda e, xa=xa: e.activation(sq[:], xa[:], AF.Square), reads=[xa], writes=[sq])
            mm(pss, pss[:], ones, ones[:], sq, sq[:], start=(k == 0), stop=(k == KC - 1))
        op('act', lambda e: e.activation(rstd[:], pss[:], AF.Sqrt, bias=cv[:, 0:1], scale=1.0 / D), reads=[pss, cv], writes=[rstd])
        op('dve', lambda e: e.reciprocal(rstd[:], rstd[:]), reads=[rstd], writes=[rstd])
        for k in range(KC):
            xa = xr.next()
            dma('sp', xa[:], src[:, k, :], reads=[x_src], writes=[xa], track=xa)
            op('dve', lambda e, k=k, xa=xa: e.scalar_tensor_tensor(B[:, k, 2:TE + 2], xa[:], g[:, k:k + 1], rstd[:], ALU.mult, ALU.mult),
               reads=[xa, g, rstd], writes=[B])

        if dbg <= 0:
            op('pool', lambda e: e.memset(Yt[:], 0.0), writes=[Yt])
            op('dve', lambda e: e.tensor_copy(Yt[:, 0, :], B[:, 0, 2:TE + 2]), reads=[B], writes=[Yt])
            ev = dma('sp', y_dst.ap.rearrange("(j p) t -> p j t", p=128)[:, :, t0:t0 + TE], Yt[:], reads=[Yt], writes=[y_dst], track=Yt)
            cx.out_events.append(ev)
            continue

        if layer2:
            lvp = pbig.next()
            for k in range(KC):
                mm(lvp, lvp[0:32, :], Weff, Weff[:, k, 0:32], B, B[:, k, 2:TE + 2], start=(k == 0), stop=False)
                mm(lvp, lvp[0:32, :], Weff, Weff[:, k, 32:64], B, B[:, k, 1:TE + 1], start=False, stop=(k == KC - 1))
            LV = tmp.next()
            op('act', lambda e: e.copy(LV[0:32, :], lvp[0:32, :]), reads=[lvp], writes=[LV])
            for j, sgv in ((0, KKN), (1, KF)):
                psg = pbig.next()
                mm(psg, psg[:], v2t, v2t[0:32, j * 128:(j + 1) * 128], LV, LV[0:32, :])
                op('act', lambda e, j=j, sgv=sgv, psg=psg: e.activation(sgv[:], psg[:], AF.Sigmoid, bias=vec[:, V_V0 + j:V_V0 + j + 1]), reads=[psg, vec], writes=[sgv])
            dma('sp', VFt[:], vf_src.ap.rearrange("(j p) t -> p j t", p=128)[:, :, t0:t0 + TE], reads=[vf_src], writes=[VFt], track=VFt)

        def proj(gi, c0, m, dst_t, dst_ap, eng='act'):
            pp = pbig.next()
            for k in range(KC):
                mm(pp, pp[0:m, :], W, W[:, gi, k, c0:c0 + m], B, B[:, k, 2:TE + 2], start=(k == 0), stop=(k == KC - 1))
            if eng == 'act':
                op('act', lambda e: e.copy(dst_ap, pp[0:m, :]), reads=[pp], writes=[dst_t])
            else:
                op('dve', lambda e: e.tensor_copy(dst_ap, pp[0:m, :]), reads=[pp], writes=[dst_t])

        proj(0, 0, 128, Zx, Zx[:, 1:TE + 1])
        if dbg <= 0.2:
            op('pool', lambda e: e.memset(Yt[:], 0.0), writes=[Yt])
            ev = dma('sp', y_dst.ap.rearrange("(j p) t -> p j t", p=128)[:, :, t0:t0 + TE], Yt[:], reads=[Yt], writes=[y_dst], track=Yt)
            cx.out_events.append(ev)
            continue
        proj(0, 128, 128, KV, KV[:, 128:128 + TE], 'dve')
        if dbg <= 0.3:
            op('pool', lambda e: e.memset(Yt[:], 0.0), writes=[Yt])
            ev = dma('sp', y_dst.ap.rearrange("(j p) t -> p j t", p=128)[:, :, t0:t0 + TE], Yt[:], reads=[Yt], writes=[y_dst], track=Yt)
            cx.out_events.append(ev)
            continue
        for j in range(2):
            proj(1, j * 128, 128, Z['r'], Z['r'][:, j, 1:TE + 1], 'act' if j == 0 else 'dve')
            proj(2, j * 128, 128, Z['k'], Z['k'][:, j, 1:TE + 1], 'act' if j == 0 else 'dve')
            proj(3, j * 128, 128, Z['v'], Z['v'][:, j, 1:TE + 1], 'act' if j == 0 else 'dve')

        if dbg <= 0.4:
            op('pool', lambda e: e.memset(Yt[:], 0.0), writes=[Yt])
            ev = dma('sp', y_dst.ap.rearrange("(j p) t -> p j t", p=128)[:, :, t0:t0 + TE], Yt[:], reads=[Yt], writes=[y_dst], track=Yt)
            cx.out_events.append(ev)
            continue
        def shift(zt, view, mucol):
            d = tmp.next()
            op('dve', lambda e: e.tensor_tensor(d[:], view(0), view(1), ALU.subtract), reads=[zt], writes=[d])
            return d

        for n, mu0 in (('r', V_MUR), ('k', V_MUK), ('v', V_MUV)):
            for j in range(2):
                zt = Z[n]
                d = tmp.next()
                op('dve', lambda e, zt=zt, j=j, d=d: e.tensor_tensor(d[:], zt[:, j, 0:TE], zt[:, j, 1:TE + 1], ALU.subtract), reads=[zt], writes=[d])
                dst_t, dst_ap = {'r': (RSs[j], RSs[j][:]), 'k': (KSs[j], KSs[j][:]), 'v': (VS, VS[:, j, :])}[n]
                op('dve', lambda e, zt=zt, j=j, d=d, mc=mu0 + j, dst_ap=dst_ap: e.scalar_tensor_tensor(dst_ap, d[:], vec[:, mc:mc + 1], zt[:, j, 1:TE + 1], ALU.mult, ALU.add),
                   reads=[d, vec, zt], writes=[dst_t])
        dx = tmp.next()
        op('dve', lambda e: e.tensor_tensor(dx[:], Zx[:, 0:TE], Zx[:, 1:TE + 1], ALU.subtract), reads=[Zx], writes=[dx])
        op('dve', lambda e: e.scalar_tensor_tensor(XS[:], dx[:], vec[:, V_MUX:V_MUX + 1], Zx[:, 1:TE + 1], ALU.mult, ALU.add),
           reads=[dx, vec, Zx], writes=[XS])
        op('pool', lambda e: e.tensor_copy(Zx[:, 0:1], Zx[:, TE:TE + 1]), reads=[Zx], writes=[Zx])
        op('act', lambda e: e.activation(TXW[0:64, :], XS[0:64, :], AF.Tanh), reads=[XS], writes=[TXW])
        if dbg <= 0.6:
            op('pool', lambda e: e.memset(Yt[:], 0.0), writes=[Yt])
            ev = dma('sp', y_dst.ap.rearrange("(j p) t -> p j t", p=128)[:, :, t0:t0 + TE], Yt[:], reads=[Yt], writes=[y_dst], track=Yt)
            cx.out_events.append(ev)
            continue
        if layer2:
            for j, sgv in ((0, KKN), (1, KF)):
                d2 = tmp.next()
                op('dve', lambda e, j=j, d2=d2: e.tensor_tensor(d2[:], VFt[:, j, :], VS[:, j, :], ALU.subtract), reads=[VFt, VS], writes=[d2])
                op('dve', lambda e, sgv=sgv, d2=d2: e.tensor_tensor(d2[:], d2[:], sgv[:], ALU.mult), reads=[d2, sgv], writes=[d2])
                op('dve', lambda e, j=j, d2=d2: e.tensor_tensor(VS[:, j, :], VS[:, j, :], d2[:], ALU.add), reads=[VS, d2], writes=[VS])
        if not layer2:
            dma('sp', vf_dst.ap.rearrange("(j p) t -> p j t", p=128)[:, :, t0:t0 + TE], VS[:], reads=[VS], writes=[vf_dst], track=VS)

        if dbg <= 1:
            op('pool', lambda e: e.memset(Yt[:], 0.0), writes=[Yt])
            ev = dma('sp', y_dst.ap.rearrange("(j p) t -> p j t", p=128)[:, :, t0:t0 + TE], Yt[:], reads=[Yt], writes=[y_dst], track=Yt)
            cx.out_events.append(ev)
            continue
        posi_t = tmp.next()
        posi = posi_t[:].bitcast(I32)
        posf = tmp.next()
        dma('sp', posi, P['pos'][t0:t0 + TE].partition_broadcast(128), writes=[posi_t], track=posi_t)
        op('dve', lambda e: e.tensor_copy(posf[:], posi), reads=[posi_t], writes=[posf])

        def rope_tables(fcol, cos_t, sin_t, npart):
            ang = tmp.next()
            op('dve', lambda e: e.tensor_scalar(ang[0:npart, :], posf[0:npart, :], cv[0:npart, fcol:fcol + 1], None, ALU.mult), reads=[posf, cv], writes=[ang])
            n1 = tmp.next()
            op('dve', lambda e: e.tensor_scalar(n1[0:npart, :], ang[0:npart, :], 1.0 / TWO_PI, 12582912.0, ALU.mult, ALU.add), reads=[ang], writes=[n1])
            op('dve', lambda e: e.tensor_scalar(n1[0:npart, :], n1[0:npart, :], -12582912.0, None, ALU.add), reads=[n1], writes=[n1])
            op('dve', lambda e: e.scalar_tensor_tensor(ang[0:npart, :], n1[0:npart, :], -TWO_PI, ang[0:npart, :], ALU.mult, ALU.add), reads=[n1, ang], writes=[ang])
            s2 = tmp.next()
            c2 = n1
            op('act', lambda e: e.activation(s2[0:npart, :], ang[0:npart, :], AF.Sin, scale=0.5), reads=[ang], writes=[s2])
            op('act', lambda e: e.activation(c2[0:npart, :], ang[0:npart, :], AF.Sin, bias=cv[0:npart, 3:4], scale=0.5), reads=[ang, cv], writes=[c2])
            op('dve', lambda e: e.scalar_tensor_tensor(sin_t[0:npart, :], s2[0:npart, :], 2.0, c2[0:npart, :], ALU.mult, ALU.mult), reads=[s2, c2], writes=[sin_t])
            op('dve', lambda e: e.tensor_tensor(s2[0:npart, :], s2[0:npart, :], s2[0:npart, :], ALU.mult), reads=[s2], writes=[s2])
            op('dve', lambda e: e.tensor_scalar(cos_t[0:npart, :], s2[0:npart, :], -2.0, 1.0, ALU.mult, ALU.add), reads=[s2], writes=[cos_t])

        ck, sk = tmp.next(), tmp.next()
        rope_tables(5, ck, sk, 128)
        pr = pbig.next()
        mm(pr, pr[:], rm, rm[:], KV, KV[:, 128:128 + TE])
        t_a = tmp.next()
        op('dve', lambda e: e.tensor_tensor(t_a[:], pr[:], sk[:], ALU.mult), reads=[pr, sk], writes=[t_a])
        op('dve', lambda e: e.tensor_tensor(KV[:, 128:128 + TE], KV[:, 128:128 + TE], ck[:], ALU.mult), reads=[KV, ck], writes=[KV])
        op('dve', lambda e: e.tensor_tensor(KV[:, 128:128 + TE], KV[:, 128:128 + TE], t_a[:], ALU.add), reads=[KV, t_a], writes=[KV])
        rope_tables(4, cosq, sinq, 64)
        if dbg <= 2:
            op('pool', lambda e: e.memset(Yt[:], 0.0), writes=[Yt])
            ev = dma('sp', y_dst.ap.rearrange("(j p) t -> p j t", p=128)[:, :, t0:t0 + TE], Yt[:], reads=[Yt], writes=[y_dst], track=Yt)
            cx.out_events.append(ev)
            continue
        def prep(j):
            RS = RSs[j]
            KS = KSs[j]
            pw = pbig.next()
            mm(pw, pw[:], w2a2, w2a2[0:64, j * 128:(j + 1) * 128], TXW, TXW[0:64, :])
            op('act', lambda e: e.activation(LW[:], pw[:], AF.Sigmoid, bias=vec[:, V_W0 + j:V_W0 + j + 1]), reads=[pw, vec], writes=[LW])
            op('dve', lambda e: e.tensor_scalar(LW[:], LW[:], -math.exp(-0.5), None, ALU.mult), reads=[LW], writes=[LW])
            pa = pbig.next()
            mm(pa, pa[:], w2a2, w2a2[64:128, j * 128:(j + 1) * 128], XS, XS[64:128, :])
            op('act', lambda e: e.activation(AA[:], pa[:], AF.Sigmoid, bias=vec[:, V_A0 + j:V_A0 + j + 1]), reads=[pa, vec], writes=[AA])
            op('dve', lambda e: e.tensor_tensor_scan(CL[:], rst[:], LW[:], 0.0, ALU.mult, ALU.add), reads=[rst, LW], writes=[CL])
            kk = tmp.next()
            op('dve', lambda e: e.tensor_scalar(kk[:], KS[:], vec[:, V_KK + j:V_KK + j + 1], None, ALU.mult), reads=[KS, vec], writes=[kk])
            kk2 = tmp.next()
            op('pool', lambda e: e.tensor_tensor(kk2[:], kk[:], kk[:], ALU.mult), reads=[kk], writes=[kk2])
            pk = pbig.next()
            mm(pk, pk[:], bones, bones[:], kk2, kk2[:])
            op('act', lambda e: e.activation(kk2[:], pk[:], AF.Sqrt, bias=cv[:, 1:2]), reads=[pk, cv], writes=[kk2])
            op('dve', lambda e: e.reciprocal(kk2[:], kk2[:]), reads=[kk2], writes=[kk2])
            op('dve', lambda e: e.tensor_tensor(KKN[:], kk[:], kk2[:], ALU.mult), reads=[kk, kk2], writes=[KKN])
            tk = kk
            op('dve', lambda e: e.tensor_scalar(tk[:], AA[:], vec[:, V_KA + j:V_KA + j + 1], vec[:, V_OMKA + j:V_OMKA + j + 1], ALU.mult, ALU.add),
               reads=[AA, vec], writes=[tk])
            op('dve', lambda e: e.tensor_tensor(KF[:], KS[:], tk[:], ALU.mult), reads=[KS, tk], writes=[KF])
            op('act', lambda e: e.activation(ECL[:], CL[:], AF.Exp), reads=[CL], writes=[ECL])
            encl = tmp.next()
            op('act', lambda e: e.activation(encl[:], CL[:], AF.Exp, scale=-1.0), reads=[CL], writes=[encl])
            eclm = kk2
            op('dve', lambda e: e.tensor_tensor(eclm[:], CL[:], LW[:], ALU.subtract), reads=[CL, LW], writes=[eclm])
            op('act', lambda e: e.activation(eclm[:], eclm[:], AF.Exp), reads=[eclm], writes=[eclm])
            ecL = tmp.next()
            clv = CL[:].rearrange("p (c l) -> p c l", l=LCH)
            op('dve', lambda e: e.tensor_tensor(ecL[:].rearrange("p (c l) -> p c l", l=LCH), clv[:, :, LCH - 1:LCH].to_broadcast([128, NCH, LCH]), clv, ALU.subtract),
               reads=[CL], writes=[ecL])
            op('act', lambda e: e.activation(ecL[:], ecL[:], AF.Exp), reads=[ecL], writes=[ecL])
            op('dve', lambda e: e.scalar_tensor_tensor(ACM[:, 0, :], KKN[:], -1.0, eclm[:], ALU.mult, ALU.mult), reads=[KKN, eclm], writes=[ACM])
            op('pool', lambda e: e.tensor_tensor(ACM[:, 1, :], RS[:], ECL[:], ALU.mult), reads=[RS, ECL], writes=[ACM])
            bb = tk
            op('dve', lambda e: e.tensor_tensor(bb[:], KKN[:], AA[:], ALU.mult), reads=[KKN, AA], writes=[bb])
            op('dve', lambda e: e.tensor_tensor(BCM[:, 0, :], bb[:], encl[:], ALU.mult), reads=[bb, encl], writes=[BCM])
            op('pool', lambda e: e.tensor_tensor(BCM[:, 1, :], KF[:], encl[:], ALU.mult), reads=[KF, encl], writes=[BCM])
            op('dve', lambda e: e.tensor_tensor(BH[:], bb[:], ecL[:], ALU.mult), reads=[bb, ecL], writes=[BH])
            op('pool', lambda e: e.tensor_tensor(KH[:], KF[:], ecL[:], ALU.mult), reads=[KF, ecL], writes=[KH])
            rk = encl
            op('dve', lambda e: e.scalar_tensor_tensor(rk[:], RS[:], vec[:, V_RK + j:V_RK + j + 1], KF[:], ALU.mult, ALU.mult), reads=[RS, vec, KF], writes=[rk])
            pbn = pbig.next()
            mm(pbn, pbn[:], bones, bones[:], rk, rk[:])
            op('dve', lambda e: e.tensor_tensor(BV[:], pbn[:], VS[:, j, :], ALU.mult), reads=[pbn, VS], writes=[BV])

            return None

        def chunk_gen(j, c, R, YV):
            cs = slice(c * LCH, (c + 1) * LCH)
            bmv = bm2[:].unsqueeze(1).to_broadcast([128, 2, 2, 64])
            AXT = R['keep'].next()
            op('dve', lambda e, AXT=AXT: e.tensor_tensor(AXT[:].rearrange("p (a h t) -> p a h t", a=2, h=2), ACM[:, :, cs].unsqueeze(2).to_broadcast([128, 2, 2, 64]), bmv, ALU.mult),
               reads=[ACM, bm2], writes=[AXT])
            BX = R['keep'].next()
            op('pool', lambda e, BX=BX: e.tensor_tensor(BX[:].rearrange("p (a h t) -> p a h t", a=2, h=2), BCM[:, :, cs].unsqueeze(2).to_broadcast([128, 2, 2, 64]), bmv, ALU.mult),
               reads=[BCM, bm2], writes=[BX])
            HX3 = [R['small'].next() for _ in range(3)]
            bm1 = bm2[:]
            for tt_, (src_t, src_ap) in zip(HX3, ((BH, BH[:, cs]), (VS, VS[:, j, cs]), (KH, KH[:, cs]))):
                op('pool', lambda e, tt_=tt_, src_ap=src_ap: e.tensor_tensor(tt_[:].rearrange("p (h t) -> p h t", h=2), src_ap.unsqueeze(1).to_broadcast([128, 2, 64]), bm1, ALU.mult),
                   reads=[src_t, bm2], writes=[tt_])
            yield
            ptm = pfull.next()
            for q_, (src_t, src_ap) in enumerate(((AXT, AXT[:, 0:128]), (HX3[0], HX3[0][:]), (HX3[1], HX3[1][:]), (HX3[2], HX3[2][:]))):
                mm(ptm, ptm[:, q_ * 128:(q_ + 1) * 128], src_t, src_ap, ident, ident[:])
            TM4 = R['tm4'].next()
            op('act', lambda e, TM4=TM4, ptm=ptm: e.copy(TM4[:], ptm[:]), reads=[ptm], writes=[TM4])
            ATm, BHm, Vm, KHm = (TM4[:, q_ * 128:(q_ + 1) * 128] for q_ in range(4))
            yield
            p1 = pbig.next()
            mm(p1, p1[:], AXT, AXT[:, 0:128], BX, BX[:])
            NN = R['keep'].next()
            op('dve', lambda e, NN=NN, p1=p1: e.tensor_tensor(NN[:].rearrange("p (a t) -> p a t", a=2), p1[:].rearrange("p (a t) -> p a t", a=2),
                                                             msk[:, 0, :].unsqueeze(1).to_broadcast([128, 2, 128]), ALU.mult), reads=[p1, msk], writes=[NN])
            p2 = pbig.next()
            mm(p2, p2[:], BX, BX[:, 0:128], AXT, AXT[:])
            TR = R['keep'].next()
            op('dve', lambda e, TR=TR, p2=p2: e.tensor_tensor(TR[:], p2[:], msk[:, 1:3, :].rearrange("p a t -> p (a t)"), ALU.mult), reads=[p2, msk], writes=[TR])
            p3 = psm.next()
            mm(p3, p3[:], BX, BX[:, 128:256], AXT, AXT[:, 128:256])
            QK = R['small'].next()
            op('dve', lambda e, QK=QK, p3=p3: e.tensor_tensor(QK[:], p3[:], msk[:, 2, :], ALU.mult), reads=[p3, msk], writes=[QK])
            yield
            PT = R['big'].next()
            op('act', lambda e, PT=PT, NN=NN: e.copy(PT[:, 0:128], NN[:, 0:128]), reads=[NN], writes=[PT])
            op('dve', lambda e, PT=PT, NN=NN: e.tensor_tensor(PT[:, 128:256], NN[:, 0:128], ident[:], ALU.add), reads=[NN, ident], writes=[PT])
            PkT_t, PkT = TR, TR[:, 0:128]
            nsteps = 7
            for kstep in range(nsteps - 1):
                last = (kstep == nsteps - 2)
                pA = pbig.next()
                if kstep == 0:
                    mm(pA, pA[:, 0:128], PkT_t, PkT, PT, PT[:, 0:128])
                elif not last:
                    mm(pA, pA[:], PkT_t, PkT, PT, PT[:])
                else:
                    mm(pA, pA[:, 128:256], PkT_t, PkT, PT, PT[:, 128:256])
                PTn = R['big'].next()
                if not last:
                    pB = psm.next()
                    mm(pB, pB[:], PT, PT[:, 0:128], PkT_t, PkT)
                    PkTn = R['small'].next()
                    op('act', lambda e, PkTn=PkTn, pB=pB: e.copy(PkTn[:], pB[:]), reads=[pB], writes=[PkTn])
                    op('act', lambda e, PTn=PTn, pA=pA: e.copy(PTn[:, 0:128], pA[:, 0:128]), reads=[pA], writes=[PTn])
                if kstep == 0:
                    op('dve', lambda e, PTn=PTn, PT=PT: e.tensor_copy(PTn[:, 128:256], PT[:, 128:256]), reads=[PT], writes=[PTn])
                else:
                    op('dve', lambda e, PTn=PTn, PT=PT, pA=pA: e.tensor_tensor(PTn[:, 128:256], pA[:, 128:256], PT[:, 128:256], ALU.add),
                       reads=[pA, PT], writes=[PTn])
                PT = PTn
                if not last:
                    PkT_t, PkT = PkTn, PkTn[:]
            yield
            Tt, Tm = PT, PT[:, 128:256]
            p4 = pbig.next()
            mm(p4, p4[:, 0:128], Tt, Tm, TR, TR[:, 128:256])
            mm(p4, p4[:, 128:256], Tt, Tm, TM4, BHm)
            GZ = R['big'].next()
            op('act', lambda e, GZ=GZ, p4=p4: e.copy(GZ[:], p4[:]), reads=[p4], writes=[GZ])
            yield
            p5 = pbig.next()
            mm(p5, p5[:], NN, NN[:, 128:256], GZ, GZ[:])
            HX = R['big'].next()
            op('dve', lambda e, HX=HX, p5=p5, QK=QK: e.tensor_tensor(HX[:, 0:128], p5[:, 0:128], QK[:], ALU.add), reads=[p5, QK], writes=[HX])
            op('dve', lambda e, HX=HX, p5=p5, TM4=TM4, KHm=KHm: e.tensor_tensor(HX[:, 128:256], p5[:, 128:256], KHm, ALU.add), reads=[p5, TM4], writes=[HX])
            yield
            DW = R['small'].next()
            op('dve', lambda e, DW=DW, c=c: e.tensor_scalar(DW[:], ident[:], ECL[:, (c + 1) * LCH - 1:(c + 1) * LCH], None, ALU.mult), reads=[ident, ECL], writes=[DW])
            p6 = pbig.next()
            mm(p6, p6[:], TM4, ATm, GZ, GZ[:])
            REM = R['big'].next()
            op('dve', lambda e, REM=REM, p6=p6, AXT=AXT: e.tensor_tensor(REM[:, 0:128], p6[:, 0:128], AXT[:, 128:256], ALU.add), reads=[p6, AXT], writes=[REM])
            op('dve', lambda e, REM=REM, p6=p6, DW=DW: e.tensor_tensor(REM[:, 128:256], p6[:, 128:256], DW[:], ALU.add), reads=[p6, DW], writes=[REM])
            yield
            Sc = S2[j][si[j] % 2]
            Sn = S2[j][(si[j] + 1) % 2]
            si[j] += 1
            pyc = psm.next()
            mm(pyc, pyc[:], TM4, Vm, HX, HX[:, 0:128], start=True, stop=False)
            mm(pyc, pyc[:], Sc, Sc[:], REM, REM[:, 0:128], start=False, stop=True)
            op('act', lambda e, pyc=pyc, cs=cs: e.copy(YV[:, cs], pyc[:, 0:64]), reads=[pyc], writes=[YV])
            op('dve', lambda e, pyc=pyc, cs=cs: e.tensor_tensor(YV[:, cs], YV[:, cs], pyc[:, 64:128], ALU.add), reads=[pyc, YV], writes=[YV])
            p7 = psm.next()
            mm(p7, p7[:], REM, REM[:, 128:256], Sc, Sc[:], start=True, stop=False)
            mm(p7, p7[:], HX, HX[:, 128:256], TM4, Vm, start=False, stop=True)
            op('act', lambda e, Sn=Sn, p7=p7: e.copy(Sn[:], p7[:]), reads=[p7], writes=[Sn])

        def gn(j, YV):
            yv = YV
            pm_ = pbig.next()
            mm(pm_, pm_[:], bones, bones[:], yv, yv[:])
            yc = tmp.next()
            op('dve', lambda e: e.scalar_tensor_tensor(yc[:], pm_[:], -1.0 / 64, yv[:], ALU.mult, ALU.add), reads=[pm_, yv], writes=[yc])
            y2 = yv
            op('pool', lambda e: e.tensor_tensor(y2[:], yc[:], yc[:], ALU.mult), reads=[yc], writes=[y2])
            pv_ = pbig.next()
            mm(pv_, pv_[:], bones, bones[:], y2, y2[:])
            op('act', lambda e: e.activation(y2[:], pv_[:], AF.Sqrt, bias=cv[:, 2:3], scale=1.0 / 64), reads=[pv_, cv], writes=[y2])
            op('dve', lambda e: e.reciprocal(y2[:], y2[:]), reads=[y2], writes=[y2])
            op('dve', lambda e: e.tensor_tensor(yc[:], yc[:], y2[:], ALU.mult), reads=[yc, y2], writes=[yc])
            op('dve', lambda e: e.tensor_scalar(yc[:], yc[:], vec[:, V_LNW + j:V_LNW + j + 1], vec[:, V_LNB + j:V_LNB + j + 1], ALU.mult, ALU.add),
               reads=[yc, vec], writes=[yc])
            op('dve', lambda e: e.tensor_tensor(yc[:], yc[:], BV[:], ALU.add), reads=[yc, BV], writes=[yc])
            sgt = y2
            pgt = pbig.next()
            for k in range(KC):
                mm(pgt, pgt[:], W, W[:, 4, k, j * 128:(j + 1) * 128], B, B[:, k, 2:TE + 2], start=(k == 0), stop=(k == KC - 1))
            op('act', lambda e: e.activation(sgt[:], pgt[:], AF.Silu), reads=[pgt], writes=[sgt])
            op('dve', lambda e: e.tensor_tensor(Yt[:, j, :], yc[:], sgt[:], ALU.mult), reads=[yc, sgt], writes=[Yt])


        def attn_gen():
            for jj in range(2):
                proj(5, jj * 128, 128, QR, QR[:, jj, :], 'act' if jj == 0 else 'dve')
            for h in range(4):
                hs_ = slice((h % 2) * 64, (h % 2) * 64 + 64)
                pq = pbig.next()
                mm(pq, pq[0:64, :], rm, rm[hs_, hs_], QR, QR[hs_, h // 2, :])
                pq2 = pbig.next()
                mm(pq2, pq2[0:64, :], ident, ident[hs_, hs_], QR, QR[hs_, h // 2, :])
                t_b = atmp.next()
                op('dve', lambda e, pq=pq, t_b=t_b: e.tensor_tensor(t_b[0:64, :], pq[0:64, :], sinq[:], ALU.mult), reads=[pq, sinq], writes=[t_b])
                op('dve', lambda e, h=h, pq2=pq2: e.tensor_tensor(Q4[:, h, :], pq2[0:64, :], cosq[:], ALU.mult), reads=[pq2, cosq], writes=[Q4])
                op('dve', lambda e, h=h, t_b=t_b: e.tensor_tensor(Q4[:, h, :], Q4[:, h, :], t_b[0:64, :], ALU.add), reads=[Q4, t_b], writes=[Q4])
                yield

            if dbg <= 4:
                op('pool', lambda e: e.memset(Yt[:, 2:4, :], 0.0), writes=[Yt])
            for bq in range(TE // 128 if dbg > 4 else 0):
                blk_first = first_tile and bq == 0
                pvt = psm.next()
                mm(pvt, pvt[:], KV, KV[:, 128 + bq * 128:128 + (bq + 1) * 128], ident, ident[:])
                vt_cur = Vtm.next()
                op('act', lambda e, vt_cur=vt_cur, pvt=pvt: e.copy(vt_cur[:], pvt[:]), reads=[pvt], writes=[vt_cur])
                if vstate['p'] is None:
                    vstate['p'] = vt_cur
                vtm_prev = vstate['p']
                po = po_ring.next()
                for h in range(4):
                    pssc = pbig.next()
                    mm(pssc, pssc[:], Q4, Q4[:, h, bq * 128:(bq + 1) * 128], KV, KV[0:64, bq * 128:bq * 128 + 256])
                    sc = atmp.next()
                    op('dve', lambda e, sc=sc, pssc=pssc: e.scalar_tensor_tensor(sc[:], pssc[:], scale, mb[:, 1 if blk_first else 0, :], ALU.mult, ALU.add),
                       reads=[pssc, mb], writes=[sc])
                    op('dve', lambda e, sc=sc: e.reduce_max(st[:, 0:1], sc[:], axis=AX.X), reads=[sc], writes=[st])
                    op('dve', lambda e, h=h: e.tensor_scalar(st[:, 1:2], st[:, 0:1], sinkb[:, h:h + 1], -1.0, ALU.max, ALU.mult), reads=[st, sinkb], writes=[st])
                    op('dve', lambda e: e.memset(st[:, 2:3], 0.0), writes=[st])
                    pe_ = atmp.next()
                    op('act', lambda e, sc=sc, pe_=pe_: e.activation(pe_[:], sc[:], AF.Exp, bias=st[:, 1:2], accum_out=st[:, 2:3]), reads=[sc, st], writes=[pe_, st])
                    op('act', lambda e, h=h: e.activation(st[:, 3:4], sinkb[:, h:h + 1], AF.Exp, bias=st[:, 1:2]), reads=[sinkb, st], writes=[st])
                    op('dve', lambda e: e.tensor_tensor(st[:, 4:5], st[:, 2:3], st[:, 3:4], ALU.add), reads=[st], writes=[st])
                    op('dve', lambda e: e.reciprocal(st[:, 5:6], st[:, 4:5]), reads=[st], writes=[st])
                    op('dve', lambda e, pe_=pe_: e.tensor_scalar(pe_[:], pe_[:], st[:, 5:6], None, ALU.mult), reads=[pe_, st], writes=[pe_])
                    yield
                    pts = []
                    for half in range(2):
                        ptp = psm.next()
                        mm(ptp, ptp[:], pe_, pe_[:, half * 128:(half + 1) * 128], ident, ident[:])
                        pT = asmall.next()
                        op('act' if half == 0 else 'dve',
                           (lambda e, pT=pT, ptp=ptp: e.copy(pT[:], ptp[:])) if half == 0 else (lambda e, pT=pT, ptp=ptp: e.tensor_copy(pT[:], ptp[:])),
                           reads=[ptp], writes=[pT])
                        pts.append(pT)
                    osl = po[(h % 2) * 64:(h % 2) * 64 + 64, (h // 2) * 128:(h // 2) * 128 + 128]
                    mm(po, osl, vtm_prev, vtm_prev[:, 64:128], pts[0], pts[0][:], start=True, stop=False)
                    mm(po, osl, vt_cur, vt_cur[:, 64:128], pts[1], pts[1][:], start=False, stop=True)
                    yield
                vstate['p'] = vt_cur
                yield
                for jj in range(2):
                    sgt = atmp.next()
                    pgb = psm.next()
                    for k in range(KC):
                        mm(pgb, pgb[:], W, W[:, 6, k, jj * 128:(jj + 1) * 128], B, B[:, k, 2 + bq * 128:2 + (bq + 1) * 128], start=(k == 0), stop=(k == KC - 1))
                    op('act', lambda e, sgt=sgt, pgb=pgb: e.activation(sgt[:, 0:128], pgb[:], AF.Silu), reads=[pgb], writes=[sgt])
                    op('dve', lambda e, sgt=sgt, jj=jj, bq=bq, po=po: e.tensor_tensor(Yt[:, 2 + jj, bq * 128:(bq + 1) * 128], po[:, jj * 128:(jj + 1) * 128], sgt[:, 0:128], ALU.mult),
                       reads=[po, sgt], writes=[Yt])
            yield

        def step(g):
            try:
                next(g)
                return True
            except StopIteration:
                return False

        ag = attn_gen() if dbg > 4 else iter(())
        ag_alive = True
        for j in range(2):
            prep(j)
            YV = yvs[j]
            if dbg < 4:
                op('dve', lambda e, YV=YV: e.tensor_copy(YV[:], BV[:]), reads=[BV], writes=[YV])
            for c0 in range(0, NCH if dbg > 3 else 0, 2):
                gens = [chunk_gen(j, c0, RS_[0], YV), chunk_gen(j, c0 + 1, RS_[1], YV)]
                while gens:
                    gens = [g for g in gens if step(g)]
                    if ag_alive:
                        ag_alive = step(ag)
            gn(j, YV)
        while ag_alive:
            ag_alive = step(ag)

        for n in ('r', 'k', 'v'):
            zt = Z[n]
            op('pool', lambda e, zt=zt: e.tensor_copy(zt[:, :, 0:1], zt[:, :, TE:TE + 1]), reads=[zt], writes=[zt])

        op('pool', lambda e: e.tensor_copy(KV[:, 0:128], KV[:, TE:TE + 128]), reads=[KV], writes=[KV])
        if layer2:
            op('pool', lambda e: e.tensor_copy(B[:, :, 1:2], B[:, :, TE + 1:TE + 2]), reads=[B], writes=[B])
        ev = dma('sp', y_dst.ap.rearrange("(j p) t -> p j t", p=128)[:, :, t0:t0 + TE], Yt[:], reads=[Yt], writes=[y_dst], track=Yt)
        cx.out_events.append(ev)
    cx.release(mark)


A_W = 1024
SHIFT_W = 3200
C_R, C_K, C_V, C_XW, C_AG, C_Q, C_KB, C_VB, C_BG = 0, 1024, 2048, 3072, 3200, 4224, 5248, 5504, 5760


def make_consts_even():
    c = make_consts()
    i = np.arange(128)
    c['bones'] = ((i[:, None] // 64) == (i[None, :] // 64)).astype(np.float32)
    c['idb2'] = ((i[:, None] % 64) == np.arange(64)[None, :]).astype(np.float32)
    t = np.arange(64)
    SL = (t[:, None] > t[None, :]).astype(np.float32)
    SU = SL.T.copy()
    UI = (t[None, :] >= t[:, None]).astype(np.float32)
    em = np.zeros((128, 3, 128), np.float32)
    for hh in range(2):
        bs_ = slice(hh * 64, (hh + 1) * 64)
        em[bs_, 0, bs_] = SL
        em[bs_, 1, bs_] = SU
        em[bs_, 2, bs_] = UI
    c['bm2'] = np.ascontiguousarray(np.repeat(((i[:, None] // 64) == np.arange(2)[None, :]).astype(np.float32)[:, :, None], 64, axis=2))
    c['emsk'] = em
    q = np.arange(128)
    NEG = -30000.0
    am = np.full((128, 2, 256), NEG, np.float32)
    prev_ok = q[None, :] > q[:, None]
    cur_ok = q[None, :] <= q[:, None]
    am[:, 0, 0:128] = np.where(prev_ok, 0.0, NEG)
    am[:, 0, 128:256] = np.where(cur_ok, 0.0, NEG)
    am[:, 1, 128:256] = np.where(cur_ok, 0.0, NEG)
    c['amask'] = am
    rmat = np.zeros((128, 128), np.float32)
    for p in range(128):
        d = p % 64
        if d < 8:
            rmat[p + 8, p] = -1.0
        elif d < 16:
            rmat[p - 8, p] = 1.0
    c['ropem'] = rmat
    rst = np.ones((128, TE), np.float32)
    rst[:, ::LCH] = 0.0
    c['rst'] = rst
    cvec = np.zeros((128, 8), np.float32)
    cvec[:, 0] = RMS_EPS
    cvec[:, 1] = 1e-12
    cvec[:, 2] = GN_EPS
    cvec[:, 3] = math.pi / 2
    inv = np.power(np.float32(500000.0), -np.arange(8, dtype=np.float32) / np.float32(8)).astype(np.float32)
    fq = np.zeros(128, np.float32)
    for p in range(128):
        d = p % 64
        if d < 16:
            fq[p] = inv[d % 8]
    fk = fq.copy()
    fk[64:] = 0.0
    cvec[:, 4] = fq
    cvec[:, 5] = fk
    c['cvec'] = cvec
    return c


def prep_even(e, hg, P):
    W = P['e_w_in'][e]
    hs = slice(hg * 256, (hg + 1) * 256)

    def cols(base):
        return W[:, base + hg * 256: base + (hg + 1) * 256]
    kvb = np.concatenate([W[:, C_KB + hg * 64: C_KB + (hg + 1) * 64], W[:, C_VB + hg * 64: C_VB + (hg + 1) * 64]], axis=1)
    wcols = np.concatenate([W[:, C_XW:C_XW + 128], kvb, cols(C_R), cols(C_K), cols(C_V), cols(C_AG), cols(C_Q), cols(C_BG)], axis=1)
    mu = P['e_mu'][e]
    vec = np.zeros((128, 40), np.float32)

    def put(col, v256):
        vec[:, col:col + 2] = v256.reshape(2, 128).T
    put(0, mu[C_R:C_R + 1024][hs]); put(2, mu[C_K:C_K + 1024][hs]); put(4, mu[C_V:C_V + 1024][hs])
    put(6, P['rwkv_w0'][e][hs]); put(8, P['rwkv_a0'][e][hs]); put(10, P['rwkv_k_k'][e][hs]); put(12, P['rwkv_k_a'][e][hs])
    put(16, P['rwkv_r_k'][e].reshape(-1)[hs]); put(18, P['rwkv_ln_w'][e][hs]); put(20, P['rwkv_ln_b'][e][hs])
    if e > 0:
        put(22, P['rwkv_v0'][e - 1][hs])
    vec[:, 24] = mu[C_XW:C_XW + 128]
    w2a2 = np.concatenate([P['rwkv_w2'][e][:, hs], P['rwkv_a2'][e][:, hs]], axis=0)
    out = {
        'norm': vec_pk(P['e_norm'][e]),
        'w_in': tile_w(wcols),
        'vec': vec,
        'w2a2': np.ascontiguousarray(w2a2, dtype=np.float32),
        'sinks': np.ascontiguousarray(P['attn_sinks'][e][hg * 4:(hg + 1) * 4], dtype=np.float32),
    }
    if e > 0:
        out['wvT'] = np.ascontiguousarray(W[:, C_V:C_V + 1024].T, dtype=np.float32)
        out['v1'] = np.ascontiguousarray(P['rwkv_v1'][e - 1].reshape(8, 128, 32).transpose(1, 0, 2), dtype=np.float32)
        out['muv'] = vec_pk(mu[C_V:C_V + 1024])
        out['v2'] = np.ascontiguousarray(P['rwkv_v2'][e - 1][:, hs], dtype=np.float32)
    return out


SEQ = 8192
_PROG_CACHE = {}


def emit_outproj_stage(cm, w_out, y_src, x_src, x_dst):
    cx, TT = cm.cx, cm.TT
    for t0 in range(0, NTOK, TT):
        cx.dma('pool', cm.Y[:], y_src.ap.rearrange("(k p) t -> p k t", p=128)[:, :, t0:t0 + TT], reads=[y_src], writes=[cm.Y], track=cm.Y)
        cm.out_proj_residual(w_out, x_src, x_dst, t0)


def build_E(e, in_shapes):
    nc = bass.Bass("TRN2", target_bir_lowering=False)
    aps = {k: declare(nc, k, v) for k, v in in_shapes.items()}
    yo = nc.dram_tensor("y", [512, SEQ], F32, kind="ExternalOutput").ap()
    cx = Ctx(nc)
    PP = {k[2:]: aps[k] for k in aps if k.startswith('p_')}
    CC = {k[2:]: aps[k] for k in aps if k.startswith('c_')}
    yd = DT(yo)
    if e == 0:
        vfo = nc.dram_tensor("vf", [256, SEQ], F32, kind="ExternalOutput").ap()
        vd = DT(vfo)
        emit_even_phase(cx, PP, CC, DT(aps['xT']), yd, vd, None, SEQ, False)
        for key, v in vd.wd.items():
            cx.out_events.append((key, v))
    else:
        emit_even_phase(cx, PP, CC, DT(aps['xT']), yd, None, DT(aps['vfin']), SEQ, True)
    cx.finish()
    return nc


def build_O(final, in_shapes):
    nc = bass.Bass("TRN2", target_bir_lowering=False)
    aps = {k: declare(nc, k, v) for k, v in in_shapes.items()}
    xo = nc.dram_tensor("xo", [D, NTOK], F32, kind="ExternalOutput").ap()
    xm = nc.dram_tensor("xmid", [D, NTOK], F32, kind="Internal").ap()
    cx = Ctx(nc)
    CC = {k[2:]: aps[k] for k in aps if k.startswith('c_')}
    cm = Common(cx, CC, 512)
    PP = {k[2:]: aps[k] for k in aps if k.startswith('p_')}
    x_in, y_in, x_mid, x_out = DT(aps['xT']), DT(aps['yT']), DT(xm), DT(xo)
    emit_outproj_stage(cm, aps['wout_e'], y_in, x_in, x_mid)
    if not final:
        emit_odd_layer(cm, PP, x_mid, x_out)
        for key, v in x_out.wd.items():
            cx.out_events.append((key, v))
    else:
        xm2 = nc.dram_tensor("xmid2", [D, NTOK], F32, kind="Internal").ap()
        x_mid2 = DT(xm2)
        emit_odd_layer(cm, PP, x_mid, x_mid2)
        fn = cx.sb([128, KC], F32)
        cx.dma('sp', fn[:], aps['fnorm'], writes=[fn], track=fn)
        emit_final_norm(cm, x_mid2, x_out, fn)
    cx.finish()
    return nc


def _shapes(d):
    return {k: v for k, v in d.items()}


def kernel(**inputs):
    P = {k: np.asarray(v) for k, v in inputs.items()}
    x = P['x'].astype(np.float32, copy=False)
    pos = P['positions'].astype(np.int32, copy=False)
    B_ = x.shape[0]
    cores = list(range(8))
    ce = make_consts_even()
    co = make_consts()
    xT_b = [np.ascontiguousarray(x[b].T) for b in range(B_)]
    vf_cores = None
    for e in range(2):
        in_maps = []
        for c in cores:
            b, hg = c // 4, c % 4
            m = {'xT': xT_b[b]}
            for k, v in ce.items():
                m['c_' + k] = v
            for k, v in prep_even(e, hg, P).items():
                m['p_' + k] = v
            m['p_pos'] = np.ascontiguousarray(pos[b])
            if e > 0:
                m['vfin'] = vf_cores[c]
            in_maps.append(m)
        nc = build_E(e, in_maps[0])
        res = run_bass_kernel_spmd(nc, in_maps, core_ids=cores)
        if e == 0:
            vf_cores = [np.ascontiguousarray(res.results[c]['vf']) for c in cores]
        yT_b = []
        for b in range(B_):
            yT = np.empty((2048, SEQ), np.float32)
            for hg in range(4):
                r = res.results[b * 4 + hg]['y']
                yT[hg * 256:(hg + 1) * 256] = r[0:256]
                yT[1024 + hg * 256:1024 + (hg + 1) * 256] = r[256:512]
            yT_b.append(yT)
        del res
        final = (e == 1)
        in_maps = []
        podd = prep_odd(P['o_norm'][e], P['o_w_in'][e], P['sgu_ln_w'][e], P['sgu_ln_b'][e], P['sgu_ws'][e], P['sgu_bs'][e], P['o_w_out'][e])
        wout_e = tile_w(P['e_w_out'][e])
        for c in cores:
            b, sg = c // 4, c % 4
            sl = slice(sg * NTOK, (sg + 1) * NTOK)
            m = {'xT': np.ascontiguousarray(xT_b[b][:, sl]), 'yT': np.ascontiguousarray(yT_b[b][:, sl]), 'wout_e': wout_e}
            for k, v in co.items():
                m['c_' + k] = v
            for k, v in podd.items():
                m['p_' + k] = v
            if final:
                m['fnorm'] = vec_pk(P['final_norm'])
            in_maps.append(m)
        nc = build_O(final, in_maps[0])
        res = run_bass_kernel_spmd(nc, in_maps, core_ids=cores)
        for b in range(B_):
            xT_b[b] = np.concatenate([res.results[b * 4 + sg]['xo'] for sg in range(4)], axis=1)
        del res
    out = np.stack([np.ascontiguousarray(xT_b[b].T) for b in range(B_)], axis=0)
    return out.astype(np.float32, copy=False)
```

```python
import math
import numpy as np
import concourse.bass as bass
import concourse.mybir as mybir
from concourse.bass_utils import run_bass_kernel_spmd

F32 = mybir.dt.float32
F32R = mybir.dt.float32r
I32 = mybir.dt.int32
AF = mybir.ActivationFunctionType
ALU = mybir.AluOpType
AX = mybir.AxisListType

D = 2048
NTOK = 2048
KC = 16
RMS_EPS = 1e-5
LN_EPS = 1e-5


class T:
    def __init__(self, t):
        self.t = t
        self.w = None
        self.r = {}
        self.dsem = None
        self.tr = self

    def __getitem__(self, k):
        return self.t[k]


class DT(T):
    def __init__(self, ap):
        super().__init__(None)
        self.ap = ap
        self.wd = {}


class Ctx:
    def __init__(self, nc):
        self.nc = nc
        self.eng = {'pe': nc.tensor, 'act': nc.scalar, 'dve': nc.vector, 'pool': nc.gpsimd, 'sp': nc.sync}
        self.sems = {}
        self.cnt = {}
        for k in ('pe', 'act', 'dve', 'pool'):
            self.sems[k] = nc.semaphore('s_' + k).__enter__()
            self.cnt[k] = 0
        self.seen = {}
        self.nsb = 0
        self.nps = 0
        self.ndram = 0
        self.out_events = []
        self.stack = []

    def sb(self, shape, dt=F32, name=None):
        self.nsb += 1
        cmgr = self.nc.sbuf_tensor(name or f"sb{self.nsb}", list(shape), dt)
        t = T(cmgr.__enter__())
        self.stack.append(cmgr)
        return t

    def mark(self):
        return len(self.stack)

    def release(self, mark):
        self.barrier()
        while len(self.stack) > mark:
            self.stack.pop().__exit__(None, None, None)

    def barrier(self):
        for e in ('pe', 'act', 'dve', 'pool', 'sp'):
            for k, v in self.cnt.items():
                if v > 0 and k != e and self.seen.get((e, k), 0) < v:
                    self.eng[e].wait_ge(self.sems[k], v)
                    self.seen[(e, k)] = v

    def ps(self, shape, dt=F32, name=None):
        self.nps += 1
        t = T(self.nc.psum_tensor(name or f"ps{self.nps}", list(shape), dt).__enter__())
        t.excl = True
        return t

    def ps_views(self, nbanks, width):
        banks = []
        for b in range(nbanks):
            self.nps += 1
            banks.append(T(self.nc.psum_tensor(f"pbank{self.nps}", [128, 512], F32).__enter__()))
            banks[-1].excl = True
        out = []
        for i in range(512 // width):
            for bk in banks:
                v = T(bk.t[:, i * width:(i + 1) * width])
                v.tr = bk
                out.append(v)
        return out

    def _dsem(self, t):
        if t.dsem is None:
            key = f"d{len(self.sems)}"
            self.sems[key] = self.nc.semaphore('s_' + key).__enter__()
            self.cnt[key] = 0
            t.dsem = key
        return t.dsem

    def _waits(self, e, reads, writes):
        need = {}
        reads = [t.tr for t in reads]
        writes = [t.tr for t in writes]
        for t in reads:
            if isinstance(t, DT):
                for k, v in t.wd.items():
                    need[k] = max(need.get(k, 0), v)
            elif t.w:
                need[t.w[0]] = max(need.get(t.w[0], 0), t.w[1])
            if getattr(t, 'excl', False):
                for k, v in t.r.items():
                    if k != e:
                        need[k] = max(need.get(k, 0), v)
        for t in writes:
            if t.w and not isinstance(t, DT):
                need[t.w[0]] = max(need.get(t.w[0], 0), t.w[1])
            for k, v in t.r.items():
                need[k] = max(need.get(k, 0), v)
        eng = self.eng[e]
        for k, v in need.items():
            if e == 'pe' and k == 'pe':
                continue
            if self.seen.get((e, k), 0) < v:
                eng.wait_ge(self.sems[k], v)
                self.seen[(e, k)] = v

    def op(self, e, fn, reads=(), writes=()):
        self._waits(e, reads, writes)
        ins = fn(self.eng[e])
        self.cnt[e] += 1
        ins.then_inc(self.sems[e], 1)
        ev = (e, self.cnt[e])
        for t in reads:
            t.tr.r[e] = ev[1]
        for t in writes:
            t.tr.w = ev
            t.tr.r = {}
        return ins

    def dma(self, q, out, in_, reads=(), writes=(), track=None):
        self._waits(q, reads, writes)
        t = track
        key = self._dsem(t)
        ins = self.eng[q].dma_start(out=out, in_=in_)
        self.cnt[key] += 16
        ins.then_inc(self.sems[key], 16)
        ev = (key, self.cnt[key])
        for r_ in reads:
            r_.r[key] = ev[1]
        for w_ in writes:
            if isinstance(w_, DT):
                w_.wd[key] = ev[1]
            else:
                w_.w = ev
                w_.r = {}
        return ev

    def collective(self, kind, in_ap, out_ap, groups, in_dt, out_dt):
        self._waits('pool', [in_dt], [out_dt])
        key = self._dsem(out_dt)
        ins = self.nc.gpsimd.collective_compute(kind, ALU.bypass, replica_groups=groups, ins=[in_ap], outs=[out_ap])
        self.cnt[key] += 16
        ins.then_inc(self.sems[key], 16)
        ev = (key, self.cnt[key])
        in_dt.r[key] = ev[1]
        out_dt.wd[key] = ev[1]
        return ev

    def finish(self):
        for key, v in self.out_events:
            self.eng['sp'].wait_ge(self.sems[key], v)


def r32(ap):
    return ap.bitcast(F32R)


def f32(ap):
    return ap.bitcast(F32)


class Common:
    def __init__(self, cx, consts, TT):
        nc = cx.nc
        self.cx = cx
        self.TT = TT
        self.ones = cx.sb([128, 128], F32R, "ones")
        cx.dma('pool', self.ones[:], consts['ones'], writes=[self.ones], track=self.ones)
        self.tri = cx.sb([128, 128], F32, "tri")
        cx.dma('sp', self.tri[:], consts['tri'], writes=[self.tri], track=self.tri)
        self.ident = cx.sb([128, 128], F32, "ident")
        cx.dma('sp', self.ident[:], consts['ident'], writes=[self.ident], track=self.ident)
        self.A = cx.sb([128, KC, TT], F32, "bufA")
        self.Y = cx.sb([128, KC, TT], F32R, "bufY")
        self.B = cx.sb([128, KC, TT], F32R, "bufB")
        self.wb = [cx.sb([128, KC, 256], F32R, f"wb{i}") for i in range(2)]
        self.wi = 0
        self.pbig = [cx.ps([128, 512], F32, f"pbig{i}") for i in range(4)]
        self.pi = 0
        self.sq = [cx.sb([128, TT], F32R, f"sq{i}") for i in range(1)]
        self.rstd = cx.sb([128, TT], F32, "rstd")
        self.xc = [cx.sb([128, TT], F32, f"xc{i}") for i in range(2)]
        self.xo = [cx.sb([128, TT], F32, f"xo{i}") for i in range(2)]
        self.xci = 0
        self.eps_rms = cx.sb([128, 1], F32, "eps_rms")
        cx.op('dve', lambda e: e.memset(self.eps_rms[:], RMS_EPS), writes=[self.eps_rms])

    def next_w(self):
        w = self.wb[self.wi % len(self.wb)]
        self.wi += 1
        return w

    def next_p(self):
        p = self.pbig[self.pi % len(self.pbig)]
        self.pi += 1
        return p

    def load_w(self, dram_group):
        w = self.next_w()
        self.cx.dma('pool', w[:], dram_group, writes=[w], track=w)
        return w

    def load_x_and_norm(self, x_src, t0, gvec):
        cx, TT = self.cx, self.TT
        A, B = self.A, self.B
        src = x_src.ap.rearrange("(k p) t -> p k t", p=128)[:, :, t0:t0 + TT]
        cx.dma('sp', A[:], src, reads=[x_src], writes=[A], track=A)
        pss = self.next_p()
        for k in range(KC):
            sq = self.sq[0]
            cx.op('act', lambda e, k=k, sq=sq: e.activation(sq[:], A[:, k, :], AF.Square), reads=[A], writes=[sq])
            cx.op('pe', lambda e, k=k, sq=sq: e.matmul(pss[:, :TT], self.ones[:], sq[:], start=(k == 0), stop=(k == KC - 1)),
                  reads=[sq, self.ones], writes=[pss])
        rstd = self.rstd
        cx.op('act', lambda e: e.activation(rstd[:], pss[:, :TT], AF.Sqrt, bias=self.eps_rms[:], scale=1.0 / D),
              reads=[pss, self.eps_rms], writes=[rstd])
        cx.op('dve', lambda e: e.reciprocal(rstd[:], rstd[:]), reads=[rstd], writes=[rstd])
        for k in range(KC):
            cx.op('dve', lambda e, k=k: e.scalar_tensor_tensor(B[:, k, :], A[:, k, :], gvec[:, k:k + 1], rstd[:],
                                                                ALU.mult, ALU.mult),
                  reads=[A, gvec, rstd], writes=[B])

    def out_proj_residual(self, wout_groups, x_src, x_dst, t0, is_output=False):
        cx, TT = self.cx, self.TT
        A = self.Y
        xs = x_src.ap.rearrange("(k p) t -> p k t", p=128)
        xd = x_dst.ap.rearrange("(k p) t -> p k t", p=128)
        wq = [self.load_w(wout_groups[0])]
        for g in range(8):
            w = wq.pop(0)
            if g + 1 < 8:
                wq.append(self.load_w(wout_groups[g + 1]))
            for j in range(2):
                dk = g * 2 + j
                ps = self.next_p()
                for k in range(KC):
                    cx.op('pe', lambda e, k=k, j=j, w=w, ps=ps: e.matmul(ps[:, :TT], w[:, k, j * 128:(j + 1) * 128],
                                                                         A[:, k, :], start=(k == 0), stop=(k == KC - 1)),
                          reads=[w, A], writes=[ps])
                xc = self.xc[self.xci % 2]
                xo = self.xo[self.xci % 2]
                self.xci += 1
                cx.dma('sp', xc[:], xs[:, dk, t0:t0 + TT], reads=[x_src], writes=[xc], track=xc)
                cx.op('dve', lambda e, xc=xc, xo=xo, ps=ps: e.tensor_tensor(xo[:], ps[:, :TT], xc[:], ALU.add),
                      reads=[ps, xc], writes=[xo])
                ev = cx.dma('sp', xd[:, dk, t0:t0 + TT], xo[:], reads=[xo], writes=[x_dst], track=xo)
                if is_output:
                    cx.out_events.append(ev)


def emit_final_norm(cm, x_src, out_dst, gvec):
    cx, TT = cm.cx, cm.TT
    od = out_dst.ap.rearrange("(k p) t -> p k t", p=128)
    for t0 in range(0, NTOK, TT):
        cm.load_x_and_norm(x_src, t0, gvec)
        A = cm.A
        for k in range(KC):
            cx.op('dve', lambda e, k=k: e.scalar_tensor_tensor(A[:, k, :], A[:, k, :], gvec[:, k:k + 1], cm.rstd[:],
                                                                ALU.mult, ALU.mult),
                  reads=[A, gvec, cm.rstd], writes=[A])
        ev = cx.dma('sp', od[:, :, t0:t0 + TT], A[:], reads=[A], writes=[out_dst], track=A)
        cx.out_events.append(ev)


def emit_odd_layer(cm, P, x_src, x_dst):
    cx, TT = cm.cx, cm.TT
    mark = cx.mark()
    nb = TT // 128
    g = cx.sb([128, KC], F32, None)
    cx.dma('sp', g[:], P['norm'], writes=[g], track=g)
    lnw = cx.sb([128, D], F32)
    lnb = cx.sb([128, D], F32)
    cx.dma('sp', lnw[:], P['ln_w'].partition_broadcast(128), writes=[lnw], track=lnw)
    cx.dma('sp', lnb[:], P['ln_b'].partition_broadcast(128), writes=[lnb], track=lnb)
    bsb2 = [cx.sb([128, 128], F32) for _ in range(2)]
    wm = cx.sb([128, 16, 128], F32R)
    wstage = cm.A[:, 0:D // TT, :].rearrange("p k t -> p (k t)").rearrange("p (g t) -> p g t", t=128)
    cx.dma('sp', wstage, P['wsT'], writes=[cm.A], track=cm.A)
    cx.op('pool', lambda e: e.tensor_tensor(wm[:], wstage, cm.tri[:].unsqueeze(1).to_broadcast([128, 16, 128]), ALU.mult),
          reads=[cm.A, cm.tri], writes=[wm])
    eps_ln = cx.sb([128, 1], F32)
    cx.op('dve', lambda e: e.memset(eps_ln[:], LN_EPS), writes=[eps_ln])
    vv = cx.sb([128, nb, D], F32R)
    st = cx.sb([128, 8], F32)
    sg = [cx.sb([128, TT], F32) for _ in range(1)]
    t1 = [cx.sb([128, TT], F32) for _ in range(2)]

    for t0 in range(0, NTOK, TT):
        cm.load_x_and_norm(x_src, t0, g)
        B = cm.B
        wq = [cm.load_w(P['w_in'][16])]
        for gi in range(8):
            w = wq.pop(0)
            if gi + 1 < 8:
                wq.append(cm.load_w(P['w_in'][16 + gi + 1]))
            for b in range(nb):
                ps = cm.next_p()
                for k in range(KC):
                    cx.op('pe', lambda e, k=k, b=b, w=w, ps=ps: e.matmul(ps[:, :256], B[:, k, b * 128:(b + 1) * 128], w[:, k, :],
                                                                         start=(k == 0), stop=(k == KC - 1)),
                          reads=[B, w], writes=[ps])
                cx.op('act', lambda e, b=b, gi=gi, ps=ps: e.copy(vv[:, b, gi * 256:(gi + 1) * 256], ps[:, :256]),
                      reads=[ps], writes=[vv])
        for b in range(nb):
            cx.op('dve', lambda e, b=b: e.reduce_sum(st[:, 0:1], f32(vv[:, b, :]), axis=AX.X), reads=[vv], writes=[st])
            cx.op('dve', lambda e: e.tensor_scalar(st[:, 1:2], st[:, 0:1], -1.0 / D, None, ALU.mult), reads=[st], writes=[st])
            cx.op('dve', lambda e, b=b: e.tensor_scalar(vv[:, b, :], f32(vv[:, b, :]), st[:, 1:2], None, ALU.add),
                  reads=[vv, st], writes=[vv])
            cx.op('dve', lambda e: e.memset(st[:, 2:3], 0.0), writes=[st])
            cx.op('act', lambda e, b=b: e.activation(cm.A[:, 0:D // TT, :].rearrange("p k t -> p (k t)"), f32(vv[:, b, :]), AF.Square,
                                                     accum_out=st[:, 2:3]),
                  reads=[vv], writes=[cm.A, st])
            cx.op('act', lambda e: e.activation(st[:, 3:4], st[:, 2:3], AF.Sqrt, bias=eps_ln[:], scale=1.0 / D),
                  reads=[st, eps_ln], writes=[st])
            cx.op('dve', lambda e: e.reciprocal(st[:, 4:5], st[:, 3:4]), reads=[st], writes=[st])
            cx.op('dve', lambda e, b=b: e.scalar_tensor_tensor(vv[:, b, :], f32(vv[:, b, :]), st[:, 4:5], lnw[:], ALU.mult, ALU.mult),
                  reads=[vv, st, lnw], writes=[vv])
            cx.op('dve', lambda e, b=b: e.tensor_tensor(vv[:, b, :], f32(vv[:, b, :]), lnb[:], ALU.add),
                  reads=[vv, lnb], writes=[vv])
        A = cm.Y
        wq = [cm.load_w(P['w_in'][0])]
        for gi in range(16):
            w = wq.pop(0)
            if gi + 1 < 16:
                wq.append(cm.load_w(P['w_in'][gi + 1]))
            pu = cm.next_p()
            pg = cm.next_p()
            for k in range(KC):
                cx.op('pe', lambda e, k=k, w=w, pu=pu: e.matmul(pu[:, :TT], w[:, k, 0:128], B[:, k, :], start=(k == 0), stop=(k == KC - 1)),
                      reads=[B, w], writes=[pu])
            for k in range(KC):
                cx.op('pe', lambda e, k=k, w=w, pg=pg: e.matmul(pg[:, :TT], w[:, k, 128:256], B[:, k, :], start=(k == 0), stop=(k == KC - 1)),
                      reads=[B, w], writes=[pg])
            pm = cm.next_p()
            for b in range(nb):
                cx.op('pe', lambda e, b=b, gi=gi, pm=pm: e.matmul(pm[:, b * 128:(b + 1) * 128], vv[:, b, gi * 128:(gi + 1) * 128], wm[:, gi, :],
                                                                  start=True, stop=True),
                      reads=[vv, wm], writes=[pm])
            sgt = sg[0]
            t1t = t1[gi % 2]
            bsb = bsb2[gi % 2]
            cx.dma('sp', bsb[:], P['bs'][gi * 128:(gi + 1) * 128].partition_broadcast(128), writes=[bsb], track=bsb)
            cx.op('act', lambda e, sgt=sgt, pg=pg: e.activation(sgt[:], pg[:, :TT], AF.Silu), reads=[pg], writes=[sgt])
            cx.op('dve', lambda e, sgt=sgt, pu=pu: e.tensor_tensor(sgt[:], pu[:, :TT], sgt[:], ALU.mult), reads=[pu, sgt], writes=[sgt])
            cx.op('dve', lambda e, t1t=t1t, pm=pm, bsb=bsb: e.tensor_tensor(
                t1t[:].rearrange("p (b t) -> p b t", t=128), pm[:, :TT].rearrange("p (b t) -> p b t", t=128),
                bsb[:].unsqueeze(1).to_broadcast([128, nb, 128]), ALU.add), reads=[pm, bsb], writes=[t1t])
            cx.op('dve', lambda e, t1t=t1t, sgt=sgt, gi=gi: e.tensor_tensor(A[:, gi, :], t1t[:], sgt[:], ALU.mult),
                  reads=[t1t, sgt], writes=[A])
        cm.out_proj_residual(P['w_out'], x_src, x_dst, t0)
    cx.release(mark)


def tile_w(Wcols):
    n = Wcols.shape[1] // 256
    a = Wcols.reshape(KC, 128, n, 256).transpose(2, 1, 0, 3)
    return np.ascontiguousarray(a, dtype=np.float32)


def vec_pk(v):
    return np.ascontiguousarray(v.reshape(-1, 128).T, dtype=np.float32)


def prep_odd(o_norm, o_w_in, ln_w, ln_b, ws, bs, w_out):
    u = o_w_in[:, 0:2048].reshape(2048, 16, 128)
    gt = o_w_in[:, 4096:6144].reshape(2048, 16, 128)
    ug = np.concatenate([u, gt], axis=2).reshape(2048, 16 * 256)
    wcols = np.concatenate([ug, o_w_in[:, 2048:4096]], axis=1)
    return {
        'norm': vec_pk(o_norm),
        'w_in': tile_w(wcols),
        'ln_w': np.ascontiguousarray(ln_w, dtype=np.float32),
        'ln_b': np.ascontiguousarray(ln_b, dtype=np.float32),
        'wsT': np.ascontiguousarray(ws.transpose(2, 0, 1), dtype=np.float32),
        'bs': np.ascontiguousarray(bs.reshape(-1), dtype=np.float32),
        'w_out': tile_w(w_out),
    }


def make_consts():
    s = np.arange(128)
    return {
        'ones': np.ones((128, 128), np.float32),
        'tri': (s[None, :] >= s[:, None]).astype(np.float32),
        'ident': np.eye(128, dtype=np.float32),
    }


def declare(nc, name, arr, kind="ExternalInput"):
    dt = I32 if arr.dtype == np.int32 else F32
    return nc.dram_tensor(name, list(arr.shape), dt, kind=kind).ap()


TE = 256
NIL = 2
LCH = 64
NCH = TE // LCH
GN_EPS = 64 * 1e-5
TWO_PI = 2.0 * math.pi


class Ring:
    def __init__(self, tiles):
        self.t = tiles
        self.i = 0

    def next(self):
        t = self.t[self.i % len(self.t)]
        self.i += 1
        return t


def emit_even_phase(cx, P, consts, x_src, y_dst, vf_dst, vf_src, ntok, layer2, dbg=9):
    nc = cx.nc
    mark = cx.mark()
    sb, ps, op, dma = cx.sb, cx.ps, cx.op, cx.dma
    ones = sb([128, 128], F32R, "e_ones")
    dma('pool', ones[:], consts['ones'], writes=[ones], track=ones)
    ident = sb([128, 128], F32, "e_ident")
    dma('sp', ident[:], consts['ident'], writes=[ident], track=ident)
    bones = sb([128, 128], F32, "e_bones")
    dma('sp', bones[:], consts['bones'], writes=[bones], track=bones)
    msk = sb([128, 3, 128], F32, "e_msk")
    dma('sp', msk[:], consts['emsk'], writes=[msk], track=msk)
    mb = sb([128, 2, 256], F32, "e_mb")
    dma('sp', mb[:], consts['amask'], writes=[mb], track=mb)
    rm = sb([128, 128], F32, "e_rm")
    dma('sp', rm[:], consts['ropem'], writes=[rm], track=rm)
    rst = sb([128, TE], F32, "e_rst")
    dma('sp', rst[:], consts['rst'], writes=[rst], track=rst)
    cv = sb([128, 8], F32, "e_cv")
    dma('sp', cv[:], consts['cvec'], writes=[cv], track=cv)
    g = sb([128, KC], F32, "e_g")
    dma('sp', g[:], P['norm'], writes=[g], track=g)
    vec = sb([128, 40], F32, "e_vec")
    dma('sp', vec[:], P['vec'], writes=[vec], track=vec)
    op('dve', lambda e: e.tensor_scalar(vec[:, 14:16], vec[:, 12:14], -1.0, 1.0, ALU.mult, ALU.add), reads=[vec], writes=[vec])
    w2a2 = sb([128, 256], F32, "e_w2a2")
    dma('sp', w2a2[:], P['w2a2'], writes=[w2a2], track=w2a2)
    sinkb = sb([128, 4], F32, "e_sink")
    dma('sp', sinkb[:], P['sinks'].partition_broadcast(128), writes=[sinkb], track=sinkb)
    W = sb([128, 7, KC, 256], mybir.dt.bfloat16, "e_W")
    for gi in range(7):
        dma('pool', W[:, gi], P['w_in'][gi], writes=[W], track=W)
    V_MUR, V_MUK, V_MUV, V_W0, V_A0, V_KK, V_KA, V_OMKA, V_RK, V_LNW, V_LNB, V_V0, V_MUX = 0, 2, 4, 6, 8, 10, 12, 14, 16, 18, 20, 22, 24

    xr = Ring([sb([128, TE], F32, f"e_xr{i}") for i in range(2)])
    B = sb([128, KC, TE + 2], mybir.dt.bfloat16, "e_B")
    sq = sb([128, TE], F32R, "e_sq")
    Z = {n: sb([128, 2, TE + 1], F32, "e_z" + n) for n in ('r', 'k', 'v')}
    Zx = sb([128, TE + 1], F32, "e_zx")
    for zt in list(Z.values()) + [Zx]:
        op('pool', lambda e, zt=zt: e.memset(zt[:], 0.0), writes=[zt])
    KV = sb([128, 128 + TE], F32, "e_kv")
    op('pool', lambda e: e.memset(KV[:], 0.0), writes=[KV])
    Yt = sb([128, 4, TE], F32, "e_y")
    S2 = [[sb([128, 128], F32, f"e_S{h}_{i}") for i in range(2)] for h in range(2)]
    for h in range(2):
        op('pool', lambda e, h=h: e.memset(S2[h][0][:], 0.0), writes=[S2[h][0]])
    si = [0, 0]
    bm2 = sb([128, 2, 64], F32, "e_bm2")
    dma('sp', bm2[:], consts['bm2'], writes=[bm2], track=bm2)
    tmp = Ring([sb([128, TE], F32, f"e_t{i}") for i in range(7)])
    ACM = sb([128, 2, TE], F32, "e_acm")
    RW4 = sb([128, 4, TE], F32, "e_rw4")

    def view(ap):
        v = T(ap)
        v.tr = RW4
        return v
    BCM = view(RW4.t[:, 0:2, :])
    BH = view(RW4.t[:, 2, :])
    KH = view(RW4.t[:, 3, :])
    Q4 = sb([64, 4, TE], F32, "e_q4")
    QR = sb([128, 2, TE], F32, "e_qr")
    VS = sb([128, 2, TE], F32, "e_vs")
    ECL = sb([128, TE], F32, "e_ecl")
    CL = sb([128, TE], F32, "e_cl")
    LW = sb([128, TE], F32, "e_lw")
    AA = sb([128, TE], F32, "e_aa")
    KKN = sb([128, TE], F32, "e_kkn")
    KF = sb([128, TE], F32, "e_kf")
    BV = sb([128, TE], F32, "e_bv")
    RSs = [sb([128, TE], F32, f"e_rs{i}") for i in range(2)]
    KSs = [sb([128, TE], F32, f"e_ks{i}") for i in range(2)]
    XS = sb([128, TE], F32, "e_xs")
    TXW = XS
    cosq = sb([64, TE], F32, "e_cos")
    sinq = sb([64, TE], F32, "e_sin")
    RS_ = [dict(small=Ring([sb([128, 128], F32, f"e_s{r}_{i}") for i in range(6)]),
                big=Ring([sb([128, 256], F32, f"e_b{r}_{i}") for i in range(5)]),
                keep=Ring([sb([128, 256], F32, f"e_k{r}_{i}") for i in range(4)]),
                tm4=Ring([sb([128, 512], F32, f"e_tm{r}_{i}") for i in range(1)])) for r in range(NIL)]
    atmp = Ring([sb([128, TE], F32, f"e_at{i}") for i in range(4)])
    asmall = Ring([sb([128, 128], F32, f"e_as{i}") for i in range(4)])
    yvs = [sb([128, TE], F32, f"e_yv{i}") for i in range(2)]
    vstate = {'p': None}
    st = sb([128, 8], F32, "e_st")
    Vtm = Ring([sb([128, 128], F32, f"e_vtm{i}") for i in range(3)])
    _pv = cx.ps_views(4, 256)
    pbig = Ring([_pv[i] for i in (0, 1, 2, 4, 5, 6)])
    po_ring = Ring([_pv[3], _pv[7]])
    psm = Ring(cx.ps_views(2, 128))
    pfull = Ring(cx.ps_views(2, 512))

    def mm(out_t, out_ap, lhsT_t, lhsT_ap, rhs_t, rhs_ap, start=True, stop=True):
        op('pe', lambda e: e.matmul(out_ap, lhsT_ap, rhs_ap, start=start, stop=stop), reads=[lhsT_t, rhs_t], writes=[out_t])

    scale = 1.0 / 8.0
    op('dve', lambda e: e.tensor_scalar(B[:, :, 0:2], g[:].unsqueeze(2).to_broadcast([128, KC, 2]), 0.0, None, ALU.mult), reads=[g], writes=[B])

    if layer2:
        v2t = sb([32, 256], F32, "e_v2")
        dma('sp', v2t[:], P['v2'], writes=[v2t], track=v2t)
        Weff = sb([128, KC, 64], mybir.dt.bfloat16, "e_weff")
        VFt = sb([128, 2, TE], F32, "e_vft")
        v1t = tmp.next()
        v1v = v1t[:].rearrange("p (c m) -> p c m", m=32)
        dma('sp', v1v, P['v1'], writes=[v1t], track=v1t)
        muv = tmp.next()
        dma('sp', muv[:, 0:8], P['muv'], writes=[muv], track=muv)
        M12 = ACM[:].rearrange("p a t -> p (a t)").rearrange("p (c m) -> p c m", m=64)
        for c in range(8):
            op('dve', lambda e, c=c: e.tensor_scalar(M12[:, c, 32:64], v1v[:, c, :], muv[:, c:c + 1], None, ALU.mult), reads=[v1t, muv], writes=[ACM])
            op('dve', lambda e, c=c: e.tensor_tensor(M12[:, c, 0:32], v1v[:, c, :], M12[:, c, 32:64], ALU.subtract), reads=[v1t, ACM], writes=[ACM])
        for dk2 in range(8):
            pw0, pw1 = psm.next(), psm.next()
            for c in range(8):
                piece = xr.next()
                dma('sp', piece[:], P['wvT'][c * 128:(c + 1) * 128, dk2 * 256:(dk2 + 1) * 256], writes=[piece], track=piece)
                mm(pw0, pw0[:, 0:64], piece, piece[:, 0:128], ACM, M12[:, c, :], start=(c == 0), stop=(c == 7))
                mm(pw1, pw1[:, 0:64], piece, piece[:, 128:256], ACM, M12[:, c, :], start=(c == 0), stop=(c == 7))
            op('act', lambda e, dk2=dk2, pw0=pw0: e.copy(Weff[:, 2 * dk2, :], pw0[:, 0:64]), reads=[pw0], writes=[Weff])
            op('dve', lambda e, dk2=dk2, pw1=pw1: e.tensor_copy(Weff[:, 2 * dk2 + 1, :], pw1[:, 0:64]), reads=[pw1], writes=[Weff])

    for t0 in range(0, ntok, TE):
        first_tile = (t0 == 0)
        src = x_src.ap.rearrange("(k p) t -> p k t", p=128)[:, :, t0:t0 + TE]
        rstd = tmp.next()
        pss = pbig.next()
        for k in range(KC):
            xa = xr.next()
            dma('sp', xa[:], src[:, k, :], reads=[x_src], writes=[xa], track=xa)
            op('act', lambda e, xa=xa: e.activation(sq[:], xa[:], AF.Square), reads=[xa], writes=[sq])
            mm(pss, pss[:], ones, ones[:], sq, sq[:], start=(k == 0), stop=(k == KC - 1))
        op('act', lambda e: e.activation(rstd[:], pss[:], AF.Sqrt, bias=cv[:, 0:1], scale=1.0 / D), reads=[pss, cv], writes=[rstd])
        op('dve', lambda e: e.reciprocal(rstd[:], rstd[:]), reads=[rstd], writes=[rstd])
        for k in range(KC):
            xa = xr.next()
            dma('sp', xa[:], src[:, k, :], reads=[x_src], writes=[xa], track=xa)
            op('dve', lambda e, k=k, xa=xa: e.scalar_tensor_tensor(B[:, k, 2:TE + 2], xa[:], g[:, k:k + 1], rstd[:], ALU.mult, ALU.mult),
               reads=[xa, g, rstd], writes=[B])

        if dbg <= 0:
            op('pool', lambda e: e.memset(Yt[:], 0.0), writes=[Yt])
            op('dve', lambda e: e.tensor_copy(Yt[:, 0, :], B[:, 0, 2:TE + 2]), reads=[B], writes=[Yt])
            ev = dma('sp', y_dst.ap.rearrange("(j p) t -> p j t", p=128)[:, :, t0:t0 + TE], Yt[:], reads=[Yt], writes=[y_dst], track=Yt)
            cx.out_events.append(ev)
            continue

        if layer2:
            lvp = pbig.next()
            for k in range(KC):
                mm(lvp, lvp[0:32, :], Weff, Weff[:, k, 0:32], B, B[:, k, 2:TE + 2], start=(k == 0), stop=False)
                mm(lvp, lvp[0:32, :], Weff, Weff[:, k, 32:64], B, B[:, k, 1:TE + 1], start=False, stop=(k == KC - 1))
            LV = tmp.next()
            op('act', lambda e: e.copy(LV[0:32, :], lvp[0:32, :]), reads=[lvp], writes=[LV])
            for j, sgv in ((0, KKN), (1, KF)):
                psg = pbig.next()
                mm(psg, psg[:], v2t, v2t[0:32, j * 128:(j + 1) * 128], LV, LV[0:32, :])
                op('act', lambda e, j=j, sgv=sgv, psg=psg: e.activation(sgv[:], psg[:], AF.Sigmoid, bias=vec[:, V_V0 + j:V_V0 + j + 1]), reads=[psg, vec], writes=[sgv])
            dma('sp', VFt[:], vf_src.ap.rearrange("(j p) t -> p j t", p=128)[:, :, t0:t0 + TE], reads=[vf_src], writes=[VFt], track=VFt)

        def proj(gi, c0, m, dst_t, dst_ap, eng='act'):
            pp = pbig.next()
            for k in range(KC):
                mm(pp, pp[0:m, :], W, W[:, gi, k, c0:c0 + m], B, B[:, k, 2:TE + 2], start=(k == 0), stop=(k == KC - 1))
            if eng == 'act':
                op('act', lambda e: e.copy(dst_ap, pp[0:m, :]), reads=[pp], writes=[dst_t])
            else:
                op('dve', lambda e: e.tensor_copy(dst_ap, pp[0:m, :]), reads=[pp], writes=[dst_t])

        proj(0, 0, 128, Zx, Zx[:, 1:TE + 1])
        if dbg <= 0.2:
            op('pool', lambda e: e.memset(Yt[:], 0.0), writes=[Yt])
            ev = dma('sp', y_dst.ap.rearrange("(j p) t -> p j t", p=128)[:, :, t0:t0 + TE], Yt[:], reads=[Yt], writes=[y_dst], track=Yt)
            cx.out_events.append(ev)
            continue
        proj(0, 128, 128, KV, KV[:, 128:128 + TE], 'dve')
        if dbg <= 0.3:
            op('pool', lambda e: e.memset(Yt[:], 0.0), writes=[Yt])
            ev = dma('sp', y_dst.ap.rearrange("(j p) t -> p j t", p=128)[:, :, t0:t0 + TE], Yt[:], reads=[Yt], writes=[y_dst], track=Yt)
            cx.out_events.append(ev)
            continue
        for j in range(2):
            proj(1, j * 128, 128, Z['r'], Z['r'][:, j, 1:TE + 1], 'act' if j == 0 else 'dve')
            proj(2, j * 128, 128, Z['k'], Z['k'][:, j, 1:TE + 1], 'act' if j == 0 else 'dve')
            proj(3, j * 128, 128, Z['v'], Z['v'][:, j, 1:TE + 1], 'act' if j == 0 else 'dve')

        if dbg <= 0.4:
            op('pool', lambda e: e.memset(Yt[:], 0.0), writes=[Yt])
            ev = dma('sp', y_dst.ap.rearrange("(j p) t -> p j t", p=128)[:, :, t0:t0 + TE], Yt[:], reads=[Yt], writes=[y_dst], track=Yt)
            cx.out_events.append(ev)
            continue
        def shift(zt, view, mucol):
            d = tmp.next()
            op('dve', lambda e: e.tensor_tensor(d[:], view(0), view(1), ALU.subtract), reads=[zt], writes=[d])
            return d

        for n, mu0 in (('r', V_MUR), ('k', V_MUK), ('v', V_MUV)):
            for j in range(2):
                zt = Z[n]
                d = tmp.next()
                op('dve', lambda e, zt=zt, j=j, d=d: e.tensor_tensor(d[:], zt[:, j, 0:TE], zt[:, j, 1:TE + 1], ALU.subtract), reads=[zt], writes=[d])
                dst_t, dst_ap = {'r': (RSs[j], RSs[j][:]), 'k': (KSs[j], KSs[j][:]), 'v': (VS, VS[:, j, :])}[n]
                op('dve', lambda e, zt=zt, j=j, d=d, mc=mu0 + j, dst_ap=dst_ap: e.scalar_tensor_tensor(dst_ap, d[:], vec[:, mc:mc + 1], zt[:, j, 1:TE + 1], ALU.mult, ALU.add),
                   reads=[d, vec, zt], writes=[dst_t])
        dx = tmp.next()
        op('dve', lambda e: e.tensor_tensor(dx[:], Zx[:, 0:TE], Zx[:, 1:TE + 1], ALU.subtract), reads=[Zx], writes=[dx])
        op('dve', lambda e: e.scalar_tensor_tensor(XS[:], dx[:], vec[:, V_MUX:V_MUX + 1], Zx[:, 1:TE + 1], ALU.mult, ALU.add),
           reads=[dx, vec, Zx], writes=[XS])
        op('pool', lambda e: e.tensor_copy(Zx[:, 0:1], Zx[:, TE:TE + 1]), reads=[Zx], writes=[Zx])
        op('act', lambda e: e.activation(TXW[0:64, :], XS[0:64, :], AF.Tanh), reads=[XS], writes=[TXW])
        if dbg <= 0.6:
            op('pool', lambda e: e.memset(Yt[:], 0.0), writes=[Yt])
            ev = dma('sp', y_dst.ap.rearrange("(j p) t -> p j t", p=128)[:, :, t0:t0 + TE], Yt[:], reads=[Yt], writes=[y_dst], track=Yt)
            cx.out_events.append(ev)
            continue
        if layer2:
            for j, sgv in ((0, KKN), (1, KF)):
                d2 = tmp.next()
                op('dve', lambda e, j=j, d2=d2: e.tensor_tensor(d2[:], VFt[:, j, :], VS[:, j, :], ALU.subtract), reads=[VFt, VS], writes=[d2])
                op('dve', lambda e, sgv=sgv, d2=d2: e.tensor_tensor(d2[:], d2[:], sgv[:], ALU.mult), reads=[d2, sgv], writes=[d2])
                op('dve', lambda e, j=j, d2=d2: e.tensor_tensor(VS[:, j, :], VS[:, j, :], d2[:], ALU.add), reads=[VS, d2], writes=[VS])
        if not layer2:
            dma('sp', vf_dst.ap.rearrange("(j p) t -> p j t", p=128)[:, :, t0:t0 + TE], VS[:], reads=[VS], writes=[vf_dst], track=VS)

        if dbg <= 1:
            op('pool', lambda e: e.memset(Yt[:], 0.0), writes=[Yt])
            ev = dma('sp', y_dst.ap.rearrange("(j p) t -> p j t", p=128)[:, :, t0:t0 + TE], Yt[:], reads=[Yt], writes=[y_dst], track=Yt)
            cx.out_events.append(ev)
            continue
        posi_t = tmp.next()
        posi = posi_t[:].bitcast(I32)
        posf = tmp.next()
        dma('sp', posi, P['pos'][t0:t0 + TE].partition_broadcast(128), writes=[posi_t], track=posi_t)
        op('dve', lambda e: e.tensor_copy(posf[:], posi), reads=[posi_t], writes=[posf])

        def rope_tables(fcol, cos_t, sin_t, npart):
            ang = tmp.next()
            op('dve', lambda e: e.tensor_scalar(ang[0:npart, :], posf[0:npart, :], cv[0:npart, fcol:fcol + 1], None, ALU.mult), reads=[posf, cv], writes=[ang])
            n1 = tmp.next()
            op('dve', lambda e: e.tensor_scalar(n1[0:npart, :], ang[0:npart, :], 1.0 / TWO_PI, 12582912.0, ALU.mult, ALU.add), reads=[ang], writes=[n1])
            op('dve', lambda e: e.tensor_scalar(n1[0:npart, :], n1[0:npart, :], -12582912.0, None, ALU.add), reads=[n1], writes=[n1])
            op('dve', lambda e: e.scalar_tensor_tensor(ang[0:npart, :], n1[0:npart, :], -TWO_PI, ang[0:npart, :], ALU.mult, ALU.add), reads=[n1, ang], writes=[ang])
            s2 = tmp.next()
            c2 = n1
            op('act', lambda e: e.activation(s2[0:npart, :], ang[0:npart, :], AF.Sin, scale=0.5), reads=[ang], writes=[s2])
            op('act', lambda e: e.activation(c2[0:npart, :], ang[0:npart, :], AF.Sin, bias=cv[0:npart, 3:4], scale=0.5), reads=[ang, cv], writes=[c2])
            op('dve', lambda e: e.scalar_tensor_tensor(sin_t[0:npart, :], s2[0:npart, :], 2.0, c2[0:npart, :], ALU.mult, ALU.mult), reads=[s2, c2], writes=[sin_t])
            op('dve', lambda e: e.tensor_tensor(s2[0:npart, :], s2[0:npart, :], s2[0:npart, :], ALU.mult), reads=[s2], writes=[s2])
            op('dve', lambda e: e.tensor_scalar(cos_t[0:npart, :], s2[0:npart, :], -2.0, 1.0, ALU.mult, ALU.add), reads=[s2], writes=[cos_t])

        ck, sk = tmp.next(), tmp.next()
        rope_tables(5, ck, sk, 128)
        pr = pbig.next()
        mm(pr, pr[:], rm, rm[:], KV, KV[:, 128:128 + TE])
        t_a = tmp.next()
        op('dve', lambda e: e.tensor_tensor(t_a[:], pr[:], sk[:], ALU.mult), reads=[pr, sk], writes=[t_a])
        op('dve', lambda e: e.tensor_tensor(KV[:, 128:128 + TE], KV[:, 128:128 + TE], ck[:], ALU.mult), reads=[KV, ck], writes=[KV])
        op('dve', lambda e: e.tensor_tensor(KV[:, 128:128 + TE], KV[:, 128:128 + TE], t_a[:], ALU.add), reads=[KV, t_a], writes=[KV])
        rope_tables(4, cosq, sinq, 64)
        if dbg <= 2:
            op('pool', lambda e: e.memset(Yt[:], 0.0), writes=[Yt])
            ev = dma('sp', y_dst.ap.rearrange("(j p) t -> p j t", p=128)[:, :, t0:t0 + TE], Yt[:], reads=[Yt], writes=[y_dst], track=Yt)
            cx.out_events.append(ev)
            continue
        def prep(j):
            RS = RSs[j]
            KS = KSs[j]
            pw = pbig.next()
            mm(pw, pw[:], w2a2, w2a2[0:64, j * 128:(j + 1) * 128], TXW, TXW[0:64, :])
            op('act', lambda e: e.activation(LW[:], pw[:], AF.Sigmoid, bias=vec[:, V_W0 + j:V_W0 + j + 1]), reads=[pw, vec], writes=[LW])
            op('dve', lambda e: e.tensor_scalar(LW[:], LW[:], -math.exp(-0.5), None, ALU.mult), reads=[LW], writes=[LW])
            pa = pbig.next()
            mm(pa, pa[:], w2a2, w2a2[64:128, j * 128:(j + 1) * 128], XS, XS[64:128, :])
            op('act', lambda e: e.activation(AA[:], pa[:], AF.Sigmoid, bias=vec[:, V_A0 + j:V_A0 + j + 1]), reads=[pa, vec], writes=[AA])
            op('dve', lambda e: e.tensor_tensor_scan(CL[:], rst[:], LW[:], 0.0, ALU.mult, ALU.add), reads=[rst, LW], writes=[CL])
            kk = tmp.next()
            op('dve', lambda e: e.tensor_scalar(kk[:], KS[:], vec[:, V_KK + j:V_KK + j + 1], None, ALU.mult), reads=[KS, vec], writes=[kk])
            kk2 = tmp.next()
            op('pool', lambda e: e.tensor_tensor(kk2[:], kk[:], kk[:], ALU.mult), reads=[kk], writes=[kk2])
            pk = pbig.next()
            mm(pk, pk[:], bones, bones[:], kk2, kk2[:])
            op('act', lambda e: e.activation(kk2[:], pk[:], AF.Sqrt, bias=cv[:, 1:2]), reads=[pk, cv], writes=[kk2])
            op('dve', lambda e: e.reciprocal(kk2[:], kk2[:]), reads=[kk2], writes=[kk2])
            op('dve', lambda e: e.tensor_tensor(KKN[:], kk[:], kk2[:], ALU.mult), reads=[kk, kk2], writes=[KKN])
            tk = kk
            op('dve', lambda e: e.tensor_scalar(tk[:], AA[:], vec[:, V_KA + j:V_KA + j + 1], vec[:, V_OMKA + j:V_OMKA + j + 1], ALU.mult, ALU.add),
               reads=[AA, vec], writes=[tk])
            op('dve', lambda e: e.tensor_tensor(KF[:], KS[:], tk[:], ALU.mult), reads=[KS, tk], writes=[KF])
            op('act', lambda e: e.activation(ECL[:], CL[:], AF.Exp), reads=[CL], writes=[ECL])
            encl = tmp.next()
            op('act', lambda e: e.activation(encl[:], CL[:], AF.Exp, scale=-1.0), reads=[CL], writes=[encl])
            eclm = kk2
            op('dve', lambda e: e.tensor_tensor(eclm[:], CL[:], LW[:], ALU.subtract), reads=[CL, LW], writes=[eclm])
            op('act', lambda e: e.activation(eclm[:], eclm[:], AF.Exp), reads=[eclm], writes=[eclm])
            ecL = tmp.next()
            clv = CL[:].rearrange("p (c l) -> p c l", l=LCH)
            op('dve', lambda e: e.tensor_tensor(ecL[:].rearrange("p (c l) -> p c l", l=LCH), clv[:, :, LCH - 1:LCH].to_broadcast([128, NCH, LCH]), clv, ALU.subtract),
               reads=[CL], writes=[ecL])
            op('act', lambda e: e.activation(ecL[:], ecL[:], AF.Exp), reads=[ecL], writes=[ecL])
            op('dve', lambda e: e.scalar_tensor_tensor(ACM[:, 0, :], KKN[:], -1.0, eclm[:], ALU.mult, ALU.mult), reads=[KKN, eclm], writes=[ACM])
            op('pool', lambda e: e.tensor_tensor(ACM[:, 1, :], RS[:], ECL[:], ALU.mult), reads=[RS, ECL], writes=[ACM])
            bb = tk
            op('dve', lambda e: e.tensor_tensor(bb[:], KKN[:], AA[:], ALU.mult), reads=[KKN, AA], writes=[bb])
            op('dve', lambda e: e.tensor_tensor(BCM[:, 0, :], bb[:], encl[:], ALU.mult), reads=[bb, encl], writes=[BCM])
            op('pool', lambda e: e.tensor_tensor(BCM[:, 1, :], KF[:], encl[:], ALU.mult), reads=[KF, encl], writes=[BCM])
            op('dve', lambda e: e.tensor_tensor(BH[:], bb[:], ecL[:], ALU.mult), reads=[bb, ecL], writes=[BH])
            op('pool', lambda e: e.tensor_tensor(KH[:], KF[:], ecL[:], ALU.mult), reads=[KF, ecL], writes=[KH])
            rk = encl
            op('dve', lambda e: e.scalar_tensor_tensor(rk[:], RS[:], vec[:, V_RK + j:V_RK + j + 1], KF[:], ALU.mult, ALU.mult), reads=[RS, vec, KF], writes=[rk])
            pbn = pbig.next()
            mm(pbn, pbn[:], bones, bones[:], rk, rk[:])
            op('dve', lambda e: e.tensor_tensor(BV[:], pbn[:], VS[:, j, :], ALU.mult), reads=[pbn, VS], writes=[BV])

            return None

        def chunk_gen(j, c, R, YV):
            cs = slice(c * LCH, (c + 1) * LCH)
            bmv = bm2[:].unsqueeze(1).to_broadcast([128, 2, 2, 64])
            AXT = R['keep'].next()
            op('dve', lambda e, AXT=AXT: e.tensor_tensor(AXT[:].rearrange("p (a h t) -> p a h t", a=2, h=2), ACM[:, :, cs].unsqueeze(2).to_broadcast([128, 2, 2, 64]), bmv, ALU.mult),
               reads=[ACM, bm2], writes=[AXT])
            BX = R['keep'].next()
            op('pool', lambda e, BX=BX: e.tensor_tensor(BX[:].rearrange("p (a h t) -> p a h t", a=2, h=2), BCM[:, :, cs].unsqueeze(2).to_broadcast([128, 2, 2, 64]), bmv, ALU.mult),
               reads=[BCM, bm2], writes=[BX])
            HX3 = [R['small'].next() for _ in range(3)]
            bm1 = bm2[:]
            for tt_, (src_t, src_ap) in zip(HX3, ((BH, BH[:, cs]), (VS, VS[:, j, cs]), (KH, KH[:, cs]))):
                op('pool', lambda e, tt_=tt_, src_ap=src_ap: e.tensor_tensor(tt_[:].rearrange("p (h t) -> p h t", h=2), src_ap.unsqueeze(1).to_broadcast([128, 2, 64]), bm1, ALU.mult),
                   reads=[src_t, bm2], writes=[tt_])
            yield
            ptm = pfull.next()
            for q_, (src_t, src_ap) in enumerate(((AXT, AXT[:, 0:128]), (HX3[0], HX3[0][:]), (HX3[1], HX3[1][:]), (HX3[2], HX3[2][:]))):
                mm(ptm, ptm[:, q_ * 128:(q_ + 1) * 128], src_t, src_ap, ident, ident[:])
            TM4 = R['tm4'].next()
            op('act', lambda e, TM4=TM4, ptm=ptm: e.copy(TM4[:], ptm[:]), reads=[ptm], writes=[TM4])
            ATm, BHm, Vm, KHm = (TM4[:, q_ * 128:(q_ + 1) * 128] for q_ in range(4))
            yield
            p1 = pbig.next()
            mm(p1, p1[:], AXT, AXT[:, 0:128], BX, BX[:])
            NN = R['keep'].next()
            op('dve', lambda e, NN=NN, p1=p1: e.tensor_tensor(NN[:].rearrange("p (a t) -> p a t", a=2), p1[:].rearrange("p (a t) -> p a t", a=2),
                                                             msk[:, 0, :].unsqueeze(1).to_broadcast([128, 2, 128]), ALU.mult), reads=[p1, msk], writes=[NN])
            p2 = pbig.next()
            mm(p2, p2[:], BX, BX[:, 0:128], AXT, AXT[:])
            TR = R['keep'].next()
            op('dve', lambda e, TR=TR, p2=p2: e.tensor_tensor(TR[:], p2[:], msk[:, 1:3, :].rearrange("p a t -> p (a t)"), ALU.mult), reads=[p2, msk], writes=[TR])
            p3 = psm.next()
            mm(p3, p3[:], BX, BX[:, 128:256], AXT, AXT[:, 128:256])
            QK = R['small'].next()
            op('dve', lambda e, QK=QK, p3=p3: e.tensor_tensor(QK[:], p3[:], msk[:, 2, :], ALU.mult), reads=[p3, msk], writes=[QK])
            yield
            PT = R['big'].next()
            op('act', lambda e, PT=PT, NN=NN: e.copy(PT[:, 0:128], NN[:, 0:128]), reads=[NN], writes=[PT])
            op('dve', lambda e, PT=PT, NN=NN: e.tensor_tensor(PT[:, 128:256], NN[:, 0:128], ident[:], ALU.add), reads=[NN, ident], writes=[PT])
            PkT_t, PkT = TR, TR[:, 0:128]
            nsteps = 7
            for kstep in range(nsteps - 1):
                last = (kstep == nsteps - 2)
                pA = pbig.next()
                if kstep == 0:
                    mm(pA, pA[:, 0:128], PkT_t, PkT, PT, PT[:, 0:128])
                elif not last:
                    mm(pA, pA[:], PkT_t, PkT, PT, PT[:])
                else:
                    mm(pA, pA[:, 128:256], PkT_t, PkT, PT, PT[:, 128:256])
                PTn = R['big'].next()
                if not last:
                    pB = psm.next()
                    mm(pB, pB[:], PT, PT[:, 0:128], PkT_t, PkT)
                    PkTn = R['small'].next()
                    op('act', lambda e, PkTn=PkTn, pB=pB: e.copy(PkTn[:], pB[:]), reads=[pB], writes=[PkTn])
                    op('act', lambda e, PTn=PTn, pA=pA: e.copy(PTn[:, 0:128], pA[:, 0:128]), reads=[pA], writes=[PTn])
                if kstep == 0:
                    op('dve', lambda e, PTn=PTn, PT=PT: e.tensor_copy(PTn[:, 128:256], PT[:, 128:256]), reads=[PT], writes=[PTn])
                else:
                    op('dve', lambda e, PTn=PTn, PT=PT, pA=pA: e.tensor_tensor(PTn[:, 128:256], pA[:, 128:256], PT[:, 128:256], ALU.add),
                       reads=[pA, PT], writes=[PTn])
                PT = PTn
                if not last:
                    PkT_t, PkT = PkTn, PkTn[:]
            yield
            Tt, Tm = PT, PT[:, 128:256]
            p4 = pbig.next()
            mm(p4, p4[:, 0:128], Tt, Tm, TR, TR[:, 128:256])
            mm(p4, p4[:, 128:256], Tt, Tm, TM4, BHm)
            GZ = R['big'].next()
            op('act', lambda e, GZ=GZ, p4=p4: e.copy(GZ[:], p4[:]), reads=[p4], writes=[GZ])
            yield
            p5 = pbig.next()
            mm(p5, p5[:], NN, NN[:, 128:256], GZ, GZ[:])
            HX = R['big'].next()
            op('dve', lambda e, HX=HX, p5=p5, QK=QK: e.tensor_tensor(HX[:, 0:128], p5[:, 0:128], QK[:], ALU.add), reads=[p5, QK], writes=[HX])
            op('dve', lambda e, HX=HX, p5=p5, TM4=TM4, KHm=KHm: e.tensor_tensor(HX[:, 128:256], p5[:, 128:256], KHm, ALU.add), reads=[p5, TM4], writes=[HX])
            yield
            DW = R['small'].next()
            op('dve', lambda e, DW=DW, c=c: e.tensor_scalar(DW[:], ident[:], ECL[:, (c + 1) * LCH - 1:(c + 1) * LCH], None, ALU.mult), reads=[ident, ECL], writes=[DW])
            p6 = pbig.next()
            mm(p6, p6[:], TM4, ATm, GZ, GZ[:])
            REM = R['big'].next()
            op('dve', lambda e, REM=REM, p6=p6, AXT=AXT: e.tensor_tensor(REM[:, 0:128], p6[:, 0:128], AXT[:, 128:256], ALU.add), reads=[p6, AXT], writes=[REM])
            op('dve', lambda e, REM=REM, p6=p6, DW=DW: e.tensor_tensor(REM[:, 128:256], p6[:, 128:256], DW[:], ALU.add), reads=[p6, DW], writes=[REM])
            yield
            Sc = S2[j][si[j] % 2]
            Sn = S2[j][(si[j] + 1) % 2]
            si[j] += 1
            pyc = psm.next()
            mm(pyc, pyc[:], TM4, Vm, HX, HX[:, 0:128], start=True, stop=False)
            mm(pyc, pyc[:], Sc, Sc[:], REM, REM[:, 0:128], start=False, stop=True)
            op('act', lambda e, pyc=pyc, cs=cs: e.copy(YV[:, cs], pyc[:, 0:64]), reads=[pyc], writes=[YV])
            op('dve', lambda e, pyc=pyc, cs=cs: e.tensor_tensor(YV[:, cs], YV[:, cs], pyc[:, 64:128], ALU.add), reads=[pyc, YV], writes=[YV])
            p7 = psm.next()
            mm(p7, p7[:], REM, REM[:, 128:256], Sc, Sc[:], start=True, stop=False)
            mm(p7, p7[:], HX, HX[:, 128:256], TM4, Vm, start=False, stop=True)
            op('act', lambda e, Sn=Sn, p7=p7: e.copy(Sn[:], p7[:]), reads=[p7], writes=[Sn])

        def gn(j, YV):
            yv = YV
            pm_ = pbig.next()
            mm(pm_, pm_[:], bones, bones[:], yv, yv[:])
            yc = tmp.next()
            op('dve', lambda e: e.scalar_tensor_tensor(yc[:], pm_[:], -1.0 / 64, yv[:], ALU.mult, ALU.add), reads=[pm_, yv], writes=[yc])
            y2 = yv
            op('pool', lambda e: e.tensor_tensor(y2[:], yc[:], yc[:], ALU.mult), reads=[yc], writes=[y2])
            pv_ = pbig.next()
            mm(pv_, pv_[:], bones, bones[:], y2, y2[:])
            op('act', lambda e: e.activation(y2[:], pv_[:], AF.Sqrt, bias=cv[:, 2:3], scale=1.0 / 64), reads=[pv_, cv], writes=[y2])
            op('dve', lambda e: e.reciprocal(y2[:], y2[:]), reads=[y2], writes=[y2])
            op('dve', lambda e: e.tensor_tensor(yc[:], yc[:], y2[:], ALU.mult), reads=[yc, y2], writes=[yc])
            op('dve', lambda e: e.tensor_scalar(yc[:], yc[:], vec[:, V_LNW + j:V_LNW + j + 1], vec[:, V_LNB + j:V_LNB + j + 1], ALU.mult, ALU.add),
               reads=[yc, vec], writes=[yc])
            op('dve', lambda e: e.tensor_tensor(yc[:], yc[:], BV[:], ALU.add), reads=[yc, BV], writes=[yc])
            sgt = y2
            pgt = pbig.next()
            for k in range(KC):
                mm(pgt, pgt[:], W, W[:, 4, k, j * 128:(j + 1) * 128], B, B[:, k, 2:TE + 2], start=(k == 0), stop=(k == KC - 1))
            op('act', lambda e: e.activation(sgt[:], pgt[:], AF.Silu), reads=[pgt], writes=[sgt])
            op('dve', lambda e: e.tensor_tensor(Yt[:, j, :], yc[:], sgt[:], ALU.mult), reads=[yc, sgt], writes=[Yt])


        def attn_gen():
            for jj in range(2):
                proj(5, jj * 128, 128, QR, QR[:, jj, :], 'act' if jj == 0 else 'dve')
            for h in range(4):
                hs_ = slice((h % 2) * 64, (h % 2) * 64 + 64)
                pq = pbig.next()
                mm(pq, pq[0:64, :], rm, rm[hs_, hs_], QR, QR[hs_, h // 2, :])
                pq2 = pbig.next()
                mm(pq2, pq2[0:64, :], ident, ident[hs_, hs_], QR, QR[hs_, h // 2, :])
                t_b = atmp.next()
                op('dve', lambda e, pq=pq, t_b=t_b: e.tensor_tensor(t_b[0:64, :], pq[0:64, :], sinq[:], ALU.mult), reads=[pq, sinq], writes=[t_b])
                op('dve', lambda e, h=h, pq2=pq2: e.tensor_tensor(Q4[:, h, :], pq2[0:64, :], cosq[:], ALU.mult), reads=[pq2, cosq], writes=[Q4])
                op('dve', lambda e, h=h, t_b=t_b: e.tensor_tensor(Q4[:, h, :], Q4[:, h, :], t_b[0:64, :], ALU.add), reads=[Q4, t_b], writes=[Q4])
                yield

            if dbg <= 4:
                op('pool', lambda e: e.memset(Yt[:, 2:4, :], 0.0), writes=[Yt])
            for bq in range(TE // 128 if dbg > 4 else 0):
                blk_first = first_tile and bq == 0
                pvt = psm.next()
                mm(pvt, pvt[:], KV, KV[:, 128 + bq * 128:128 + (bq + 1) * 128], ident, ident[:])
                vt_cur = Vtm.next()
                op('act', lambda e, vt_cur=vt_cur, pvt=pvt: e.copy(vt_cur[:], pvt[:]), reads=[pvt], writes=[vt_cur])
                if vstate['p'] is None:
                    vstate['p'] = vt_cur
                vtm_prev = vstate['p']
                po = po_ring.next()
                for h in range(4):
                    pssc = pbig.next()
                    mm(pssc, pssc[:], Q4, Q4[:, h, bq * 128:(bq + 1) * 128], KV, KV[0:64, bq * 128:bq * 128 + 256])
                    sc = atmp.next()
                    op('dve', lambda e, sc=sc, pssc=pssc: e.scalar_tensor_tensor(sc[:], pssc[:], scale, mb[:, 1 if blk_first else 0, :], ALU.mult, ALU.add),
                       reads=[pssc, mb], writes=[sc])
                    op('dve', lambda e, sc=sc: e.reduce_max(st[:, 0:1], sc[:], axis=AX.X), reads=[sc], writes=[st])
                    op('dve', lambda e, h=h: e.tensor_scalar(st[:, 1:2], st[:, 0:1], sinkb[:, h:h + 1], -1.0, ALU.max, ALU.mult), reads=[st, sinkb], writes=[st])
                    op('dve', lambda e: e.memset(st[:, 2:3], 0.0), writes=[st])
                    pe_ = atmp.next()
                    op('act', lambda e, sc=sc, pe_=pe_: e.activation(pe_[:], sc[:], AF.Exp, bias=st[:, 1:2], accum_out=st[:, 2:3]), reads=[sc, st], writes=[pe_, st])
                    op('act', lambda e, h=h: e.activation(st[:, 3:4], sinkb[:, h:h + 1], AF.Exp, bias=st[:, 1:2]), reads=[sinkb, st], writes=[st])
                    op('dve', lambda e: e.tensor_tensor(st[:, 4:5], st[:, 2:3], st[:, 3:4], ALU.add), reads=[st], writes=[st])
                    op('dve', lambda e: e.reciprocal(st[:, 5:6], st[:, 4:5]), reads=[st], writes=[st])
                    op('dve', lambda e, pe_=pe_: e.tensor_scalar(pe_[:], pe_[:], st[:, 5:6], None, ALU.mult), reads=[pe_, st], writes=[pe_])
                    yield
                    pts = []
                    for half in range(2):
                        ptp = psm.next()
                        mm(ptp, ptp[:], pe_, pe_[:, half * 128:(half + 1) * 128], ident, ident[:])
                        pT = asmall.next()
                        op('act' if half == 0 else 'dve',
                           (lambda e, pT=pT, ptp=ptp: e.copy(pT[:], ptp[:])) if half == 0 else (lambda e, pT=pT, ptp=ptp: e.tensor_copy(pT[:], ptp[:])),
                           reads=[ptp], writes=[pT])
                        pts.append(pT)
                    osl = po[(h % 2) * 64:(h % 2) * 64 + 64, (h // 2) * 128:(h // 2) * 128 + 128]
                    mm(po, osl, vtm_prev, vtm_prev[:, 64:128], pts[0], pts[0][:], start=True, stop=False)
                    mm(po, osl, vt_cur, vt_cur[:, 64:128], pts[1], pts[1][:], start=False, stop=True)
                    yield
                vstate['p'] = vt_cur
                yield
                for jj in range(2):
                    sgt = atmp.next()
                    pgb = psm.next()
                    for k in range(KC):
                        mm(pgb, pgb[:], W, W[:, 6, k, jj * 128:(jj + 1) * 128], B, B[:, k, 2 + bq * 128:2 + (bq + 1) * 128], start=(k == 0), stop=(k == KC - 1))
                    op('act', lambda e, sgt=sgt, pgb=pgb: e.activation(sgt[:, 0:128], pgb[:], AF.Silu), reads=[pgb], writes=[sgt])
                    op('dve', lambda e, sgt=sgt, jj=jj, bq=bq, po=po: e.tensor_tensor(Yt[:, 2 + jj, bq * 128:(bq + 1) * 128], po[:, jj * 128:(jj + 1) * 128], sgt[:, 0:128], ALU.mult),
                       reads=[po, sgt], writes=[Yt])
            yield

        def step(g):
            try:
                next(g)
                return True
            except StopIteration:
                return False

        ag = attn_gen() if dbg > 4 else iter(())
        ag_alive = True
        for j in range(2):
            prep(j)
            YV = yvs[j]
            if dbg < 4:
                op('dve', lambda e, YV=YV: e.tensor_copy(YV[:], BV[:]), reads=[BV], writes=[YV])
            for c0 in range(0, NCH if dbg > 3 else 0, NIL):
                gens = [chunk_gen(j, c0 + q_, RS_[q_], YV) for q_ in range(NIL)]
                while gens:
                    gens = [g for g in gens if step(g)]
                    if ag_alive:
                        ag_alive = step(ag)
            gn(j, YV)
        while ag_alive:
            ag_alive = step(ag)

        for n in ('r', 'k', 'v'):
            zt = Z[n]
            op('pool', lambda e, zt=zt: e.tensor_copy(zt[:, :, 0:1], zt[:, :, TE:TE + 1]), reads=[zt], writes=[zt])

        op('pool', lambda e: e.tensor_copy(KV[:, 0:128], KV[:, TE:TE + 128]), reads=[KV], writes=[KV])
        if layer2:
            op('pool', lambda e: e.tensor_copy(B[:, :, 1:2], B[:, :, TE + 1:TE + 2]), reads=[B], writes=[B])
        ev = dma('sp', y_dst.ap.rearrange("(j p) t -> p j t", p=128)[:, :, t0:t0 + TE], Yt[:], reads=[Yt], writes=[y_dst], track=Yt)
        cx.out_events.append(ev)
    cx.release(mark)


A_W = 1024
SHIFT_W = 3200
C_R, C_K, C_V, C_XW, C_AG, C_Q, C_KB, C_VB, C_BG = 0, 1024, 2048, 3072, 3200, 4224, 5248, 5504, 5760


def make_consts_even():
    c = make_consts()
    i = np.arange(128)
    c['bones'] = ((i[:, None] // 64) == (i[None, :] // 64)).astype(np.float32)
    c['idb2'] = ((i[:, None] % 64) == np.arange(64)[None, :]).astype(np.float32)
    t = np.arange(64)
    SL = (t[:, None] > t[None, :]).astype(np.float32)
    SU = SL.T.copy()
    UI = (t[None, :] >= t[:, None]).astype(np.float32)
    em = np.zeros((128, 3, 128), np.float32)
    for hh in range(2):
        bs_ = slice(hh * 64, (hh + 1) * 64)
        em[bs_, 0, bs_] = SL
        em[bs_, 1, bs_] = SU
        em[bs_, 2, bs_] = UI
    c['bm2'] = np.ascontiguousarray(np.repeat(((i[:, None] // 64) == np.arange(2)[None, :]).astype(np.float32)[:, :, None], 64, axis=2))
    c['emsk'] = em
    q = np.arange(128)
    NEG = -30000.0
    am = np.full((128, 2, 256), NEG, np.float32)
    prev_ok = q[None, :] > q[:, None]
    cur_ok = q[None, :] <= q[:, None]
    am[:, 0, 0:128] = np.where(prev_ok, 0.0, NEG)
    am[:, 0, 128:256] = np.where(cur_ok, 0.0, NEG)
    am[:, 1, 128:256] = np.where(cur_ok, 0.0, NEG)
    c['amask'] = am
    rmat = np.zeros((128, 128), np.float32)
    for p in range(128):
        d = p % 64
        if d < 8:
            rmat[p + 8, p] = -1.0
        elif d < 16:
            rmat[p - 8, p] = 1.0
    c['ropem'] = rmat
    rst = np.ones((128, TE), np.float32)
    rst[:, ::LCH] = 0.0
    c['rst'] = rst
    cvec = np.zeros((128, 8), np.float32)
    cvec[:, 0] = RMS_EPS
    cvec[:, 1] = 1e-12
    cvec[:, 2] = GN_EPS
    cvec[:, 3] = math.pi / 2
    inv = np.power(np.float32(500000.0), -np.arange(8, dtype=np.float32) / np.float32(8)).astype(np.float32)
    fq = np.zeros(128, np.float32)
    for p in range(128):
        d = p % 64
        if d < 16:
            fq[p] = inv[d % 8]
    fk = fq.copy()
    fk[64:] = 0.0
    cvec[:, 4] = fq
    cvec[:, 5] = fk
    c['cvec'] = cvec
    return c


def prep_even(e, hg, P):
    W = P['e_w_in'][e]
    hs = slice(hg * 256, (hg + 1) * 256)

    def cols(base):
        return W[:, base + hg * 256: base + (hg + 1) * 256]
    kvb = np.concatenate([W[:, C_KB + hg * 64: C_KB + (hg + 1) * 64], W[:, C_VB + hg * 64: C_VB + (hg + 1) * 64]], axis=1)
    wcols = np.concatenate([W[:, C_XW:C_XW + 128], kvb, cols(C_R), cols(C_K), cols(C_V), cols(C_AG), cols(C_Q), cols(C_BG)], axis=1)
    mu = P['e_mu'][e]
    vec = np.zeros((128, 40), np.float32)

    def put(col, v256):
        vec[:, col:col + 2] = v256.reshape(2, 128).T
    put(0, mu[C_R:C_R + 1024][hs]); put(2, mu[C_K:C_K + 1024][hs]); put(4, mu[C_V:C_V + 1024][hs])
    put(6, P['rwkv_w0'][e][hs]); put(8, P['rwkv_a0'][e][hs]); put(10, P['rwkv_k_k'][e][hs]); put(12, P['rwkv_k_a'][e][hs])
    put(16, P['rwkv_r_k'][e].reshape(-1)[hs]); put(18, P['rwkv_ln_w'][e][hs]); put(20, P['rwkv_ln_b'][e][hs])
    if e > 0:
        put(22, P['rwkv_v0'][e - 1][hs])
    vec[:, 24] = mu[C_XW:C_XW + 128]
    w2a2 = np.concatenate([P['rwkv_w2'][e][:, hs], P['rwkv_a2'][e][:, hs]], axis=0)
    out = {
        'norm': vec_pk(P['e_norm'][e]),
        'w_in': tile_w(wcols),
        'vec': vec,
        'w2a2': np.ascontiguousarray(w2a2, dtype=np.float32),
        'sinks': np.ascontiguousarray(P['attn_sinks'][e][hg * 4:(hg + 1) * 4], dtype=np.float32),
    }
    if e > 0:
        out['wvT'] = np.ascontiguousarray(W[:, C_V:C_V + 1024].T, dtype=np.float32)
        out['v1'] = np.ascontiguousarray(P['rwkv_v1'][e - 1].reshape(8, 128, 32).transpose(1, 0, 2), dtype=np.float32)
        out['muv'] = vec_pk(mu[C_V:C_V + 1024])
        out['v2'] = np.ascontiguousarray(P['rwkv_v2'][e - 1][:, hs], dtype=np.float32)
    return out


SEQ = 8192
_PROG_CACHE = {}


def emit_outproj_stage(cm, w_out, y_src, x_src, x_dst):
    cx, TT = cm.cx, cm.TT
    for t0 in range(0, NTOK, TT):
        cx.dma('pool', cm.Y[:], y_src.ap.rearrange("(k p) t -> p k t", p=128)[:, :, t0:t0 + TT], reads=[y_src], writes=[cm.Y], track=cm.Y)
        cm.out_proj_residual(w_out, x_src, x_dst, t0)


def build_E(e, in_shapes):
    nc = bass.Bass("TRN2", target_bir_lowering=False)
    aps = {k: declare(nc, k, v) for k, v in in_shapes.items()}
    yo = nc.dram_tensor("y", [512, SEQ], F32, kind="ExternalOutput").ap()
    cx = Ctx(nc)
    PP = {k[2:]: aps[k] for k in aps if k.startswith('p_')}
    CC = {k[2:]: aps[k] for k in aps if k.startswith('c_')}
    yd = DT(yo)
    if e == 0:
        vfo = nc.dram_tensor("vf", [256, SEQ], F32, kind="ExternalOutput").ap()
        vd = DT(vfo)
        emit_even_phase(cx, PP, CC, DT(aps['xT']), yd, vd, None, SEQ, False)
        for key, v in vd.wd.items():
            cx.out_events.append((key, v))
    else:
        emit_even_phase(cx, PP, CC, DT(aps['xT']), yd, None, DT(aps['vfin']), SEQ, True)
    cx.finish()
    return nc


def build_O(final, in_shapes):
    nc = bass.Bass("TRN2", target_bir_lowering=False)
    aps = {k: declare(nc, k, v) for k, v in in_shapes.items()}
    xo = nc.dram_tensor("xo", [D, NTOK], F32, kind="ExternalOutput").ap()
    xm = nc.dram_tensor("xmid", [D, NTOK], F32, kind="Internal").ap()
    cx = Ctx(nc)
    CC = {k[2:]: aps[k] for k in aps if k.startswith('c_')}
    cm = Common(cx, CC, 512)
    PP = {k[2:]: aps[k] for k in aps if k.startswith('p_')}
    x_in, y_in, x_mid, x_out = DT(aps['xT']), DT(aps['yT']), DT(xm), DT(xo)
    emit_outproj_stage(cm, aps['wout_e'], y_in, x_in, x_mid)
    if not final:
        emit_odd_layer(cm, PP, x_mid, x_out)
        for key, v in x_out.wd.items():
            cx.out_events.append((key, v))
    else:
        xm2 = nc.dram_tensor("xmid2", [D, NTOK], F32, kind="Internal").ap()
        x_mid2 = DT(xm2)
        emit_odd_layer(cm, PP, x_mid, x_mid2)
        fn = cx.sb([128, KC], F32)
        cx.dma('sp', fn[:], aps['fnorm'], writes=[fn], track=fn)
        emit_final_norm(cm, x_mid2, x_out, fn)
    cx.finish()
    return nc


def _shapes(d):
    return {k: v for k, v in d.items()}


def kernel(**inputs):
    P = {k: np.asarray(v) for k, v in inputs.items()}
    x = P['x'].astype(np.float32, copy=False)
    pos = P['positions'].astype(np.int32, copy=False)
    B_ = x.shape[0]
    cores = list(range(8))
    ce = make_consts_even()
    co = make_consts()
    xT_b = [np.ascontiguousarray(x[b].T) for b in range(B_)]
    vf_cores = None
    for e in range(2):
        in_maps = []
        for c in cores:
            b, hg = c // 4, c % 4
            m = {'xT': xT_b[b]}
            for k, v in ce.items():
                m['c_' + k] = v
            for k, v in prep_even(e, hg, P).items():
                m['p_' + k] = v
            m['p_pos'] = np.ascontiguousarray(pos[b])
            if e > 0:
                m['vfin'] = vf_cores[c]
            in_maps.append(m)
        nc = build_E(e, in_maps[0])
        res = run_bass_kernel_spmd(nc, in_maps, core_ids=cores)
        if e == 0:
            vf_cores = [np.ascontiguousarray(res.results[c]['vf']) for c in cores]
        yT_b = []
        for b in range(B_):
            yT = np.empty((2048, SEQ), np.float32)
            for hg in range(4):
                r = res.results[b * 4 + hg]['y']
                yT[hg * 256:(hg + 1) * 256] = r[0:256]
                yT[1024 + hg * 256:1024 + (hg + 1) * 256] = r[256:512]
            yT_b.append(yT)
        del res
        final = (e == 1)
        in_maps = []
        podd = prep_odd(P['o_norm'][e], P['o_w_in'][e], P['sgu_ln_w'][e], P['sgu_ln_b'][e], P['sgu_ws'][e], P['sgu_bs'][e], P['o_w_out'][e])
        wout_e = tile_w(P['e_w_out'][e])
        for c in cores:
            b, sg = c // 4, c % 4
            sl = slice(sg * NTOK, (sg + 1) * NTOK)
            m = {'xT': np.ascontiguousarray(xT_b[b][:, sl]), 'yT': np.ascontiguousarray(yT_b[b][:, sl]), 'wout_e': wout_e}
            for k, v in co.items():
                m['c_' + k] = v
            for k, v in podd.items():
                m['p_' + k] = v
            if final:
                m['fnorm'] = vec_pk(P['final_norm'])
            in_maps.append(m)
        nc = build_O(final, in_maps[0])
        res = run_bass_kernel_spmd(nc, in_maps, core_ids=cores)
        for b in range(B_):
            xT_b[b] = np.concatenate([res.results[b * 4 + sg]['xo'] for sg in range(4)], axis=1)
        del res
    out = np.stack([np.ascontiguousarray(xT_b[b].T) for b in range(B_)], axis=0)
    return out.astype(np.float32, copy=False)
```

```python
import math
import numpy as np
import concourse.bass as bass
import concourse.mybir as mybir
from concourse.bass_utils import run_bass_kernel_spmd

F32 = mybir.dt.float32
F32R = mybir.dt.float32r
I32 = mybir.dt.int32
AF = mybir.ActivationFunctionType
ALU = mybir.AluOpType
AX = mybir.AxisListType

D = 2048
NTOK = 2048
KC = 16
RMS_EPS = 1e-5
LN_EPS = 1e-5


class T:
    def __init__(self, t):
        self.t = t
        self.w = None
        self.r = {}
        self.dsem = None
        self.tr = self

    def __getitem__(self, k):
        return self.t[k]


class DT(T):
    def __init__(self, ap):
        super().__init__(None)
        self.ap = ap
        self.wd = {}


class Ctx:
    def __init__(self, nc):
        self.nc = nc
        self.eng = {'pe': nc.tensor, 'act': nc.scalar, 'dve': nc.vector, 'pool': nc.gpsimd, 'sp': nc.sync}
        self.sems = {}
        self.cnt = {}
        for k in ('pe', 'act', 'dve', 'pool'):
            self.sems[k] = nc.semaphore('s_' + k).__enter__()
            self.cnt[k] = 0
        self.seen = {}
        self.nsb = 0
        self.nps = 0
        self.ndram = 0
        self.out_events = []
        self.stack = []

    def sb(self, shape, dt=F32, name=None):
        self.nsb += 1
        cmgr = self.nc.sbuf_tensor(name or f"sb{self.nsb}", list(shape), dt)
        t = T(cmgr.__enter__())
        self.stack.append(cmgr)
        return t

    def mark(self):
        return len(self.stack)

    def release(self, mark):
        self.barrier()
        while len(self.stack) > mark:
            self.stack.pop().__exit__(None, None, None)

    def barrier(self):
        for e in ('pe', 'act', 'dve', 'pool', 'sp'):
            for k, v in self.cnt.items():
                if v > 0 and k != e and self.seen.get((e, k), 0) < v:
                    self.eng[e].wait_ge(self.sems[k], v)
                    self.seen[(e, k)] = v

    def ps(self, shape, dt=F32, name=None):
        self.nps += 1
        t = T(self.nc.psum_tensor(name or f"ps{self.nps}", list(shape), dt).__enter__())
        t.excl = True
        return t

    def ps_views(self, nbanks, width):
        banks = []
        for b in range(nbanks):
            self.nps += 1
            banks.append(T(self.nc.psum_tensor(f"pbank{self.nps}", [128, 512], F32).__enter__()))
            banks[-1].excl = True
        out = []
        for i in range(512 // width):
            for bk in banks:
                v = T(bk.t[:, i * width:(i + 1) * width])
                v.tr = bk
                out.append(v)
        return out

    def _dsem(self, t):
        if t.dsem is None:
            key = f"d{len(self.sems)}"
            self.sems[key] = self.nc.semaphore('s_' + key).__enter__()
            self.cnt[key] = 0
            t.dsem = key
        return t.dsem

    def _waits(self, e, reads, writes, allow_inline=False):
        need = {}
        reads = [t.tr for t in reads]
        writes = [t.tr for t in writes]
        for t in reads:
            if isinstance(t, DT):
                for k, v in t.wd.items():
                    need[k] = max(need.get(k, 0), v)
            elif t.w:
                need[t.w[0]] = max(need.get(t.w[0], 0), t.w[1])
            if getattr(t, 'excl', False):
                for k, v in t.r.items():
                    if k != e:
                        need[k] = max(need.get(k, 0), v)
        for t in writes:
            if t.w and not isinstance(t, DT):
                need[t.w[0]] = max(need.get(t.w[0], 0), t.w[1])
            for k, v in t.r.items():
                need[k] = max(need.get(k, 0), v)
        eng = self.eng[e]
        pend = []
        for k, v in need.items():
            if e == 'pe' and k == 'pe':
                continue
            if self.seen.get((e, k), 0) < v:
                pend.append((k, v))
                self.seen[(e, k)] = v
        inline = None
        if INLINE_WAIT and allow_inline and pend:
            inline = pend.pop()
        for k, v in pend:
            eng.wait_ge(self.sems[k], v)
        return inline

    def op(self, e, fn, reads=(), writes=()):
        inline = self._waits(e, reads, writes, allow_inline=True)
        ins = fn(self.eng[e])
        if inline is not None:
            ins._wait_ge(self.sems[inline[0]], inline[1])
        self.cnt[e] += 1
        ins.then_inc(self.sems[e], 1)
        ev = (e, self.cnt[e])
        for t in reads:
            t.tr.r[e] = ev[1]
        for t in writes:
            t.tr.w = ev
            t.tr.r = {}
        return ins

    def dma(self, q, out, in_, reads=(), writes=(), track=None):
        self._waits(q, reads, writes)
        t = track
        key = self._dsem(t)
        ins = self.eng[q].dma_start(out=out, in_=in_)
        self.cnt[key] += 16
        ins.then_inc(self.sems[key], 16)
        ev = (key, self.cnt[key])
        for r_ in reads:
            r_.r[key] = ev[1]
        for w_ in writes:
            if isinstance(w_, DT):
                w_.wd[key] = ev[1]
            else:
                w_.w = ev
                w_.r = {}
        return ev

    def collective(self, kind, in_ap, out_ap, groups, in_dt, out_dt):
        self._waits('pool', [in_dt], [out_dt])
        key = self._dsem(out_dt)
        ins = self.nc.gpsimd.collective_compute(kind, ALU.bypass, replica_groups=groups, ins=[in_ap], outs=[out_ap])
        self.cnt[key] += 16
        ins.then_inc(self.sems[key], 16)
        ev = (key, self.cnt[key])
        in_dt.r[key] = ev[1]
        out_dt.wd[key] = ev[1]
        return ev

    def finish(self):
        for key, v in self.out_events:
            self.eng['sp'].wait_ge(self.sems[key], v)


def r32(ap):
    return ap.bitcast(F32R)


def f32(ap):
    return ap.bitcast(F32)


class Common:
    def __init__(self, cx, consts, TT):
        nc = cx.nc
        self.cx = cx
        self.TT = TT
        self.ones = cx.sb([128, 128], F32R, "ones")
        cx.dma('pool', self.ones[:], consts['ones'], writes=[self.ones], track=self.ones)
        self.tri = cx.sb([128, 128], F32, "tri")
        cx.dma('sp', self.tri[:], consts['tri'], writes=[self.tri], track=self.tri)
        self.ident = cx.sb([128, 128], F32, "ident")
        cx.dma('sp', self.ident[:], consts['ident'], writes=[self.ident], track=self.ident)
        self.A = cx.sb([128, KC, TT], F32, "bufA")
        self.Y = cx.sb([128, KC, TT], F32R, "bufY")
        self.B = cx.sb([128, KC, TT], F32R, "bufB")
        self.wb = [cx.sb([128, KC, 256], F32R, f"wb{i}") for i in range(2)]
        self.wi = 0
        self.pbig = [cx.ps([128, 512], F32, f"pbig{i}") for i in range(4)]
        self.pi = 0
        self.sq = [cx.sb([128, TT], F32R, f"sq{i}") for i in range(1)]
        self.rstd = cx.sb([128, TT], F32, "rstd")
        self.xc = [cx.sb([128, TT], F32, f"xc{i}") for i in range(2)]
        self.xo = [cx.sb([128, TT], F32, f"xo{i}") for i in range(2)]
        self.xci = 0
        self.eps_rms = cx.sb([128, 1], F32, "eps_rms")
        cx.op('dve', lambda e: e.memset(self.eps_rms[:], RMS_EPS), writes=[self.eps_rms])

    def next_w(self):
        w = self.wb[self.wi % len(self.wb)]
        self.wi += 1
        return w

    def next_p(self):
        p = self.pbig[self.pi % len(self.pbig)]
        self.pi += 1
        return p

    def load_w(self, dram_group):
        w = self.next_w()
        self.cx.dma('pool', w[:], dram_group, writes=[w], track=w)
        return w

    def load_x_and_norm(self, x_src, t0, gvec):
        cx, TT = self.cx, self.TT
        A, B = self.A, self.B
        src = x_src.ap.rearrange("(k p) t -> p k t", p=128)[:, :, t0:t0 + TT]
        cx.dma('sp', A[:], src, reads=[x_src], writes=[A], track=A)
        pss = self.next_p()
        for k in range(KC):
            sq = self.sq[0]
            cx.op('act', lambda e, k=k, sq=sq: e.activation(sq[:], A[:, k, :], AF.Square), reads=[A], writes=[sq])
            cx.op('pe', lambda e, k=k, sq=sq: e.matmul(pss[:, :TT], self.ones[:], sq[:], start=(k == 0), stop=(k == KC - 1)),
                  reads=[sq, self.ones], writes=[pss])
        rstd = self.rstd
        cx.op('act', lambda e: e.activation(rstd[:], pss[:, :TT], AF.Sqrt, bias=self.eps_rms[:], scale=1.0 / D),
              reads=[pss, self.eps_rms], writes=[rstd])
        cx.op('dve', lambda e: e.reciprocal(rstd[:], rstd[:]), reads=[rstd], writes=[rstd])
        for k in range(KC):
            cx.op('dve', lambda e, k=k: e.scalar_tensor_tensor(B[:, k, :], A[:, k, :], gvec[:, k:k + 1], rstd[:],
                                                                ALU.mult, ALU.mult),
                  reads=[A, gvec, rstd], writes=[B])

    def out_proj_residual(self, wout_groups, x_src, x_dst, t0, is_output=False):
        cx, TT = self.cx, self.TT
        A = self.Y
        xs = x_src.ap.rearrange("(k p) t -> p k t", p=128)
        xd = x_dst.ap.rearrange("(k p) t -> p k t", p=128)
        wq = [self.load_w(wout_groups[0])]
        for g in range(8):
            w = wq.pop(0)
            if g + 1 < 8:
                wq.append(self.load_w(wout_groups[g + 1]))
            for j in range(2):
                dk = g * 2 + j
                ps = self.next_p()
                for k in range(KC):
                    cx.op('pe', lambda e, k=k, j=j, w=w, ps=ps: e.matmul(ps[:, :TT], w[:, k, j * 128:(j + 1) * 128],
                                                                         A[:, k, :], start=(k == 0), stop=(k == KC - 1)),
                          reads=[w, A], writes=[ps])
                xc = self.xc[self.xci % 2]
                xo = self.xo[self.xci % 2]
                self.xci += 1
                cx.dma('sp', xc[:], xs[:, dk, t0:t0 + TT], reads=[x_src], writes=[xc], track=xc)
                cx.op('dve', lambda e, xc=xc, xo=xo, ps=ps: e.tensor_tensor(xo[:], ps[:, :TT], xc[:], ALU.add),
                      reads=[ps, xc], writes=[xo])
                ev = cx.dma('sp', xd[:, dk, t0:t0 + TT], xo[:], reads=[xo], writes=[x_dst], track=xo)
                if is_output:
                    cx.out_events.append(ev)


def emit_final_norm(cm, x_src, out_dst, gvec):
    cx, TT = cm.cx, cm.TT
    od = out_dst.ap.rearrange("(k p) t -> p k t", p=128)
    for t0 in range(0, NTOK, TT):
        cm.load_x_and_norm(x_src, t0, gvec)
        A = cm.A
        for k in range(KC):
            cx.op('dve', lambda e, k=k: e.scalar_tensor_tensor(A[:, k, :], A[:, k, :], gvec[:, k:k + 1], cm.rstd[:],
                                                                ALU.mult, ALU.mult),
                  reads=[A, gvec, cm.rstd], writes=[A])
        ev = cx.dma('sp', od[:, :, t0:t0 + TT], A[:], reads=[A], writes=[out_dst], track=A)
        cx.out_events.append(ev)


def emit_odd_layer(cm, P, x_src, x_dst):
    cx, TT = cm.cx, cm.TT
    mark = cx.mark()
    nb = TT // 128
    g = cx.sb([128, KC], F32, None)
    cx.dma('sp', g[:], P['norm'], writes=[g], track=g)
    lnw = cx.sb([128, D], F32)
    lnb = cx.sb([128, D], F32)
    cx.dma('sp', lnw[:], P['ln_w'].partition_broadcast(128), writes=[lnw], track=lnw)
    cx.dma('sp', lnb[:], P['ln_b'].partition_broadcast(128), writes=[lnb], track=lnb)
    bsb2 = [cx.sb([128, 128], F32) for _ in range(2)]
    wm = cx.sb([128, 16, 128], F32R)
    wstage = cm.A[:, 0:D // TT, :].rearrange("p k t -> p (k t)").rearrange("p (g t) -> p g t", t=128)
    cx.dma('sp', wstage, P['wsT'], writes=[cm.A], track=cm.A)
    cx.op('pool', lambda e: e.tensor_tensor(wm[:], wstage, cm.tri[:].unsqueeze(1).to_broadcast([128, 16, 128]), ALU.mult),
          reads=[cm.A, cm.tri], writes=[wm])
    eps_ln = cx.sb([128, 1], F32)
    cx.op('dve', lambda e: e.memset(eps_ln[:], LN_EPS), writes=[eps_ln])
    vv = cx.sb([128, nb, D], F32R)
    st = cx.sb([128, 8], F32)
    sg = [cx.sb([128, TT], F32) for _ in range(1)]
    t1 = [cx.sb([128, TT], F32) for _ in range(2)]

    for t0 in range(0, NTOK, TT):
        cm.load_x_and_norm(x_src, t0, g)
        B = cm.B
        wq = [cm.load_w(P['w_in'][16])]
        for gi in range(8):
            w = wq.pop(0)
            if gi + 1 < 8:
                wq.append(cm.load_w(P['w_in'][16 + gi + 1]))
            for b in range(nb):
                ps = cm.next_p()
                for k in range(KC):
                    cx.op('pe', lambda e, k=k, b=b, w=w, ps=ps: e.matmul(ps[:, :256], B[:, k, b * 128:(b + 1) * 128], w[:, k, :],
                                                                         start=(k == 0), stop=(k == KC - 1)),
                          reads=[B, w], writes=[ps])
                cx.op('act', lambda e, b=b, gi=gi, ps=ps: e.copy(vv[:, b, gi * 256:(gi + 1) * 256], ps[:, :256]),
                      reads=[ps], writes=[vv])
        for b in range(nb):
            cx.op('dve', lambda e, b=b: e.reduce_sum(st[:, 0:1], f32(vv[:, b, :]), axis=AX.X), reads=[vv], writes=[st])
            cx.op('dve', lambda e: e.tensor_scalar(st[:, 1:2], st[:, 0:1], -1.0 / D, None, ALU.mult), reads=[st], writes=[st])
            cx.op('dve', lambda e, b=b: e.tensor_scalar(vv[:, b, :], f32(vv[:, b, :]), st[:, 1:2], None, ALU.add),
                  reads=[vv, st], writes=[vv])
            cx.op('dve', lambda e: e.memset(st[:, 2:3], 0.0), writes=[st])
            cx.op('act', lambda e, b=b: e.activation(cm.A[:, 0:D // TT, :].rearrange("p k t -> p (k t)"), f32(vv[:, b, :]), AF.Square,
                                                     accum_out=st[:, 2:3]),
                  reads=[vv], writes=[cm.A, st])
            cx.op('act', lambda e: e.activation(st[:, 3:4], st[:, 2:3], AF.Sqrt, bias=eps_ln[:], scale=1.0 / D),
                  reads=[st, eps_ln], writes=[st])
            cx.op('dve', lambda e: e.reciprocal(st[:, 4:5], st[:, 3:4]), reads=[st], writes=[st])
            cx.op('dve', lambda e, b=b: e.scalar_tensor_tensor(vv[:, b, :], f32(vv[:, b, :]), st[:, 4:5], lnw[:], ALU.mult, ALU.mult),
                  reads=[vv, st, lnw], writes=[vv])
            cx.op('dve', lambda e, b=b: e.tensor_tensor(vv[:, b, :], f32(vv[:, b, :]), lnb[:], ALU.add),
                  reads=[vv, lnb], writes=[vv])
        A = cm.Y
        wq = [cm.load_w(P['w_in'][0])]
        for gi in range(16):
            w = wq.pop(0)
            if gi + 1 < 16:
                wq.append(cm.load_w(P['w_in'][gi + 1]))
            pu = cm.next_p()
            pg = cm.next_p()
            for k in range(KC):
                cx.op('pe', lambda e, k=k, w=w, pu=pu: e.matmul(pu[:, :TT], w[:, k, 0:128], B[:, k, :], start=(k == 0), stop=(k == KC - 1)),
                      reads=[B, w], writes=[pu])
            for k in range(KC):
                cx.op('pe', lambda e, k=k, w=w, pg=pg: e.matmul(pg[:, :TT], w[:, k, 128:256], B[:, k, :], start=(k == 0), stop=(k == KC - 1)),
                      reads=[B, w], writes=[pg])
            pm = cm.next_p()
            for b in range(nb):
                cx.op('pe', lambda e, b=b, gi=gi, pm=pm: e.matmul(pm[:, b * 128:(b + 1) * 128], vv[:, b, gi * 128:(gi + 1) * 128], wm[:, gi, :],
                                                                  start=True, stop=True),
                      reads=[vv, wm], writes=[pm])
            sgt = sg[0]
            t1t = t1[gi % 2]
            bsb = bsb2[gi % 2]
            cx.dma('sp', bsb[:], P['bs'][gi * 128:(gi + 1) * 128].partition_broadcast(128), writes=[bsb], track=bsb)
            cx.op('act', lambda e, sgt=sgt, pg=pg: e.activation(sgt[:], pg[:, :TT], AF.Silu), reads=[pg], writes=[sgt])
            cx.op('dve', lambda e, sgt=sgt, pu=pu: e.tensor_tensor(sgt[:], pu[:, :TT], sgt[:], ALU.mult), reads=[pu, sgt], writes=[sgt])
            cx.op('dve', lambda e, t1t=t1t, pm=pm, bsb=bsb: e.tensor_tensor(
                t1t[:].rearrange("p (b t) -> p b t", t=128), pm[:, :TT].rearrange("p (b t) -> p b t", t=128),
                bsb[:].unsqueeze(1).to_broadcast([128, nb, 128]), ALU.add), reads=[pm, bsb], writes=[t1t])
            cx.op('dve', lambda e, t1t=t1t, sgt=sgt, gi=gi: e.tensor_tensor(A[:, gi, :], t1t[:], sgt[:], ALU.mult),
                  reads=[t1t, sgt], writes=[A])
        cm.out_proj_residual(P['w_out'], x_src, x_dst, t0)
    cx.release(mark)


def tile_w(Wcols):
    n = Wcols.shape[1] // 256
    a = Wcols.reshape(KC, 128, n, 256).transpose(2, 1, 0, 3)
    return np.ascontiguousarray(a, dtype=np.float32)


def vec_pk(v):
    return np.ascontiguousarray(v.reshape(-1, 128).T, dtype=np.float32)


def prep_odd(o_norm, o_w_in, ln_w, ln_b, ws, bs, w_out):
    u = o_w_in[:, 0:2048].reshape(2048, 16, 128)
    gt = o_w_in[:, 4096:6144].reshape(2048, 16, 128)
    ug = np.concatenate([u, gt], axis=2).reshape(2048, 16 * 256)
    wcols = np.concatenate([ug, o_w_in[:, 2048:4096]], axis=1)
    return {
        'norm': vec_pk(o_norm),
        'w_in': tile_w(wcols),
        'ln_w': np.ascontiguousarray(ln_w, dtype=np.float32),
        'ln_b': np.ascontiguousarray(ln_b, dtype=np.float32),
        'wsT': np.ascontiguousarray(ws.transpose(2, 0, 1), dtype=np.float32),
        'bs': np.ascontiguousarray(bs.reshape(-1), dtype=np.float32),
        'w_out': tile_w(w_out),
    }


def make_consts():
    s = np.arange(128)
    return {
        'ones': np.ones((128, 128), np.float32),
        'tri': (s[None, :] >= s[:, None]).astype(np.float32),
        'ident': np.eye(128, dtype=np.float32),
    }


def declare(nc, name, arr, kind="ExternalInput"):
    dt = I32 if arr.dtype == np.int32 else F32
    return nc.dram_tensor(name, list(arr.shape), dt, kind=kind).ap()


TE = 256
INLINE_WAIT = True
NIL = 2
LCH = 64
NCH = TE // LCH
GN_EPS = 64 * 1e-5
TWO_PI = 2.0 * math.pi


class Ring:
    def __init__(self, tiles):
        self.t = tiles
        self.i = 0

    def next(self):
        t = self.t[self.i % len(self.t)]
        self.i += 1
        return t


def emit_even_phase(cx, P, consts, x_src, y_dst, vf_dst, vf_src, ntok, layer2, dbg=9):
    nc = cx.nc
    mark = cx.mark()
    sb, ps, op, dma = cx.sb, cx.ps, cx.op, cx.dma
    ones = sb([128, 128], F32R, "e_ones")
    dma('pool', ones[:], consts['ones'], writes=[ones], track=ones)
    ident = sb([128, 128], F32, "e_ident")
    dma('sp', ident[:], consts['ident'], writes=[ident], track=ident)
    bones = sb([128, 128], F32, "e_bones")
    dma('sp', bones[:], consts['bones'], writes=[bones], track=bones)
    msk = sb([128, 3, 128], F32, "e_msk")
    dma('sp', msk[:], consts['emsk'], writes=[msk], track=msk)
    mb = sb([128, 2, 256], F32, "e_mb")
    dma('sp', mb[:], consts['amask'], writes=[mb], track=mb)
    rm = sb([128, 128], F32, "e_rm")
    dma('sp', rm[:], consts['ropem'], writes=[rm], track=rm)
    rst = sb([128, TE], F32, "e_rst")
    dma('sp', rst[:], consts['rst'], writes=[rst], track=rst)
    cv = sb([128, 8], F32, "e_cv")
    dma('sp', cv[:], consts['cvec'], writes=[cv], track=cv)
    g = sb([128, KC], F32, "e_g")
    dma('sp', g[:], P['norm'], writes=[g], track=g)
    vec = sb([128, 40], F32, "e_vec")
    dma('sp', vec[:], P['vec'], writes=[vec], track=vec)
    op('dve', lambda e: e.tensor_scalar(vec[:, 14:16], vec[:, 12:14], -1.0, 1.0, ALU.mult, ALU.add), reads=[vec], writes=[vec])
    w2a2 = sb([128, 256], F32, "e_w2a2")
    dma('sp', w2a2[:], P['w2a2'], writes=[w2a2], track=w2a2)
    sinkb = sb([128, 4], F32, "e_sink")
    dma('sp', sinkb[:], P['sinks'].partition_broadcast(128), writes=[sinkb], track=sinkb)
    W = sb([128, 7, KC, 256], mybir.dt.bfloat16, "e_W")
    for gi in range(7):
        dma('pool', W[:, gi], P['w_in'][gi], writes=[W], track=W)
    V_MUR, V_MUK, V_MUV, V_W0, V_A0, V_KK, V_KA, V_OMKA, V_RK, V_LNW, V_LNB, V_V0, V_MUX = 0, 2, 4, 6, 8, 10, 12, 14, 16, 18, 20, 22, 24

    xr = Ring([sb([128, TE], F32, f"e_xr{i}") for i in range(2)])
    XA = sb([128, KC, TE], F32, "e_XA")
    B = sb([128, KC, TE + 2], mybir.dt.bfloat16, "e_B")
    sq = sb([128, TE], F32R, "e_sq")
    Z = {n: sb([128, 2, TE + 1], F32, "e_z" + n) for n in ('r', 'k', 'v')}
    Zx = sb([128, TE + 1], F32, "e_zx")
    for zt in list(Z.values()) + [Zx]:
        op('pool', lambda e, zt=zt: e.memset(zt[:], 0.0), writes=[zt])
    KV = sb([128, 128 + TE], F32, "e_kv")
    op('pool', lambda e: e.memset(KV[:], 0.0), writes=[KV])
    Yt = sb([128, 4, TE], F32, "e_y")
    S2 = [[sb([128, 128], F32, f"e_S{h}_{i}") for i in range(2)] for h in range(2)]
    for h in range(2):
        op('pool', lambda e, h=h: e.memset(S2[h][0][:], 0.0), writes=[S2[h][0]])
    si = [0, 0]
    bm2 = sb([128, 2, 64], F32, "e_bm2")
    dma('sp', bm2[:], consts['bm2'], writes=[bm2], track=bm2)
    tmp = Ring([sb([128, TE], F32, f"e_t{i}") for i in range(7)])
    ACM = sb([128, 2, TE], F32, "e_acm")
    RW4 = sb([128, 4, TE], F32, "e_rw4")

    def view(ap):
        v = T(ap)
        v.tr = RW4
        return v
    BCM = view(RW4.t[:, 0:2, :])
    BH = view(RW4.t[:, 2, :])
    KH = view(RW4.t[:, 3, :])
    Q4 = sb([64, 4, TE], F32, "e_q4")
    QR = sb([128, 2, TE], F32, "e_qr")
    VS = sb([128, 2, TE], F32, "e_vs")
    ECL = sb([128, TE], F32, "e_ecl")
    CL = sb([128, TE], F32, "e_cl")
    LW = sb([128, TE], F32, "e_lw")
    AA = sb([128, TE], F32, "e_aa")
    KKN = sb([128, TE], F32, "e_kkn")
    KF = sb([128, TE], F32, "e_kf")
    BV = sb([128, TE], F32, "e_bv")
    RSs = [sb([128, TE], F32, f"e_rs{i}") for i in range(2)]
    KSs = [sb([128, TE], F32, f"e_ks{i}") for i in range(2)]
    XS = sb([128, TE], F32, "e_xs")
    TXW = XS
    cosq = sb([64, TE], F32, "e_cos")
    sinq = sb([64, TE], F32, "e_sin")
    RS_ = [dict(small=Ring([sb([128, 128], F32, f"e_s{r}_{i}") for i in range(6)]),
                big=Ring([sb([128, 256], F32, f"e_b{r}_{i}") for i in range(5)]),
                keep=Ring([sb([128, 256], F32, f"e_k{r}_{i}") for i in range(4)]),
                tm4=Ring([sb([128, 512], F32, f"e_tm{r}_{i}") for i in range(1)])) for r in range(NIL)]
    atmp = Ring([sb([128, TE], F32, f"e_at{i}") for i in range(4)])
    asmall = Ring([sb([128, 128], F32, f"e_as{i}") for i in range(4)])
    yvs = [sb([128, TE], F32, f"e_yv{i}") for i in range(2)]
    vstate = {'p': None}
    st = sb([128, 8], F32, "e_st")
    Vtm = Ring([sb([128, 128], F32, f"e_vtm{i}") for i in range(3)])
    _pv = cx.ps_views(4, 256)
    pbig = Ring([_pv[i] for i in (0, 1, 2, 4, 5, 6)])
    po_ring = Ring([_pv[3], _pv[7]])
    psm = Ring(cx.ps_views(2, 128))
    pfull = Ring(cx.ps_views(2, 512))

    def mm(out_t, out_ap, lhsT_t, lhsT_ap, rhs_t, rhs_ap, start=True, stop=True):
        op('pe', lambda e: e.matmul(out_ap, lhsT_ap, rhs_ap, start=start, stop=stop), reads=[lhsT_t, rhs_t], writes=[out_t])

    scale = 1.0 / 8.0
    op('dve', lambda e: e.tensor_scalar(B[:, :, 0:2], g[:].unsqueeze(2).to_broadcast([128, KC, 2]), 0.0, None, ALU.mult), reads=[g], writes=[B])

    if layer2:
        v2t = sb([32, 256], F32, "e_v2")
        dma('sp', v2t[:], P['v2'], writes=[v2t], track=v2t)
        Weff = sb([128, KC, 64], mybir.dt.bfloat16, "e_weff")
        VFt = sb([128, 2, TE], F32, "e_vft")
        v1t = tmp.next()
        v1v = v1t[:].rearrange("p (c m) -> p c m", m=32)
        dma('sp', v1v, P['v1'], writes=[v1t], track=v1t)
        muv = tmp.next()
        dma('sp', muv[:, 0:8], P['muv'], writes=[muv], track=muv)
        M12 = ACM[:].rearrange("p a t -> p (a t)").rearrange("p (c m) -> p c m", m=64)
        for c in range(8):
            op('dve', lambda e, c=c: e.tensor_scalar(M12[:, c, 32:64], v1v[:, c, :], muv[:, c:c + 1], None, ALU.mult), reads=[v1t, muv], writes=[ACM])
            op('dve', lambda e, c=c: e.tensor_tensor(M12[:, c, 0:32], v1v[:, c, :], M12[:, c, 32:64], ALU.subtract), reads=[v1t, ACM], writes=[ACM])
        for dk2 in range(8):
            pw0, pw1 = psm.next(), psm.next()
            for c in range(8):
                piece = xr.next()
                dma('sp', piece[:], P['wvT'][c * 128:(c + 1) * 128, dk2 * 256:(dk2 + 1) * 256], writes=[piece], track=piece)
                mm(pw0, pw0[:, 0:64], piece, piece[:, 0:128], ACM, M12[:, c, :], start=(c == 0), stop=(c == 7))
                mm(pw1, pw1[:, 0:64], piece, piece[:, 128:256], ACM, M12[:, c, :], start=(c == 0), stop=(c == 7))
            op('act', lambda e, dk2=dk2, pw0=pw0: e.copy(Weff[:, 2 * dk2, :], pw0[:, 0:64]), reads=[pw0], writes=[Weff])
            op('dve', lambda e, dk2=dk2, pw1=pw1: e.tensor_copy(Weff[:, 2 * dk2 + 1, :], pw1[:, 0:64]), reads=[pw1], writes=[Weff])

    for t0 in range(0, ntok, TE):
        first_tile = (t0 == 0)
        xsrc_v = x_src.ap.rearrange("(k p) t -> p k t", p=128)
        if first_tile:
            dma('sp', XA[:], xsrc_v[:, :, t0:t0 + TE], reads=[x_src], writes=[XA], track=XA)
        rstd = tmp.next()
        pss = pbig.next()
        for k in range(KC):
            op('act', lambda e, k=k: e.activation(sq[:], XA[:, k, :], AF.Square), reads=[XA], writes=[sq])
            mm(pss, pss[:], ones, ones[:], sq, sq[:], start=(k == 0), stop=(k == KC - 1))
        op('act', lambda e: e.activation(rstd[:], pss[:], AF.Sqrt, bias=cv[:, 0:1], scale=1.0 / D), reads=[pss, cv], writes=[rstd])
        op('dve', lambda e: e.reciprocal(rstd[:], rstd[:]), reads=[rstd], writes=[rstd])
        for k in range(KC):
            op('dve', lambda e, k=k: e.scalar_tensor_tensor(B[:, k, 2:TE + 2], XA[:, k, :], g[:, k:k + 1], rstd[:], ALU.mult, ALU.mult),
               reads=[XA, g, rstd], writes=[B])
        if t0 + TE < ntok:
            dma('sp', XA[:], xsrc_v[:, :, t0 + TE:t0 + 2 * TE], reads=[x_src], writes=[XA], track=XA)

        if dbg <= 0:
            op('pool', lambda e: e.memset(Yt[:], 0.0), writes=[Yt])
            op('dve', lambda e: e.tensor_copy(Yt[:, 0, :], B[:, 0, 2:TE + 2]), reads=[B], writes=[Yt])
            ev = dma('sp', y_dst.ap.rearrange("(j p) t -> p j t", p=128)[:, :, t0:t0 + TE], Yt[:], reads=[Yt], writes=[y_dst], track=Yt)
            cx.out_events.append(ev)
            continue

        if layer2:
            lvp = pbig.next()
            for k in range(KC):
                mm(lvp, lvp[0:32, :], Weff, Weff[:, k, 0:32], B, B[:, k, 2:TE + 2], start=(k == 0), stop=False)
                mm(lvp, lvp[0:32, :], Weff, Weff[:, k, 32:64], B, B[:, k, 1:TE + 1], start=False, stop=(k == KC - 1))
            LV = tmp.next()
            op('act', lambda e: e.copy(LV[0:32, :], lvp[0:32, :]), reads=[lvp], writes=[LV])
            for j, sgv in ((0, KKN), (1, KF)):
                psg = pbig.next()
                mm(psg, psg[:], v2t, v2t[0:32, j * 128:(j + 1) * 128], LV, LV[0:32, :])
                op('act', lambda e, j=j, sgv=sgv, psg=psg: e.activation(sgv[:], psg[:], AF.Sigmoid, bias=vec[:, V_V0 + j:V_V0 + j + 1]), reads=[psg, vec], writes=[sgv])
            dma('sp', VFt[:], vf_src.ap.rearrange("(j p) t -> p j t", p=128)[:, :, t0:t0 + TE], reads=[vf_src], writes=[VFt], track=VFt)

        def proj(gi, c0, m, dst_t, dst_ap, eng='act'):
            pp = pbig.next()
            for k in range(KC):
                mm(pp, pp[0:m, :], W, W[:, gi, k, c0:c0 + m], B, B[:, k, 2:TE + 2], start=(k == 0), stop=(k == KC - 1))
            if eng == 'act':
                op('act', lambda e: e.copy(dst_ap, pp[0:m, :]), reads=[pp], writes=[dst_t])
            else:
                op('dve', lambda e: e.tensor_copy(dst_ap, pp[0:m, :]), reads=[pp], writes=[dst_t])

        proj(0, 0, 128, Zx, Zx[:, 1:TE + 1])
        if dbg <= 0.2:
            op('pool', lambda e: e.memset(Yt[:], 0.0), writes=[Yt])
            ev = dma('sp', y_dst.ap.rearrange("(j p) t -> p j t", p=128)[:, :, t0:t0 + TE], Yt[:], reads=[Yt], writes=[y_dst], track=Yt)
            cx.out_events.append(ev)
            continue
        proj(0, 128, 128, KV, KV[:, 128:128 + TE], 'dve')
        if dbg <= 0.3:
            op('pool', lambda e: e.memset(Yt[:], 0.0), writes=[Yt])
            ev = dma('sp', y_dst.ap.rearrange("(j p) t -> p j t", p=128)[:, :, t0:t0 + TE], Yt[:], reads=[Yt], writes=[y_dst], track=Yt)
            cx.out_events.append(ev)
            continue
        for j in range(2):
            proj(1, j * 128, 128, Z['r'], Z['r'][:, j, 1:TE + 1], 'act' if j == 0 else 'dve')
            proj(2, j * 128, 128, Z['k'], Z['k'][:, j, 1:TE + 1], 'act' if j == 0 else 'dve')
            proj(3, j * 128, 128, Z['v'], Z['v'][:, j, 1:TE + 1], 'act' if j == 0 else 'dve')

        if dbg <= 0.4:
            op('pool', lambda e: e.memset(Yt[:], 0.0), writes=[Yt])
            ev = dma('sp', y_dst.ap.rearrange("(j p) t -> p j t", p=128)[:, :, t0:t0 + TE], Yt[:], reads=[Yt], writes=[y_dst], track=Yt)
            cx.out_events.append(ev)
            continue
        def shift(zt, view, mucol):
            d = tmp.next()
            op('dve', lambda e: e.tensor_tensor(d[:], view(0), view(1), ALU.subtract), reads=[zt], writes=[d])
            return d

        for n, mu0 in (('r', V_MUR), ('k', V_MUK), ('v', V_MUV)):
            for j in range(2):
                zt = Z[n]
                d = tmp.next()
                op('dve', lambda e, zt=zt, j=j, d=d: e.tensor_tensor(d[:], zt[:, j, 0:TE], zt[:, j, 1:TE + 1], ALU.subtract), reads=[zt], writes=[d])
                dst_t, dst_ap = {'r': (RSs[j], RSs[j][:]), 'k': (KSs[j], KSs[j][:]), 'v': (VS, VS[:, j, :])}[n]
                op('dve', lambda e, zt=zt, j=j, d=d, mc=mu0 + j, dst_ap=dst_ap: e.scalar_tensor_tensor(dst_ap, d[:], vec[:, mc:mc + 1], zt[:, j, 1:TE + 1], ALU.mult, ALU.add),
                   reads=[d, vec, zt], writes=[dst_t])
        dx = tmp.next()
        op('dve', lambda e: e.tensor_tensor(dx[:], Zx[:, 0:TE], Zx[:, 1:TE + 1], ALU.subtract), reads=[Zx], writes=[dx])
        op('dve', lambda e: e.scalar_tensor_tensor(XS[:], dx[:], vec[:, V_MUX:V_MUX + 1], Zx[:, 1:TE + 1], ALU.mult, ALU.add),
           reads=[dx, vec, Zx], writes=[XS])
        op('pool', lambda e: e.tensor_copy(Zx[:, 0:1], Zx[:, TE:TE + 1]), reads=[Zx], writes=[Zx])
        op('act', lambda e: e.activation(TXW[0:64, :], XS[0:64, :], AF.Tanh), reads=[XS], writes=[TXW])
        if dbg <= 0.6:
            op('pool', lambda e: e.memset(Yt[:], 0.0), writes=[Yt])
            ev = dma('sp', y_dst.ap.rearrange("(j p) t -> p j t", p=128)[:, :, t0:t0 + TE], Yt[:], reads=[Yt], writes=[y_dst], track=Yt)
            cx.out_events.append(ev)
            continue
        if layer2:
            for j, sgv in ((0, KKN), (1, KF)):
                d2 = tmp.next()
                op('dve', lambda e, j=j, d2=d2: e.tensor_tensor(d2[:], VFt[:, j, :], VS[:, j, :], ALU.subtract), reads=[VFt, VS], writes=[d2])
                op('dve', lambda e, sgv=sgv, d2=d2: e.tensor_tensor(d2[:], d2[:], sgv[:], ALU.mult), reads=[d2, sgv], writes=[d2])
                op('dve', lambda e, j=j, d2=d2: e.tensor_tensor(VS[:, j, :], VS[:, j, :], d2[:], ALU.add), reads=[VS, d2], writes=[VS])
        if not layer2:
            dma('sp', vf_dst.ap.rearrange("(j p) t -> p j t", p=128)[:, :, t0:t0 + TE], VS[:], reads=[VS], writes=[vf_dst], track=VS)

        if dbg <= 1:
            op('pool', lambda e: e.memset(Yt[:], 0.0), writes=[Yt])
            ev = dma('sp', y_dst.ap.rearrange("(j p) t -> p j t", p=128)[:, :, t0:t0 + TE], Yt[:], reads=[Yt], writes=[y_dst], track=Yt)
            cx.out_events.append(ev)
            continue
        posi_t = tmp.next()
        posi = posi_t[:].bitcast(I32)
        posf = tmp.next()
        dma('sp', posi, P['pos'][t0:t0 + TE].partition_broadcast(128), writes=[posi_t], track=posi_t)
        op('dve', lambda e: e.tensor_copy(posf[:], posi), reads=[posi_t], writes=[posf])

        def rope_tables(fcol, cos_t, sin_t, npart):
            ang = tmp.next()
            op('dve', lambda e: e.tensor_scalar(ang[0:npart, :], posf[0:npart, :], cv[0:npart, fcol:fcol + 1], None, ALU.mult), reads=[posf, cv], writes=[ang])
            n1 = tmp.next()
            op('dve', lambda e: e.tensor_scalar(n1[0:npart, :], ang[0:npart, :], 1.0 / TWO_PI, 12582912.0, ALU.mult, ALU.add), reads=[ang], writes=[n1])
            op('dve', lambda e: e.tensor_scalar(n1[0:npart, :], n1[0:npart, :], -12582912.0, None, ALU.add), reads=[n1], writes=[n1])
            op('dve', lambda e: e.scalar_tensor_tensor(ang[0:npart, :], n1[0:npart, :], -TWO_PI, ang[0:npart, :], ALU.mult, ALU.add), reads=[n1, ang], writes=[ang])
            s2 = tmp.next()
            c2 = n1
            op('act', lambda e: e.activation(s2[0:npart, :], ang[0:npart, :], AF.Sin, scale=0.5), reads=[ang], writes=[s2])
            op('act', lambda e: e.activation(c2[0:npart, :], ang[0:npart, :], AF.Sin, bias=cv[0:npart, 3:4], scale=0.5), reads=[ang, cv], writes=[c2])
            op('dve', lambda e: e.scalar_tensor_tensor(sin_t[0:npart, :], s2[0:npart, :], 2.0, c2[0:npart, :], ALU.mult, ALU.mult), reads=[s2, c2], writes=[sin_t])
            op('dve', lambda e: e.tensor_tensor(s2[0:npart, :], s2[0:npart, :], s2[0:npart, :], ALU.mult), reads=[s2], writes=[s2])
            op('dve', lambda e: e.tensor_scalar(cos_t[0:npart, :], s2[0:npart, :], -2.0, 1.0, ALU.mult, ALU.add), reads=[s2], writes=[cos_t])

        ck, sk = tmp.next(), tmp.next()
        rope_tables(5, ck, sk, 128)
        pr = pbig.next()
        mm(pr, pr[:], rm, rm[:], KV, KV[:, 128:128 + TE])
        t_a = tmp.next()
        op('dve', lambda e: e.tensor_tensor(t_a[:], pr[:], sk[:], ALU.mult), reads=[pr, sk], writes=[t_a])
        op('dve', lambda e: e.tensor_tensor(KV[:, 128:128 + TE], KV[:, 128:128 + TE], ck[:], ALU.mult), reads=[KV, ck], writes=[KV])
        op('dve', lambda e: e.tensor_tensor(KV[:, 128:128 + TE], KV[:, 128:128 + TE], t_a[:], ALU.add), reads=[KV, t_a], writes=[KV])
        rope_tables(4, cosq, sinq, 64)
        if dbg <= 2:
            op('pool', lambda e: e.memset(Yt[:], 0.0), writes=[Yt])
            ev = dma('sp', y_dst.ap.rearrange("(j p) t -> p j t", p=128)[:, :, t0:t0 + TE], Yt[:], reads=[Yt], writes=[y_dst], track=Yt)
            cx.out_events.append(ev)
            continue
        def prep(j):
            RS = RSs[j]
            KS = KSs[j]
            pw = pbig.next()
            mm(pw, pw[:], w2a2, w2a2[0:64, j * 128:(j + 1) * 128], TXW, TXW[0:64, :])
            op('act', lambda e: e.activation(LW[:], pw[:], AF.Sigmoid, bias=vec[:, V_W0 + j:V_W0 + j + 1]), reads=[pw, vec], writes=[LW])
            op('dve', lambda e: e.tensor_scalar(LW[:], LW[:], -math.exp(-0.5), None, ALU.mult), reads=[LW], writes=[LW])
            pa = pbig.next()
            mm(pa, pa[:], w2a2, w2a2[64:128, j * 128:(j + 1) * 128], XS, XS[64:128, :])
            op('act', lambda e: e.activation(AA[:], pa[:], AF.Sigmoid, bias=vec[:, V_A0 + j:V_A0 + j + 1]), reads=[pa, vec], writes=[AA])
            op('dve', lambda e: e.tensor_tensor_scan(CL[:], rst[:], LW[:], 0.0, ALU.mult, ALU.add), reads=[rst, LW], writes=[CL])
            kk = tmp.next()
            op('dve', lambda e: e.tensor_scalar(kk[:], KS[:], vec[:, V_KK + j:V_KK + j + 1], None, ALU.mult), reads=[KS, vec], writes=[kk])
            kk2 = tmp.next()
            op('pool', lambda e: e.tensor_tensor(kk2[:], kk[:], kk[:], ALU.mult), reads=[kk], writes=[kk2])
            pk = pbig.next()
            mm(pk, pk[:], bones, bones[:], kk2, kk2[:])
            op('act', lambda e: e.activation(kk2[:], pk[:], AF.Sqrt, bias=cv[:, 1:2]), reads=[pk, cv], writes=[kk2])
            op('dve', lambda e: e.reciprocal(kk2[:], kk2[:]), reads=[kk2], writes=[kk2])
            op('dve', lambda e: e.tensor_tensor(KKN[:], kk[:], kk2[:], ALU.mult), reads=[kk, kk2], writes=[KKN])
            tk = kk
            op('dve', lambda e: e.tensor_scalar(tk[:], AA[:], vec[:, V_KA + j:V_KA + j + 1], vec[:, V_OMKA + j:V_OMKA + j + 1], ALU.mult, ALU.add),
               reads=[AA, vec], writes=[tk])
            op('dve', lambda e: e.tensor_tensor(KF[:], KS[:], tk[:], ALU.mult), reads=[KS, tk], writes=[KF])
            op('act', lambda e: e.activation(ECL[:], CL[:], AF.Exp), reads=[CL], writes=[ECL])
            encl = tmp.next()
            op('act', lambda e: e.activation(encl[:], CL[:], AF.Exp, scale=-1.0), reads=[CL], writes=[encl])
            eclm = kk2
            op('dve', lambda e: e.tensor_tensor(eclm[:], CL[:], LW[:], ALU.subtract), reads=[CL, LW], writes=[eclm])
            op('act', lambda e: e.activation(eclm[:], eclm[:], AF.Exp), reads=[eclm], writes=[eclm])
            ecL = tmp.next()
            clv = CL[:].rearrange("p (c l) -> p c l", l=LCH)
            op('dve', lambda e: e.tensor_tensor(ecL[:].rearrange("p (c l) -> p c l", l=LCH), clv[:, :, LCH - 1:LCH].to_broadcast([128, NCH, LCH]), clv, ALU.subtract),
               reads=[CL], writes=[ecL])
            op('act', lambda e: e.activation(ecL[:], ecL[:], AF.Exp), reads=[ecL], writes=[ecL])
            op('dve', lambda e: e.scalar_tensor_tensor(ACM[:, 0, :], KKN[:], -1.0, eclm[:], ALU.mult, ALU.mult), reads=[KKN, eclm], writes=[ACM])
            op('pool', lambda e: e.tensor_tensor(ACM[:, 1, :], RS[:], ECL[:], ALU.mult), reads=[RS, ECL], writes=[ACM])
            bb = tk
            op('dve', lambda e: e.tensor_tensor(bb[:], KKN[:], AA[:], ALU.mult), reads=[KKN, AA], writes=[bb])
            op('dve', lambda e: e.tensor_tensor(BCM[:, 0, :], bb[:], encl[:], ALU.mult), reads=[bb, encl], writes=[BCM])
            op('pool', lambda e: e.tensor_tensor(BCM[:, 1, :], KF[:], encl[:], ALU.mult), reads=[KF, encl], writes=[BCM])
            op('dve', lambda e: e.tensor_tensor(BH[:], bb[:], ecL[:], ALU.mult), reads=[bb, ecL], writes=[BH])
            op('pool', lambda e: e.tensor_tensor(KH[:], KF[:], ecL[:], ALU.mult), reads=[KF, ecL], writes=[KH])
            rk = encl
            op('dve', lambda e: e.scalar_tensor_tensor(rk[:], RS[:], vec[:, V_RK + j:V_RK + j + 1], KF[:], ALU.mult, ALU.mult), reads=[RS, vec, KF], writes=[rk])
            pbn = pbig.next()
            mm(pbn, pbn[:], bones, bones[:], rk, rk[:])
            op('dve', lambda e: e.tensor_tensor(BV[:], pbn[:], VS[:, j, :], ALU.mult), reads=[pbn, VS], writes=[BV])

            return None

        def chunk_gen(j, c, R, YV):
            cs = slice(c * LCH, (c + 1) * LCH)
            bmv = bm2[:].unsqueeze(1).to_broadcast([128, 2, 2, 64])
            AXT = R['keep'].next()
            op('dve', lambda e, AXT=AXT: e.tensor_tensor(AXT[:].rearrange("p (a h t) -> p a h t", a=2, h=2), ACM[:, :, cs].unsqueeze(2).to_broadcast([128, 2, 2, 64]), bmv, ALU.mult),
               reads=[ACM, bm2], writes=[AXT])
            BX = R['keep'].next()
            op('pool', lambda e, BX=BX: e.tensor_tensor(BX[:].rearrange("p (a h t) -> p a h t", a=2, h=2), BCM[:, :, cs].unsqueeze(2).to_broadcast([128, 2, 2, 64]), bmv, ALU.mult),
               reads=[BCM, bm2], writes=[BX])
            HX3 = [R['small'].next() for _ in range(3)]
            bm1 = bm2[:]
            for tt_, (src_t, src_ap) in zip(HX3, ((BH, BH[:, cs]), (VS, VS[:, j, cs]), (KH, KH[:, cs]))):
                op('pool', lambda e, tt_=tt_, src_ap=src_ap: e.tensor_tensor(tt_[:].rearrange("p (h t) -> p h t", h=2), src_ap.unsqueeze(1).to_broadcast([128, 2, 64]), bm1, ALU.mult),
                   reads=[src_t, bm2], writes=[tt_])
            yield
            ptm = pfull.next()
            for q_, (src_t, src_ap) in enumerate(((AXT, AXT[:, 0:128]), (HX3[0], HX3[0][:]), (HX3[1], HX3[1][:]), (HX3[2], HX3[2][:]))):
                mm(ptm, ptm[:, q_ * 128:(q_ + 1) * 128], src_t, src_ap, ident, ident[:])
            TM4 = R['tm4'].next()
            op('act', lambda e, TM4=TM4, ptm=ptm: e.copy(TM4[:], ptm[:]), reads=[ptm], writes=[TM4])
            ATm, BHm, Vm, KHm = (TM4[:, q_ * 128:(q_ + 1) * 128] for q_ in range(4))
            yield
            p1 = pbig.next()
            mm(p1, p1[:], AXT, AXT[:, 0:128], BX, BX[:])
            NN = R['keep'].next()
            op('dve', lambda e, NN=NN, p1=p1: e.tensor_tensor(NN[:].rearrange("p (a t) -> p a t", a=2), p1[:].rearrange("p (a t) -> p a t", a=2),
                                                             msk[:, 0, :].unsqueeze(1).to_broadcast([128, 2, 128]), ALU.mult), reads=[p1, msk], writes=[NN])
            p2 = pbig.next()
            mm(p2, p2[:], BX, BX[:, 0:128], AXT, AXT[:])
            TR = R['keep'].next()
            op('dve', lambda e, TR=TR, p2=p2: e.tensor_tensor(TR[:], p2[:], msk[:, 1:3, :].rearrange("p a t -> p (a t)"), ALU.mult), reads=[p2, msk], writes=[TR])
            p3 = psm.next()
            mm(p3, p3[:], BX, BX[:, 128:256], AXT, AXT[:, 128:256])
            QK = R['small'].next()
            op('dve', lambda e, QK=QK, p3=p3: e.tensor_tensor(QK[:], p3[:], msk[:, 2, :], ALU.mult), reads=[p3, msk], writes=[QK])
            yield
            PT = R['big'].next()
            op('act', lambda e, PT=PT, NN=NN: e.copy(PT[:, 0:128], NN[:, 0:128]), reads=[NN], writes=[PT])
            op('dve', lambda e, PT=PT, NN=NN: e.tensor_tensor(PT[:, 128:256], NN[:, 0:128], ident[:], ALU.add), reads=[NN, ident], writes=[PT])
            PkT_t, PkT = TR, TR[:, 0:128]
            nsteps = 7
            for kstep in range(nsteps - 1):
                last = (kstep == nsteps - 2)
                pA = pbig.next()
                if kstep == 0:
                    mm(pA, pA[:, 0:128], PkT_t, PkT, PT, PT[:, 0:128])
                elif not last:
                    mm(pA, pA[:], PkT_t, PkT, PT, PT[:])
                else:
                    mm(pA, pA[:, 128:256], PkT_t, PkT, PT, PT[:, 128:256])
                PTn = R['big'].next()
                if not last:
                    pB = psm.next()
                    mm(pB, pB[:], PT, PT[:, 0:128], PkT_t, PkT)
                    PkTn = R['small'].next()
                    op('act', lambda e, PkTn=PkTn, pB=pB: e.copy(PkTn[:], pB[:]), reads=[pB], writes=[PkTn])
                    op('act', lambda e, PTn=PTn, pA=pA: e.copy(PTn[:, 0:128], pA[:, 0:128]), reads=[pA], writes=[PTn])
                if kstep == 0:
                    op('dve', lambda e, PTn=PTn, PT=PT: e.tensor_copy(PTn[:, 128:256], PT[:, 128:256]), reads=[PT], writes=[PTn])
                else:
                    op('dve', lambda e, PTn=PTn, PT=PT, pA=pA: e.tensor_tensor(PTn[:, 128:256], pA[:, 128:256], PT[:, 128:256], ALU.add),
                       reads=[pA, PT], writes=[PTn])
                PT = PTn
                if not last:
                    PkT_t, PkT = PkTn, PkTn[:]
            yield
            Tt, Tm = PT, PT[:, 128:256]
            p4 = pbig.next()
            mm(p4, p4[:, 0:128], Tt, Tm, TR, TR[:, 128:256])
            mm(p4, p4[:, 128:256], Tt, Tm, TM4, BHm)
            GZ = R['big'].next()
            op('act', lambda e, GZ=GZ, p4=p4: e.copy(GZ[:], p4[:]), reads=[p4], writes=[GZ])
            yield
            p5 = pbig.next()
            mm(p5, p5[:], NN, NN[:, 128:256], GZ, GZ[:])
            HX = R['big'].next()
            op('dve', lambda e, HX=HX, p5=p5, QK=QK: e.tensor_tensor(HX[:, 0:128], p5[:, 0:128], QK[:], ALU.add), reads=[p5, QK], writes=[HX])
            op('dve', lambda e, HX=HX, p5=p5, TM4=TM4, KHm=KHm: e.tensor_tensor(HX[:, 128:256], p5[:, 128:256], KHm, ALU.add), reads=[p5, TM4], writes=[HX])
            yield
            DW = R['small'].next()
            op('dve', lambda e, DW=DW, c=c: e.tensor_scalar(DW[:], ident[:], ECL[:, (c + 1) * LCH - 1:(c + 1) * LCH], None, ALU.mult), reads=[ident, ECL], writes=[DW])
            p6 = pbig.next()
            mm(p6, p6[:], TM4, ATm, GZ, GZ[:])
            REM = R['big'].next()
            op('dve', lambda e, REM=REM, p6=p6, AXT=AXT: e.tensor_tensor(REM[:, 0:128], p6[:, 0:128], AXT[:, 128:256], ALU.add), reads=[p6, AXT], writes=[REM])
            op('dve', lambda e, REM=REM, p6=p6, DW=DW: e.tensor_tensor(REM[:, 128:256], p6[:, 128:256], DW[:], ALU.add), reads=[p6, DW], writes=[REM])
            yield
            Sc = S2[j][si[j] % 2]
            Sn = S2[j][(si[j] + 1) % 2]
            si[j] += 1
            pyc = psm.next()
            mm(pyc, pyc[:], TM4, Vm, HX, HX[:, 0:128], start=True, stop=False)
            mm(pyc, pyc[:], Sc, Sc[:], REM, REM[:, 0:128], start=False, stop=True)
            op('act', lambda e, pyc=pyc, cs=cs: e.copy(YV[:, cs], pyc[:, 0:64]), reads=[pyc], writes=[YV])
            op('dve', lambda e, pyc=pyc, cs=cs: e.tensor_tensor(YV[:, cs], YV[:, cs], pyc[:, 64:128], ALU.add), reads=[pyc, YV], writes=[YV])
            p7 = psm.next()
            mm(p7, p7[:], REM, REM[:, 128:256], Sc, Sc[:], start=True, stop=False)
            mm(p7, p7[:], HX, HX[:, 128:256], TM4, Vm, start=False, stop=True)
            op('act', lambda e, Sn=Sn, p7=p7: e.copy(Sn[:], p7[:]), reads=[p7], writes=[Sn])

        def gn(j, YV):
            yv = YV
            pm_ = pbig.next()
            mm(pm_, pm_[:], bones, bones[:], yv, yv[:])
            yc = tmp.next()
            op('dve', lambda e: e.scalar_tensor_tensor(yc[:], pm_[:], -1.0 / 64, yv[:], ALU.mult, ALU.add), reads=[pm_, yv], writes=[yc])
            y2 = yv
            op('pool', lambda e: e.tensor_tensor(y2[:], yc[:], yc[:], ALU.mult), reads=[yc], writes=[y2])
            pv_ = pbig.next()
            mm(pv_, pv_[:], bones, bones[:], y2, y2[:])
            op('act', lambda e: e.activation(y2[:], pv_[:], AF.Sqrt, bias=cv[:, 2:3], scale=1.0 / 64), reads=[pv_, cv], writes=[y2])
            op('dve', lambda e: e.reciprocal(y2[:], y2[:]), reads=[y2], writes=[y2])
            op('dve', lambda e: e.tensor_tensor(yc[:], yc[:], y2[:], ALU.mult), reads=[yc, y2], writes=[yc])
            op('dve', lambda e: e.tensor_scalar(yc[:], yc[:], vec[:, V_LNW + j:V_LNW + j + 1], vec[:, V_LNB + j:V_LNB + j + 1], ALU.mult, ALU.add),
               reads=[yc, vec], writes=[yc])
            op('dve', lambda e: e.tensor_tensor(yc[:], yc[:], BV[:], ALU.add), reads=[yc, BV], writes=[yc])
            sgt = y2
            pgt = pbig.next()
            for k in range(KC):
                mm(pgt, pgt[:], W, W[:, 4, k, j * 128:(j + 1) * 128], B, B[:, k, 2:TE + 2], start=(k == 0), stop=(k == KC - 1))
            op('act', lambda e: e.activation(sgt[:], pgt[:], AF.Silu), reads=[pgt], writes=[sgt])
            op('dve', lambda e: e.tensor_tensor(Yt[:, j, :], yc[:], sgt[:], ALU.mult), reads=[yc, sgt], writes=[Yt])


        def attn_gen():
            for jj in range(2):
                proj(5, jj * 128, 128, QR, QR[:, jj, :], 'act' if jj == 0 else 'dve')
            for h in range(4):
                hs_ = slice((h % 2) * 64, (h % 2) * 64 + 64)
                pq = pbig.next()
                mm(pq, pq[0:64, :], rm, rm[hs_, hs_], QR, QR[hs_, h // 2, :])
                pq2 = pbig.next()
                mm(pq2, pq2[0:64, :], ident, ident[hs_, hs_], QR, QR[hs_, h // 2, :])
                t_b = atmp.next()
                op('dve', lambda e, pq=pq, t_b=t_b: e.tensor_tensor(t_b[0:64, :], pq[0:64, :], sinq[:], ALU.mult), reads=[pq, sinq], writes=[t_b])
                op('dve', lambda e, h=h, pq2=pq2: e.tensor_tensor(Q4[:, h, :], pq2[0:64, :], cosq[:], ALU.mult), reads=[pq2, cosq], writes=[Q4])
                op('dve', lambda e, h=h, t_b=t_b: e.tensor_tensor(Q4[:, h, :], Q4[:, h, :], t_b[0:64, :], ALU.add), reads=[Q4, t_b], writes=[Q4])
                yield

            if dbg <= 4:
                op('pool', lambda e: e.memset(Yt[:, 2:4, :], 0.0), writes=[Yt])
            for bq in range(TE // 128 if dbg > 4 else 0):
                blk_first = first_tile and bq == 0
                pvt = psm.next()
                mm(pvt, pvt[:], KV, KV[:, 128 + bq * 128:128 + (bq + 1) * 128], ident, ident[:])
                vt_cur = Vtm.next()
                op('act', lambda e, vt_cur=vt_cur, pvt=pvt: e.copy(vt_cur[:], pvt[:]), reads=[pvt], writes=[vt_cur])
                if vstate['p'] is None:
                    vstate['p'] = vt_cur
                vtm_prev = vstate['p']
                po = po_ring.next()
                for h in range(4):
                    pssc = pbig.next()
                    mm(pssc, pssc[:], Q4, Q4[:, h, bq * 128:(bq + 1) * 128], KV, KV[0:64, bq * 128:bq * 128 + 256])
                    sc = atmp.next()
                    op('dve', lambda e, sc=sc, pssc=pssc: e.scalar_tensor_tensor(sc[:], pssc[:], scale, mb[:, 1 if blk_first else 0, :], ALU.mult, ALU.add),
                       reads=[pssc, mb], writes=[sc])
                    op('dve', lambda e, sc=sc: e.reduce_max(st[:, 0:1], sc[:], axis=AX.X), reads=[sc], writes=[st])
                    op('dve', lambda e, h=h: e.tensor_scalar(st[:, 1:2], st[:, 0:1], sinkb[:, h:h + 1], -1.0, ALU.max, ALU.mult), reads=[st, sinkb], writes=[st])
                    op('dve', lambda e: e.memset(st[:, 2:3], 0.0), writes=[st])
                    pe_ = atmp.next()
                    op('act', lambda e, sc=sc, pe_=pe_: e.activation(pe_[:], sc[:], AF.Exp, bias=st[:, 1:2], accum_out=st[:, 2:3]), reads=[sc, st], writes=[pe_, st])
                    op('act', lambda e, h=h: e.activation(st[:, 3:4], sinkb[:, h:h + 1], AF.Exp, bias=st[:, 1:2]), reads=[sinkb, st], writes=[st])
                    op('dve', lambda e: e.tensor_tensor(st[:, 4:5], st[:, 2:3], st[:, 3:4], ALU.add), reads=[st], writes=[st])
                    op('dve', lambda e: e.reciprocal(st[:, 5:6], st[:, 4:5]), reads=[st], writes=[st])
                    op('dve', lambda e, pe_=pe_: e.tensor_scalar(pe_[:], pe_[:], st[:, 5:6], None, ALU.mult), reads=[pe_, st], writes=[pe_])
                    yield
                    pts = []
                    for half in range(2):
                        ptp = psm.next()
                        mm(ptp, ptp[:], pe_, pe_[:, half * 128:(half + 1) * 128], ident, ident[:])
                        pT = asmall.next()
                        op('act' if half == 0 else 'dve',
                           (lambda e, pT=pT, ptp=ptp: e.copy(pT[:], ptp[:])) if half == 0 else (lambda e, pT=pT, ptp=ptp: e.tensor_copy(pT[:], ptp[:])),
                           reads=[ptp], writes=[pT])
                        pts.append(pT)
                    osl = po[(h % 2) * 64:(h % 2) * 64 + 64, (h // 2) * 128:(h // 2) * 128 + 128]
                    mm(po, osl, vtm_prev, vtm_prev[:, 64:128], pts[0], pts[0][:], start=True, stop=False)
                    mm(po, osl, vt_cur, vt_cur[:, 64:128], pts[1], pts[1][:], start=False, stop=True)
                    yield
                vstate['p'] = vt_cur
                yield
                for jj in range(2):
                    sgt = atmp.next()
                    pgb = psm.next()
                    for k in range(KC):
                        mm(pgb, pgb[:], W, W[:, 6, k, jj * 128:(jj + 1) * 128], B, B[:, k, 2 + bq * 128:2 + (bq + 1) * 128], start=(k == 0), stop=(k == KC - 1))
                    op('act', lambda e, sgt=sgt, pgb=pgb: e.activation(sgt[:, 0:128], pgb[:], AF.Silu), reads=[pgb], writes=[sgt])
                    op('dve', lambda e, sgt=sgt, jj=jj, bq=bq, po=po: e.tensor_tensor(Yt[:, 2 + jj, bq * 128:(bq + 1) * 128], po[:, jj * 128:(jj + 1) * 128], sgt[:, 0:128], ALU.mult),
                       reads=[po, sgt], writes=[Yt])
            yield

        def step(g):
            try:
                next(g)
                return True
            except StopIteration:
                return False

        ag = attn_gen() if dbg > 4 else iter(())
        ag_alive = True
        for j in range(2):
            prep(j)
            YV = yvs[j]
            if dbg < 4:
                op('dve', lambda e, YV=YV: e.tensor_copy(YV[:], BV[:]), reads=[BV], writes=[YV])
            for c0 in range(0, NCH if dbg > 3 else 0, NIL):
                gens = [chunk_gen(j, c0 + q_, RS_[q_], YV) for q_ in range(NIL)]
                while gens:
                    gens = [g for g in gens if step(g)]
                    if ag_alive:
                        ag_alive = step(ag)
            gn(j, YV)
        while ag_alive:
            ag_alive = step(ag)

        for n in ('r', 'k', 'v'):
            zt = Z[n]
            op('pool', lambda e, zt=zt: e.tensor_copy(zt[:, :, 0:1], zt[:, :, TE:TE + 1]), reads=[zt], writes=[zt])

        op('pool', lambda e: e.tensor_copy(KV[:, 0:128], KV[:, TE:TE + 128]), reads=[KV], writes=[KV])
        if layer2:
            op('pool', lambda e: e.tensor_copy(B[:, :, 1:2], B[:, :, TE + 1:TE + 2]), reads=[B], writes=[B])
        ev = dma('sp', y_dst.ap.rearrange("(j p) t -> p j t", p=128)[:, :, t0:t0 + TE], Yt[:], reads=[Yt], writes=[y_dst], track=Yt)
        cx.out_events.append(ev)
    cx.release(mark)


A_W = 1024
SHIFT_W = 3200
C_R, C_K, C_V, C_XW, C_AG, C_Q, C_KB, C_VB, C_BG = 0, 1024, 2048, 3072, 3200, 4224, 5248, 5504, 5760


def make_consts_even():
    c = make_consts()
    i = np.arange(128)
    c['bones'] = ((i[:, None] // 64) == (i[None, :] // 64)).astype(np.float32)
    c['idb2'] = ((i[:, None] % 64) == np.arange(64)[None, :]).astype(np.float32)
    t = np.arange(64)
    SL = (t[:, None] > t[None, :]).astype(np.float32)
    SU = SL.T.copy()
    UI = (t[None, :] >= t[:, None]).astype(np.float32)
    em = np.zeros((128, 3, 128), np.float32)
    for hh in range(2):
        bs_ = slice(hh * 64, (hh + 1) * 64)
        em[bs_, 0, bs_] = SL
        em[bs_, 1, bs_] = SU
        em[bs_, 2, bs_] = UI
    c['bm2'] = np.ascontiguousarray(np.repeat(((i[:, None] // 64) == np.arange(2)[None, :]).astype(np.float32)[:, :, None], 64, axis=2))
    c['emsk'] = em
    q = np.arange(128)
    NEG = -30000.0
    am = np.full((128, 2, 256), NEG, np.float32)
    prev_ok = q[None, :] > q[:, None]
    cur_ok = q[None, :] <= q[:, None]
    am[:, 0, 0:128] = np.where(prev_ok, 0.0, NEG)
    am[:, 0, 128:256] = np.where(cur_ok, 0.0, NEG)
    am[:, 1, 128:256] = np.where(cur_ok, 0.0, NEG)
    c['amask'] = am
    rmat = np.zeros((128, 128), np.float32)
    for p in range(128):
        d = p % 64
        if d < 8:
            rmat[p + 8, p] = -1.0
        elif d < 16:
            rmat[p - 8, p] = 1.0
    c['ropem'] = rmat
    rst = np.ones((128, TE), np.float32)
    rst[:, ::LCH] = 0.0
    c['rst'] = rst
    cvec = np.zeros((128, 8), np.float32)
    cvec[:, 0] = RMS_EPS
    cvec[:, 1] = 1e-12
    cvec[:, 2] = GN_EPS
    cvec[:, 3] = math.pi / 2
    inv = np.power(np.float32(500000.0), -np.arange(8, dtype=np.float32) / np.float32(8)).astype(np.float32)
    fq = np.zeros(128, np.float32)
    for p in range(128):
        d = p % 64
        if d < 16:
            fq[p] = inv[d % 8]
    fk = fq.copy()
    fk[64:] = 0.0
    cvec[:, 4] = fq
    cvec[:, 5] = fk
    c['cvec'] = cvec
    return c


def prep_even(e, hg, P):
    W = P['e_w_in'][e]
    hs = slice(hg * 256, (hg + 1) * 256)

    def cols(base):
        return W[:, base + hg * 256: base + (hg + 1) * 256]
    kvb = np.concatenate([W[:, C_KB + hg * 64: C_KB + (hg + 1) * 64], W[:, C_VB + hg * 64: C_VB + (hg + 1) * 64]], axis=1)
    wcols = np.concatenate([W[:, C_XW:C_XW + 128], kvb, cols(C_R), cols(C_K), cols(C_V), cols(C_AG), cols(C_Q), cols(C_BG)], axis=1)
    mu = P['e_mu'][e]
    vec = np.zeros((128, 40), np.float32)

    def put(col, v256):
        vec[:, col:col + 2] = v256.reshape(2, 128).T
    put(0, mu[C_R:C_R + 1024][hs]); put(2, mu[C_K:C_K + 1024][hs]); put(4, mu[C_V:C_V + 1024][hs])
    put(6, P['rwkv_w0'][e][hs]); put(8, P['rwkv_a0'][e][hs]); put(10, P['rwkv_k_k'][e][hs]); put(12, P['rwkv_k_a'][e][hs])
    put(16, P['rwkv_r_k'][e].reshape(-1)[hs]); put(18, P['rwkv_ln_w'][e][hs]); put(20, P['rwkv_ln_b'][e][hs])
    if e > 0:
        put(22, P['rwkv_v0'][e - 1][hs])
    vec[:, 24] = mu[C_XW:C_XW + 128]
    w2a2 = np.concatenate([P['rwkv_w2'][e][:, hs], P['rwkv_a2'][e][:, hs]], axis=0)
    out = {
        'norm': vec_pk(P['e_norm'][e]),
        'w_in': tile_w(wcols),
        'vec': vec,
        'w2a2': np.ascontiguousarray(w2a2, dtype=np.float32),
        'sinks': np.ascontiguousarray(P['attn_sinks'][e][hg * 4:(hg + 1) * 4], dtype=np.float32),
    }
    if e > 0:
        out['wvT'] = np.ascontiguousarray(W[:, C_V:C_V + 1024].T, dtype=np.float32)
        out['v1'] = np.ascontiguousarray(P['rwkv_v1'][e - 1].reshape(8, 128, 32).transpose(1, 0, 2), dtype=np.float32)
        out['muv'] = vec_pk(mu[C_V:C_V + 1024])
        out['v2'] = np.ascontiguousarray(P['rwkv_v2'][e - 1][:, hs], dtype=np.float32)
    return out


SEQ = 8192
_PROG_CACHE = {}


def emit_outproj_stage(cm, w_out, y_src, x_src, x_dst):
    cx, TT = cm.cx, cm.TT
    for t0 in range(0, NTOK, TT):
        cx.dma('pool', cm.Y[:], y_src.ap.rearrange("(k p) t -> p k t", p=128)[:, :, t0:t0 + TT], reads=[y_src], writes=[cm.Y], track=cm.Y)
        cm.out_proj_residual(w_out, x_src, x_dst, t0)


def build_E(e, in_shapes):
    nc = bass.Bass("TRN2", target_bir_lowering=False)
    aps = {k: declare(nc, k, v) for k, v in in_shapes.items()}
    yo = nc.dram_tensor("y", [512, SEQ], F32, kind="ExternalOutput").ap()
    cx = Ctx(nc)
    PP = {k[2:]: aps[k] for k in aps if k.startswith('p_')}
    CC = {k[2:]: aps[k] for k in aps if k.startswith('c_')}
    yd = DT(yo)
    if e == 0:
        vfo = nc.dram_tensor("vf", [256, SEQ], F32, kind="ExternalOutput").ap()
        vd = DT(vfo)
        emit_even_phase(cx, PP, CC, DT(aps['xT']), yd, vd, None, SEQ, False)
        for key, v in vd.wd.items():
            cx.out_events.append((key, v))
    else:
        emit_even_phase(cx, PP, CC, DT(aps['xT']), yd, None, DT(aps['vfin']), SEQ, True)
    cx.finish()
    return nc


def build_O(final, in_shapes):
    nc = bass.Bass("TRN2", target_bir_lowering=False)
    aps = {k: declare(nc, k, v) for k, v in in_shapes.items()}
    xo = nc.dram_tensor("xo", [D, NTOK], F32, kind="ExternalOutput").ap()
    xm = nc.dram_tensor("xmid", [D, NTOK], F32, kind="Internal").ap()
    cx = Ctx(nc)
    CC = {k[2:]: aps[k] for k in aps if k.startswith('c_')}
    cm = Common(cx, CC, 512)
    PP = {k[2:]: aps[k] for k in aps if k.startswith('p_')}
    x_in, y_in, x_mid, x_out = DT(aps['xT']), DT(aps['yT']), DT(xm), DT(xo)
    emit_outproj_stage(cm, aps['wout_e'], y_in, x_in, x_mid)
    if not final:
        emit_odd_layer(cm, PP, x_mid, x_out)
        for key, v in x_out.wd.items():
            cx.out_events.append((key, v))
    else:
        xm2 = nc.dram_tensor("xmid2", [D, NTOK], F32, kind="Internal").ap()
        x_mid2 = DT(xm2)
        emit_odd_layer(cm, PP, x_mid, x_mid2)
        fn = cx.sb([128, KC], F32)
        cx.dma('sp', fn[:], aps['fnorm'], writes=[fn], track=fn)
        emit_final_norm(cm, x_mid2, x_out, fn)
    cx.finish()
    return nc


def _shapes(d):
    return {k: v for k, v in d.items()}


def kernel(**inputs):
    P = {k: np.asarray(v) for k, v in inputs.items()}
    x = P['x'].astype(np.float32, copy=False)
    pos = P['positions'].astype(np.int32, copy=False)
    B_ = x.shape[0]
    cores = list(range(8))
    ce = make_consts_even()
    co = make_consts()
    xT_b = [np.ascontiguousarray(x[b].T) for b in range(B_)]
    vf_cores = None
    for e in range(2):
        in_maps = []
        for c in cores:
            b, hg = c // 4, c % 4
            m = {'xT': xT_b[b]}
            for k, v in ce.items():
                m['c_' + k] = v
            for k, v in prep_even(e, hg, P).items():
                m['p_' + k] = v
            m['p_pos'] = np.ascontiguousarray(pos[b])
            if e > 0:
                m['vfin'] = vf_cores[c]
            in_maps.append(m)
        nc = build_E(e, in_maps[0])
        res = run_bass_kernel_spmd(nc, in_maps, core_ids=cores)
        if e == 0:
            vf_cores = [np.ascontiguousarray(res.results[c]['vf']) for c in cores]
        yT_b = []
        for b in range(B_):
            yT = np.empty((2048, SEQ), np.float32)
            for hg in range(4):
                r = res.results[b * 4 + hg]['y']
                yT[hg * 256:(hg + 1) * 256] = r[0:256]
                yT[1024 + hg * 256:1024 + (hg + 1) * 256] = r[256:512]
            yT_b.append(yT)
        del res
        final = (e == 1)
        in_maps = []
        podd = prep_odd(P['o_norm'][e], P['o_w_in'][e], P['sgu_ln_w'][e], P['sgu_ln_b'][e], P['sgu_ws'][e], P['sgu_bs'][e], P['o_w_out'][e])
        wout_e = tile_w(P['e_w_out'][e])
        for c in cores:
            b, sg = c // 4, c % 4
            sl = slice(sg * NTOK, (sg + 1) * NTOK)
            m = {'xT': np.ascontiguousarray(xT_b[b][:, sl]), 'yT': np.ascontiguousarray(yT_b[b][:, sl]), 'wout_e': wout_e}
            for k, v in co.items():
                m['c_' + k] = v
            for k, v in podd.items():
                m['p_' + k] = v
            if final:
                m['fnorm'] = vec_pk(P['final_norm'])
            in_maps.append(m)
        nc = build_O(final, in_maps[0])
        res = run_bass_kernel_spmd(nc, in_maps, core_ids=cores)
        for b in range(B_):
            xT_b[b] = np.concatenate([res.results[b * 4 + sg]['xo'] for sg in range(4)], axis=1)
        del res
    out = np.stack([np.ascontiguousarray(xT_b[b].T) for b in range(B_)], axis=0)
    return out.astype(np.float32, copy=False)
```
